# Optimizing a Trainium2 kernel written in Bass

```python
import math
import jax
import jax.numpy as jnp
from jax import lax
import numpy as np

D_MODEL = 4096
BATCH = 1
SEQ = 8192
DEPTH = 2

CTX_LEN = 256
GRID_W = 64
EPS = 1e-6
N_MOD = 9
D_FF = 5632

HY_WIDTH = 1024
HY_ORDER = 2
HY_BANDS = 16
HY_EMB = 1 + 2 * HY_BANDS
HY_FILTER_HIDDEN = 64
HY_DECAY_TARGET = 1e-2
HY_FAST_DECAY = 0.3
HY_SLOW_DECAY = 1.5
SHORT_CONV = 3

DA_HEADS = 16
DA_QK = 64
DA_V = 2 * DA_QK
DA_WIDTH = DA_HEADS * DA_V
ROPE_THETA = 10000.0
Q_BLOCK = 128

RW_HEAD = 64
RW_WIDTH = 1024
RW_HEADS = RW_WIDTH // RW_HEAD
RW_DECAY_RANK = 64
RW_ICL_RANK = 64
RW_GATE_RANK = 160
RW_GN_EPS = 64e-5

N_BRANCH = 3
MIX_WIDTH = HY_WIDTH + DA_WIDTH + RW_WIDTH

HY_COLS = 3 * HY_WIDTH
DA_QK_COLS = DA_HEADS * 2 * DA_QK
RW_STATE_COLS = 2 * RW_WIDTH + 2 * RW_DECAY_RANK + 2 * RW_ICL_RANK
RW_COLS = RW_STATE_COLS + RW_WIDTH + RW_GATE_RANK
O_HY = 0
O_Q = O_HY + HY_COLS
O_K = O_Q + DA_QK_COLS
O_V = O_K + DA_QK_COLS
O_RW = O_V + DA_WIDTH
O_RW_RG = O_RW + RW_STATE_COLS
O_GATE = O_RW + RW_COLS
IN_COLS = O_GATE + N_BRANCH * D_MODEL

kernel_name = 'hybrid_hyena_diffattn_rwkv7_dit'


def rms_norm(x, gain, eps=EPS):
    xf = x.astype(jnp.float32)
    y = xf * lax.rsqrt(jnp.mean(jnp.square(xf), axis=-1, keepdims=True) + eps)
    return (y * gain.astype(jnp.float32)).astype(x.dtype)


def modulate(h, shift, scale):
    return h * (1.0 + scale) + shift


def swiglu(h, w_in, w_out):
    gate, up = jnp.split(h @ w_in, 2, axis=-1)
    return (jax.nn.silu(gate) * up) @ w_out


def adaln_half_ffn(x, gain, shift, scale, gate, w_in, w_out):
    return x + 0.5 * gate * swiglu(modulate(rms_norm(x, gain), shift, scale), w_in, w_out)


def centred_conv(u, w):
    width = w.shape[0]
    half = width // 2
    length = u.shape[1]
    up = jnp.pad(u, ((0, 0), (half, half), (0, 0)))
    return sum(up[:, j:j + length] * w[j] for j in range(width))


def take_cols(t, start, stop, offset):
    return t[..., start - offset:stop - offset]


def hyena_filter_spectra(length, w1, b1, f1, w2, b2, f2, w3):
    t = jnp.arange(length, dtype=jnp.float32)
    t_norm = t / length
    bands = jnp.linspace(1e-4, HY_BANDS - 1, HY_BANDS, dtype=jnp.float32)
    ang = (2.0 * math.pi / length) * t[:, None] * bands[None, :]
    z = jnp.concatenate([t_norm[:, None], jnp.cos(ang), -jnp.sin(ang)], axis=-1)
    h = jnp.sin(f1 * (z @ w1 + b1))
    h = jnp.sin(f2 * (h @ w2 + b2))
    h = (h @ w3).astype(jnp.float32).reshape(length, HY_ORDER, 2, HY_WIDTH)
    deltas = jnp.abs(jnp.linspace(math.log(HY_DECAY_TARGET) / HY_SLOW_DECAY,
                                  math.log(HY_DECAY_TARGET) / HY_FAST_DECAY, HY_WIDTH, dtype=jnp.float32))
    window = jnp.exp(-t_norm[:, None] * deltas[None, :])
    h = h * window[:, None, None, :]
    fwd, bwd = h[:, :, 0], h[:, :, 1]
    kernel = jnp.concatenate([fwd, jnp.zeros((1, HY_ORDER, HY_WIDTH), jnp.float32),
                              jnp.flip(bwd[1:], axis=0)], axis=0)
    return jnp.fft.rfft(kernel, axis=0)


def fft_conv(u, k_spec, skip):
    length = u.shape[1]
    u32 = u.astype(jnp.float32)
    y = jnp.fft.irfft(jnp.fft.rfft(u32, n=2 * length, axis=1) * k_spec[None], n=2 * length, axis=1)[:, :length]
    return (y + u32 * skip.astype(jnp.float32)).astype(u.dtype)


def hyena_mix(u, hy_params, skip):
    v, x1, x2 = jnp.split(u, 3, axis=-1)
    spec = hyena_filter_spectra(u.shape[1], *hy_params)
    z = x1 * fft_conv(v, spec[:, 0], skip[0])
    return x2 * fft_conv(z, spec[:, 1], skip[1])


def split_qk(t):
    return t.reshape(*t.shape[:-1], DA_HEADS, 2, DA_QK)


def rope_1d(x, pos):
    half = x.shape[-1] // 2
    inv = ROPE_THETA ** (-jnp.arange(half, dtype=jnp.float32) / half)
    ang = pos.astype(jnp.float32)[:, None] * inv[None, :]
    cos = jnp.cos(ang)[:, None, None, :]
    sin = jnp.sin(ang)[:, None, None, :]
    x1 = x[..., :half].astype(jnp.float32)
    x2 = x[..., half:].astype(jnp.float32)
    return jnp.concatenate([x1 * cos - x2 * sin, x1 * sin + x2 * cos], axis=-1).astype(x.dtype)


def axial_rope(x, row_pos, col_pos):
    a = x.shape[-1] // 2
    return jnp.concatenate([rope_1d(x[..., :a], row_pos), rope_1d(x[..., a:], col_pos)], axis=-1)


def diff_attn_core(q, k, v, lam):
    s = jnp.einsum('bqhmd,bkhmd->bhmqk', q, k).astype(jnp.float32) * (DA_QK ** -0.5)
    p = jax.nn.softmax(s, axis=-1)
    p = p[:, :, 0] - lam * p[:, :, 1]
    return jnp.einsum('bhqk,bkhe->bqhe', p.astype(v.dtype), v)


def diff_attn_latent(q, k, v, k_ctx, v_ctx, lam):
    bsz, seq = q.shape[:2]
    k_all = jnp.concatenate([k, k_ctx], axis=1)
    v_all = jnp.concatenate([v, v_ctx], axis=1)
    qb = q.reshape(bsz, seq // Q_BLOCK, Q_BLOCK, DA_HEADS, 2, DA_QK).swapaxes(0, 1)
    out = lax.map(lambda qi: diff_attn_core(qi, k_all, v_all, lam), qb)
    return out.swapaxes(0, 1).reshape(bsz, seq, DA_HEADS, DA_V)


def diff_out(y, gain, lam_init):
    y = rms_norm(y, gain) * (1.0 - lam_init)
    return y.reshape(*y.shape[:2], DA_WIDTH)


def heads(t):
    return t.reshape(*t.shape[:-1], RW_HEADS, RW_HEAD)


def rwkv_state_terms(u, w0, w2, a0, a2, k_k, k_a):
    bsz, length = u.shape[:2]
    k = u[..., :RW_WIDTH]
    v = u[..., RW_WIDTH:2 * RW_WIDTH]
    wd = u[..., 2 * RW_WIDTH:2 * RW_WIDTH + 2 * RW_DECAY_RANK].reshape(bsz, length, 2, RW_DECAY_RANK)
    ad = u[..., 2 * RW_WIDTH + 2 * RW_DECAY_RANK:].reshape(bsz, length, 2, RW_ICL_RANK)
    w_log = -jax.nn.softplus(-(w0 + jnp.einsum('bldr,drc->bldc', jnp.tanh(wd), w2))) - 0.5
    decay = jnp.exp(-jnp.exp(w_log.astype(jnp.float32)))
    a = jax.nn.sigmoid(a0 + jnp.einsum('bldr,drc->bldc', ad, a2)).astype(jnp.float32)
    kk = heads(k * k_k).astype(jnp.float32)
    kk = kk / jnp.maximum(jnp.sqrt(jnp.sum(jnp.square(kk), axis=-1, keepdims=True)), 1e-12)
    k_dir = k[:, :, None] * (1.0 + (a - 1.0) * k_a)
    b = kk[:, :, None] * heads(a)
    return (heads(decay), heads(k_dir), heads(v), -kk, b)


def rwkv_receptance_gate(u, g2):
    r = heads(u[..., :RW_WIDTH])
    g = jax.nn.sigmoid(u[..., RW_WIDTH:]) @ g2
    return r, g


def rwkv_scan(state, decay, k, v, a, b, r=None):
    def step(s, inp):
        w_t, k_t, v_t, a_t, b_t = inp[:5]
        s = (s * w_t[:, :, None, :]
             + jnp.einsum('bhvk,bhk->bhv', s, a_t)[..., None] * b_t[:, :, None, :]
             + v_t[..., None] * k_t[:, :, None, :])
        y = None if r is None else jnp.einsum('bhvk,bhk->bhv', s, inp[5])
        return s, y
    seqs = (decay, k, v, a, b) if r is None else (decay, k, v, a, b, r)
    xs = tuple(jnp.moveaxis(t.astype(jnp.float32), 1, 0) for t in seqs)
    state, ys = lax.scan(step, state, xs)
    return state, (None if r is None else jnp.moveaxis(ys, 0, 1))


def rwkv_bidir(terms, r, s_fwd, s_bwd):
    decay, k, v, a, b = terms
    sf, yf = rwkv_scan(s_fwd, decay[:, :, 0], k[:, :, 0], v, a, b[:, :, 0], r)
    rev = lambda t: jnp.flip(t, axis=1)
    sb, yb = rwkv_scan(s_bwd, rev(decay[:, :, 1]), rev(k[:, :, 1]), rev(v), rev(a), rev(b[:, :, 1]),
                       None if r is None else rev(r))
    y = None if r is None else yf + rev(yb)
    return sf, sb, y


def rwkv_out(y, r, terms, g, gn_w, gn_b, r_k):
    _, k_dir, v, _, _ = terms
    mu = jnp.mean(y, axis=-1, keepdims=True)
    var = jnp.mean(jnp.square(y - mu), axis=-1, keepdims=True)
    yn = (y - mu) * lax.rsqrt(var + RW_GN_EPS) * heads(gn_w) + heads(gn_b)
    bonus = jnp.sum(r * jnp.sum(k_dir, axis=2) * r_k, axis=-1, keepdims=True) * v
    out = (yn + bonus).reshape(*y.shape[:2], RW_WIDTH) * g
    return out.astype(r.dtype)


def merge_branches(gate_logits, y_h, y_a, y_r, w_branch, w_out):
    g = jax.nn.sigmoid(gate_logits).reshape(*gate_logits.shape[:-1], N_BRANCH, D_MODEL)
    w_h = w_branch[:HY_WIDTH]
    w_a = w_branch[HY_WIDTH:HY_WIDTH + DA_WIDTH]
    w_r = w_branch[HY_WIDTH + DA_WIDTH:]
    merged = g[..., 0, :] * (y_h @ w_h) + g[..., 1, :] * (y_a @ w_a) + g[..., 2, :] * (y_r @ w_r)
    return merged @ w_out


def setup_inputs(seed: int = 0) -> dict:
    key = jax.random.key(seed)
    keys = iter(jax.random.split(key, 48))

    def nrm(shape, std):
        return jax.random.normal(next(keys), shape, jnp.float32) * std

    def near_one(shape):
        return 1.0 + nrm(shape, 0.01)

    D = D_MODEL
    centre_tap = jnp.zeros((SHORT_CONV, 1), jnp.float32).at[SHORT_CONV // 2].set(1.0)
    return {
        'x': nrm((BATCH, SEQ, D), 1.0),
        'c': nrm((BATCH, D), 1.0),
        'ctx': nrm((BATCH, CTX_LEN, D), 1.0),
        'c_ctx': nrm((D,), 1.0),
        'w_mod': nrm((DEPTH, D, N_MOD * D), 0.5 * D ** -0.5),
        'b_mod': nrm((DEPTH, N_MOD * D), 0.01),
        'norm_gain': near_one((DEPTH, 3, D)),
        'w_ff_in': nrm((DEPTH, 2, D, 2 * D_FF), D ** -0.5),
        'w_ff_out': nrm((DEPTH, 2, D_FF, D), D_FF ** -0.5),
        'w_in': nrm((DEPTH, D, IN_COLS), D ** -0.5),
        'hy_short': nrm((DEPTH, SHORT_CONV, HY_COLS), SHORT_CONV ** -0.5),
        'hy_w1': nrm((DEPTH, HY_EMB, HY_FILTER_HIDDEN), HY_EMB ** -0.5),
        'hy_b1': nrm((DEPTH, HY_FILTER_HIDDEN), 0.02),
        'hy_f1': near_one((DEPTH, HY_FILTER_HIDDEN)),
        'hy_w2': nrm((DEPTH, HY_FILTER_HIDDEN, HY_FILTER_HIDDEN), HY_FILTER_HIDDEN ** -0.5),
        'hy_b2': nrm((DEPTH, HY_FILTER_HIDDEN), 0.02),
        'hy_f2': near_one((DEPTH, HY_FILTER_HIDDEN)),
        'hy_w3': nrm((DEPTH, HY_FILTER_HIDDEN, HY_ORDER * 2 * HY_WIDTH), 0.02 * HY_FILTER_HIDDEN ** -0.5),
        'hy_skip': nrm((DEPTH, HY_ORDER, HY_WIDTH), 0.5),
        'da_lambda': nrm((DEPTH, 4, DA_QK), 0.1),
        'da_subln': near_one((DEPTH, DA_V)),
        'rw_shift': centre_tap + nrm((DEPTH, SHORT_CONV, RW_COLS), 0.2),
        'rw_w0': nrm((DEPTH, 2, RW_WIDTH), 1.0) - 3.0,
        'rw_w2': nrm((DEPTH, 2, RW_DECAY_RANK, RW_WIDTH), 0.1 * RW_DECAY_RANK ** -0.5),
        'rw_a0': nrm((DEPTH, 2, RW_WIDTH), 0.1),
        'rw_a2': nrm((DEPTH, 2, RW_ICL_RANK, RW_WIDTH), 0.1 * RW_ICL_RANK ** -0.5),
        'rw_g2': nrm((DEPTH, RW_GATE_RANK, RW_WIDTH), RW_GATE_RANK ** -0.5),
        'rw_k_k': 1.0 + nrm((DEPTH, RW_WIDTH), 0.1),
        'rw_k_a': 1.0 + nrm((DEPTH, RW_WIDTH), 0.1),
        'rw_r_k': nrm((DEPTH, RW_HEADS, RW_HEAD), 0.1),
        'rw_gn_w': near_one((DEPTH, RW_WIDTH)),
        'rw_gn_b': nrm((DEPTH, RW_WIDTH), 0.01),
        'w_branch': nrm((DEPTH, MIX_WIDTH, D), RW_WIDTH ** -0.5),
        'w_out': nrm((DEPTH, D, D), D ** -0.5),
        'final_gain': near_one((D,)),
    }


def reference(x, c, ctx, c_ctx, w_mod, b_mod, norm_gain, w_ff_in, w_ff_out, w_in,
              hy_short, hy_w1, hy_b1, hy_f1, hy_w2, hy_b2, hy_f2, hy_w3, hy_skip,
              da_lambda, da_subln,
              rw_shift, rw_w0, rw_w2, rw_a0, rw_a2, rw_g2, rw_k_k, rw_k_a, rw_r_k, rw_gn_w, rw_gn_b,
              w_branch, w_out, final_gain):
    B, S, _ = x.shape
    ROWS = S // GRID_W
    row_pos = jnp.repeat(jnp.arange(ROWS, dtype=jnp.int32), GRID_W)
    col_pos = jnp.tile(jnp.arange(GRID_W, dtype=jnp.int32), ROWS)
    zero_state = jnp.zeros((B, RW_HEADS, RW_HEAD, RW_HEAD), jnp.float32)
    xc = ctx
    for l in range(DEPTH):
        last = l == DEPTH - 1
        mod = (jax.nn.silu(c) @ w_mod[l] + b_mod[l]).reshape(B, 1, N_MOD, D_MODEL)
        mod_c = (jax.nn.silu(c_ctx) @ w_mod[l] + b_mod[l]).reshape(N_MOD, D_MODEL)
        lam_init = 0.8 - 0.6 * math.exp(-0.3 * l)
        lq1, lk1, lq2, lk2 = [da_lambda[l, i].astype(jnp.float32) for i in range(4)]
        lam = jnp.exp(jnp.sum(lq1 * lk1)) - jnp.exp(jnp.sum(lq2 * lk2)) + lam_init
        hy_params = (hy_w1[l], hy_b1[l], hy_f1[l], hy_w2[l], hy_b2[l], hy_f2[l], hy_w3[l])
        rw_params = (rw_w0[l], rw_w2[l], rw_a0[l], rw_a2[l], rw_k_k[l], rw_k_a[l])
        shift_state = rw_shift[l][:, :RW_STATE_COLS]
        shift_rg = rw_shift[l][:, RW_STATE_COLS:]

        x = adaln_half_ffn(x, norm_gain[l, 0], mod[:, :, 0], mod[:, :, 1], mod[:, :, 2], w_ff_in[l, 0], w_ff_out[l, 0])
        xc = adaln_half_ffn(xc, norm_gain[l, 0], mod_c[0], mod_c[1], mod_c[2], w_ff_in[l, 0], w_ff_out[l, 0])

        u = modulate(rms_norm(x, norm_gain[l, 1]), mod[:, :, 3], mod[:, :, 4])
        uc = modulate(rms_norm(xc, norm_gain[l, 1]), mod_c[3], mod_c[4])
        p = u @ w_in[l]
        c0, c1 = (O_K, O_RW_RG) if last else (0, IN_COLS)
        pc = uc @ w_in[l][:, c0:c1]
        n_ctx = pc.shape[1]

        k_ctx = split_qk(take_cols(pc, O_K, O_V, c0))
        v_ctx = take_cols(pc, O_V, O_RW, c0).reshape(B, n_ctx, DA_HEADS, DA_V)
        terms_c = rwkv_state_terms(centred_conv(take_cols(pc, O_RW, O_RW_RG, c0), shift_state), *rw_params)

        y_h = hyena_mix(centred_conv(p[..., O_HY:O_Q], hy_short[l]), hy_params, hy_skip[l])

        q = axial_rope(split_qk(p[..., O_Q:O_K]), row_pos, col_pos)
        k = axial_rope(split_qk(p[..., O_K:O_V]), row_pos, col_pos)
        v = p[..., O_V:O_RW].reshape(B, S, DA_HEADS, DA_V)
        y_a = diff_out(diff_attn_latent(q, k, v, k_ctx, v_ctx, lam), da_subln[l], lam_init)

        terms = rwkv_state_terms(centred_conv(p[..., O_RW:O_RW_RG], shift_state), *rw_params)
        r, g = rwkv_receptance_gate(centred_conv(p[..., O_RW_RG:O_GATE], shift_rg), rw_g2[l])
        if last:
            sf_c, sb_c, _ = rwkv_bidir(terms_c, None, zero_state, zero_state)
        else:
            r_c, g_c = rwkv_receptance_gate(centred_conv(take_cols(pc, O_RW_RG, O_GATE, c0), shift_rg), rw_g2[l])
            sf_c, sb_c, yc_scan = rwkv_bidir(terms_c, r_c, zero_state, zero_state)
        _, _, y_scan = rwkv_bidir(terms, r, sf_c, sb_c)
        y_r = rwkv_out(y_scan, r, terms, g, rw_gn_w[l], rw_gn_b[l], rw_r_k[l])

        x = x + mod[:, :, 5] * merge_branches(p[..., O_GATE:], y_h, y_a, y_r, w_branch[l], w_out[l])
        x = adaln_half_ffn(x, norm_gain[l, 2], mod[:, :, 6], mod[:, :, 7], mod[:, :, 8], w_ff_in[l, 1], w_ff_out[l, 1])

        if not last:
            yc_h = hyena_mix(centred_conv(take_cols(pc, O_HY, O_Q, c0), hy_short[l]), hy_params, hy_skip[l])
            q_c = split_qk(take_cols(pc, O_Q, O_K, c0))
            yc_a = diff_out(diff_attn_core(q_c, k_ctx, v_ctx, lam), da_subln[l], lam_init)
            yc_r = rwkv_out(yc_scan, r_c, terms_c, g_c, rw_gn_w[l], rw_gn_b[l], rw_r_k[l])
            xc = xc + mod_c[5] * merge_branches(take_cols(pc, O_GATE, IN_COLS, c0), yc_h, yc_a, yc_r,
                                                w_branch[l], w_out[l])
            xc = adaln_half_ffn(xc, norm_gain[l, 2], mod_c[6], mod_c[7], mod_c[8], w_ff_in[l, 1], w_ff_out[l, 1])
    return rms_norm(x, final_gain)
```

```python
import math
import numpy as np

from contextlib import ExitStack
import numpy as np
import concourse.bass as bass
import concourse.mybir as mybir

F32 = mybir.dt.float32
BF16 = mybir.dt.bfloat16
AF = mybir.ActivationFunctionType
ALU = mybir.AluOpType
AX = mybir.AxisListType


class Buf:
    __slots__ = ("name", "t", "lw", "rd")

    def __init__(self, name, t=None):
        self.name = name
        self.t = t
        self.lw = None
        self.rd = []

    def __getitem__(self, k):
        return self.t[k]


class Op:
    __slots__ = ("eng", "fn", "dma", "deps", "sig", "sem", "val", "waits", "idx")

    def __init__(self, eng, fn, dma):
        self.eng, self.fn, self.dma = eng, fn, dma
        self.deps = []
        self.sig = False
        self.sem = None
        self.val = 0
        self.waits = []


class Prog:
    ENGS = ("pe", "dve", "act", "pool", "sp")

    def __init__(self, nc):
        self.nc = nc
        self.es = ExitStack()
        self.ops = []
        self.dma_keys = {}
        self.nbuf = 0

    def sb(self, shape, dt=F32, name=None):
        self.nbuf += 1
        name = name or f"sb{self.nbuf}"
        t = self.es.enter_context(self.nc.sbuf_tensor(name, list(shape), dt))
        return Buf(name, t)

    def ps(self, shape, dt=F32, name=None):
        self.nbuf += 1
        name = name or f"ps{self.nbuf}"
        t = self.es.enter_context(self.nc.psum_tensor(name, list(shape), dt))
        return Buf(name, t)

    def dram(self, name, shape, dt=F32, kind="Internal"):
        t = self.nc.dram_tensor(name, list(shape), dt, kind=kind)
        return Buf(name, t.ap())

    def _rec(self, eng, fn, reads, writes, dma=False, key=None):
        op = Op(eng, fn, dma)
        op.idx = len(self.ops)
        if dma:
            op.sem = key if key is not None else writes[0].name
        for b in reads:
            if b.lw is not None:
                op.deps.append((b.lw, "raw"))
        for b in writes:
            if b.lw is not None:
                op.deps.append((b.lw, "waw"))
            for r in b.rd:
                op.deps.append((r, "war"))
        for b in reads:
            b.rd.append(op)
        for b in writes:
            b.lw = op
            b.rd = []
        self.ops.append(op)
        return op

    def op(self, eng, fn, reads=(), writes=()):
        return self._rec(eng, fn, list(reads), list(writes))

    def dma(self, out_ap, in_ap, reads, writes, eng="sp", key=None, **kw):
        def fn(e):
            return e.dma_start(out=out_ap, in_=in_ap, **kw)
        return self._rec(eng, fn, list(reads), list(writes), dma=True, key=key)

    def build(self):
        nc = self.nc
        need = []
        for x in self.ops:
            for (p, kind) in x.deps:
                if p is x:
                    continue
                if not p.dma and not x.dma and p.eng == x.eng:
                    if p.eng == "pe" or kind != "raw":
                        continue
                need.append((x, p))
                p.sig = True
        cnt = {e: 0 for e in self.ENGS}
        dcnt = {}
        for o in self.ops:
            if o.dma:
                dcnt[o.sem] = dcnt.get(o.sem, 0) + 16
                o.val = dcnt[o.sem]
                o.sig = True
            elif o.sig:
                cnt[o.eng] += 1
                o.val = cnt[o.eng]
                o.sem = "eng_" + o.eng
        semnames = ["eng_" + e for e in self.ENGS if cnt[e] > 0] + list(dcnt.keys())
        sems = {}
        for i, n in enumerate(semnames):
            sems[n] = self.es.enter_context(nc.semaphore(f"s{i}"))
        self.nsems = len(sems)
        waited = {e: {} for e in self.ENGS}
        per_eng = {e: [] for e in self.ENGS}
        needmap = {}
        for (x, p) in need:
            needmap.setdefault(id(x), []).append(p)
        for x in self.ops:
            w = waited[x.eng]
            best = {}
            for p in needmap.get(id(x), []):
                if w.get(p.sem, 0) >= p.val:
                    continue
                if best.get(p.sem, 0) < p.val:
                    best[p.sem] = p.val
            for s, v in best.items():
                w[s] = v
                x.waits.append((s, v))
            per_eng[x.eng].append(x)
        finals = [(s, v) for s, v in dcnt.items()]
        engobj = {"pe": "tensor", "dve": "vector", "act": "scalar", "pool": "gpsimd", "sp": "sync"}
        with nc.Block() as block:
            for e in self.ENGS:
                lst = per_eng[e]
                if not lst and e != "sp":
                    continue

                def body(eng, lst=lst, e=e):
                    for x in lst:
                        for (s, v) in x.waits:
                            eng.wait_ge(sems[s], v)
                        ins = x.fn(eng)
                        if x.sig:
                            ins.then_inc(sems[x.sem], 16 if x.dma else 1)
                    if e == "sp":
                        for (s, v) in finals:
                            eng.wait_ge(sems[s], v)
                getattr(block, engobj[e])(body)
        self.es.close()
        return cnt, dcnt


EPS = 1e-6


def split_tokens(TT):
    n = (TT + 511) // 512
    sz = TT // n
    assert sz * n == TT
    return [(i * sz, sz) for i in range(n)]


class Env:
    pass


def setup_env(P, D, TT, KCmax, nvec):
    E = Env()
    E.D, E.TT = D, TT
    E.DC = D // 128
    E.tts = split_tokens(TT)
    E.ps = [P.ps([128, 512], F32, name=f"psb{i}") for i in range(8)]
    E.psi = 0
    E.wt = [P.sb([128, KCmax, 256], BF16, name=f"wt{i}") for i in range(3)]
    E.wi = 0
    E.actA = P.sb([128, KCmax * TT], BF16, name="actA")
    E.xin = [P.sb([128, TT], F32, name=f"xin{i}") for i in range(2)]
    E.xi = 0
    E.tmp = [P.sb([128, TT], F32, name=f"tmp{i}") for i in range(2)]
    E.ti = 0
    E.ost = [P.sb([128, TT], F32, name=f"ost{i}") for i in range(3)]
    E.oi = 0
    E.acc = P.sb([128, TT], F32, name="acc")
    E.rstd = P.sb([128, TT], F32, name="rstd")
    E.ones = P.sb([128, 128], F32, name="ones")
    P.op("dve", lambda e: e.memset(E.ones[:], 1.0), [], [E.ones])
    E.vec = P.sb([128, nvec, E.DC], F32, name="vec")
    return E


def rot(E, name, idx):
    lst = getattr(E, name)
    i = getattr(E, idx)
    setattr(E, idx, i + 1)
    return lst[i % len(lst)]


def psum_group(E):
    g = []
    for _ in E.tts:
        g.append(E.ps[E.psi % 8])
        E.psi += 1
    return g


def load_w(P, E, Wd, r0, KC, c0, width):
    wt = rot(E, "wt", "wi")
    src = Wd.t[r0:r0 + KC * 128, c0:c0 + width].rearrange("(c p) n -> p c n", p=128)
    P.dma(wt[:, :KC, :width], src, [Wd], [wt], eng="pool")
    return wt


def mm_chunk(P, E, wt, wo, width, KC, Xb, xview, psg, first=True, last=True, kbase=0):
    for k in range(KC):
        for ti, (t0, sz) in enumerate(E.tts):
            ps = psg[ti]
            P.op("pe", lambda e, ps=ps, k=k, t0=t0, sz=sz: e.matmul(
                ps[:width, :sz], wt[:, k, wo:wo + width], xview[:, kbase + k, t0:t0 + sz],
                start=(first and k == 0), stop=(last and k == KC - 1)), [wt, Xb], [ps])


def rms_stats(P, E, acc_ready_buf):
    D = E.D
    psg = psum_group(E)
    for ti, (t0, sz) in enumerate(E.tts):
        ps = psg[ti]
        P.op("pe", lambda e, ps=ps, t0=t0, sz=sz: e.matmul(ps[:, :sz], E.ones[:, :], E.acc[:, t0:t0 + sz], start=True, stop=True),
             [E.ones, E.acc], [ps])
        P.op("dve", lambda e, ps=ps, t0=t0, sz=sz: e.tensor_scalar(E.rstd[:, t0:t0 + sz], ps[:, :sz], 1.0 / D, EPS, ALU.mult, ALU.add),
             [ps], [E.rstd])
    P.op("act", lambda e: e.activation(E.rstd[:, :], E.rstd[:, :], AF.Sqrt), [E.rstd], [E.rstd])
    P.op("dve", lambda e: e.reciprocal(E.rstd[:, :], E.rstd[:, :]), [E.rstd], [E.rstd])


def norm_pass(P, E, src_chunks, NL, gs, sh, gsc, shc):
    TT, DC = E.TT, E.DC
    for c in range(DC):
        xc = rot(E, "xin", "xi")
        P.dma(xc[:, :], src_chunks[c].t, [src_chunks[c]], [xc])
        if c == 0:
            P.op("act", lambda e, xc=xc: e.activation(E.acc[:, :], xc[:, :], AF.Square), [xc], [E.acc])
        else:
            sq = rot(E, "tmp", "ti")
            P.op("act", lambda e, xc=xc, sq=sq: e.activation(sq[:, :], xc[:, :], AF.Square), [xc], [sq])
            P.op("dve", lambda e, sq=sq: e.tensor_tensor(E.acc[:, :], E.acc[:, :], sq[:, :], ALU.add), [sq, E.acc], [E.acc])
    rms_stats(P, E, E.acc)
    xv = E.actA.t[:, :DC * TT].rearrange("p (c t) -> p c t", t=TT)
    for c in range(DC):
        xc = rot(E, "xin", "xi")
        P.dma(xc[:, :], src_chunks[c].t, [src_chunks[c]], [xc])
        tm = rot(E, "tmp", "ti")
        for (a, b, g_, s_) in ((0, NL, gs, sh), (NL, TT, gsc, shc)):
            if b <= a:
                continue
            P.op("dve", lambda e, xc=xc, tm=tm, a=a, b=b, g_=g_, c=c: e.scalar_tensor_tensor(
                tm[:, a:b], xc[:, a:b], E.vec[:, g_, c:c + 1], E.rstd[:, a:b], ALU.mult, ALU.mult), [xc, E.vec, E.rstd], [tm])
            P.op("act", lambda e, tm=tm, a=a, b=b, s_=s_, c=c: e.activation(
                xv[:, c, a:b], tm[:, a:b], AF.Identity, bias=E.vec[:, s_, c:c + 1], scale=1.0), [tm, E.vec], [E.actA])
    return xv


def derive_gs(P, E, dst, g, sc):
    P.op("dve", lambda e: e.scalar_tensor_tensor(E.vec[:, dst, :], E.vec[:, sc, :], 1.0, E.vec[:, g, :], ALU.add, ALU.mult), [E.vec], [E.vec])


def derive_half(P, E, dst, src):
    P.op("dve", lambda e: e.tensor_scalar(E.vec[:, dst, :], E.vec[:, src, :], 0.5, None, ALU.mult), [E.vec], [E.vec])


def ffn_block(P, E, NL, src_ch, dst_ch, wfi, wfo, FF, aT_ch, gs, sh, gsc, shc, hg, hgc):
    DC, TT = E.DC, E.TT
    FC = FF // 128
    hv = norm_pass(P, E, src_ch, NL, gs, sh, gsc, shc)
    for f in range(FC):
        if f % 2 == 0:
            nb = min(2, FC - f) * 128
            wg = load_w(P, E, wfi, 0, DC, f * 128, nb)
            wu = load_w(P, E, wfi, 0, DC, FF + f * 128, nb)
        wo = (f % 2) * 128
        pg = psum_group(E)
        mm_chunk(P, E, wg, wo, 128, DC, E.actA, hv, pg)
        pu = psum_group(E)
        mm_chunk(P, E, wu, wo, 128, DC, E.actA, hv, pu)
        sg = rot(E, "tmp", "ti")
        ao = rot(E, "ost", "oi")
        for ti, (t0, sz) in enumerate(E.tts):
            P.op("act", lambda e, ti=ti, t0=t0, sz=sz, pg=pg, sg=sg: e.activation(sg[:, t0:t0 + sz], pg[ti][:, :sz], AF.Silu), [pg[ti]], [sg])
            P.op("dve", lambda e, ti=ti, t0=t0, sz=sz, pu=pu, sg=sg, ao=ao: e.tensor_tensor(ao[:, t0:t0 + sz], sg[:, t0:t0 + sz], pu[ti][:, :sz], ALU.mult), [pu[ti], sg], [ao])
        P.dma(aT_ch[f].t, ao[:, :], [ao], [aT_ch[f]], eng="pool", key=f"st{ao.name}")
    av = E.actA.t[:, :FC * TT].rearrange("p (c t) -> p c t", t=TT)
    for f in range(FC):
        P.dma(av[:, f, :], aT_ch[f].t, [aT_ch[f]], [E.actA], key="actA_ld")
    for c in range(DC):
        if c % 2 == 0:
            wt = load_w(P, E, wfo, 0, FC, c * 128, 256)
        wo = (c % 2) * 128
        pg = psum_group(E)
        mm_chunk(P, E, wt, wo, 128, FC, E.actA, av, pg)
        xc = rot(E, "xin", "xi")
        P.dma(xc[:, :], src_ch[c].t, [src_ch[c]], [xc])
        xo = rot(E, "ost", "oi")
        for ti, (t0, sz) in enumerate(E.tts):
            for (a, b, hgi) in ((t0, min(t0 + sz, NL), hg), (max(t0, NL), t0 + sz, hgc)):
                if b <= a:
                    continue
                P.op("dve", lambda e, ti=ti, t0=t0, a=a, b=b, hgi=hgi, pg=pg, xc=xc, xo=xo, c=c: e.scalar_tensor_tensor(
                    xo[:, a:b], pg[ti][:, a - t0:b - t0], E.vec[:, hgi, c:c + 1], xc[:, a:b], ALU.mult, ALU.add),
                    [pg[ti], E.vec, xc], [xo])
        P.dma(dst_ch[c].t, xo[:, :], [xo], [dst_ch[c]], key=f"st{xo.name}")


def chunks_of(ap, n, pfx):
    return [Buf(f"{pfx}{c}", ap[c * 128:(c + 1) * 128, :]) for c in range(n)]


def build_A(nc, D, FF, NP, TT, NL):
    P = Prog(nc)
    DC, FC = D // 128, FF // 128
    E = setup_env(P, D, TT, max(DC, FC), 18)
    xT = P.dram("xT", [D, TT], F32, kind="ExternalInput")
    vecs = P.dram("vecs", [128, 12, DC], F32, kind="ExternalInput")
    wfi = P.dram("wfi", [D, 2 * FF], F32, kind="ExternalInput")
    wfo = P.dram("wfo", [FF, D], F32, kind="ExternalInput")
    win = P.dram("win", [D, NP], F32, kind="ExternalInput")
    x1T = nc.dram_tensor("x1T", [D, TT], F32, kind="ExternalOutput").ap()
    pT = nc.dram_tensor("pT", [NP, TT], F32, kind="ExternalOutput").ap()
    aT = nc.dram_tensor("aT", [FF, TT], BF16, kind="Internal").ap()
    xT_ch = chunks_of(xT.t, DC, "xT")
    x1_ch = chunks_of(x1T, DC, "x1T")
    aT_ch = chunks_of(aT, FC, "aT")
    P.dma(E.vec[:, 0:12, :], vecs.t, [vecs], [E.vec])
    derive_gs(P, E, 12, 0, 3); derive_gs(P, E, 13, 0, 8); derive_gs(P, E, 16, 1, 6); derive_gs(P, E, 17, 1, 11)
    derive_half(P, E, 14, 4); derive_half(P, E, 15, 9)
    ffn_block(P, E, NL, xT_ch, x1_ch, wfi, wfo, FF, aT_ch, 12, 2, 13, 7, 14, 15)
    uv = norm_pass(P, E, x1_ch, NL, 16, 5, 17, 10)
    nchunks = (NP + 127) // 128
    for n in range(nchunks):
        if n % 2 == 0:
            nb = min(256, NP - n * 128)
            wt = load_w(P, E, win, 0, DC, n * 128, nb)
        wo = (n % 2) * 128
        width = min(128, NP - n * 128)
        pg = psum_group(E)
        mm_chunk(P, E, wt, wo, width, DC, E.actA, uv, pg)
        po = rot(E, "ost", "oi")
        for ti, (t0, sz) in enumerate(E.tts):
            if (ti + n) % 2 == 0:
                P.op("act", lambda e, ti=ti, t0=t0, sz=sz, pg=pg, po=po, width=width: e.copy(po[:width, t0:t0 + sz], pg[ti][:width, :sz]), [pg[ti]], [po])
            else:
                P.op("dve", lambda e, ti=ti, t0=t0, sz=sz, pg=pg, po=po, width=width: e.tensor_copy(po[:width, t0:t0 + sz], pg[ti][:width, :sz]), [pg[ti]], [po])
        pch = Buf(f"pT{n}", pT[n * 128:n * 128 + width, :])
        P.dma(pch.t, po[:width, :], [po], [pch], key=f"st{po.name}")
    return P.build()


def build_C(nc, D, FF, TT, NL, WH, WA, WR, last):
    P = Prog(nc)
    DC, FC = D // 128, FF // 128
    MIX = WH + WA + WR
    MC = MIX // 128
    E = setup_env(P, D, TT, max(DC, FC, MC), 24)
    x1T = P.dram("x1T", [D, TT], F32, kind="ExternalInput")
    yT = P.dram("yT", [MIX, TT], F32, kind="ExternalInput")
    vecs = P.dram("vecs", [128, 15, DC], F32, kind="ExternalInput")
    wg = P.dram("wg", [D, 3 * D], F32, kind="ExternalInput")
    wbr = P.dram("wbr", [MIX, D], F32, kind="ExternalInput")
    wo_ = P.dram("wo", [D, D], F32, kind="ExternalInput")
    wfi = P.dram("wfi", [D, 2 * FF], F32, kind="ExternalInput")
    wfo = P.dram("wfo", [FF, D], F32, kind="ExternalInput")
    x3T = nc.dram_tensor("x3T", [D, TT], F32, kind="ExternalOutput").ap()
    x2T = nc.dram_tensor("x2T", [D, TT], F32, kind="Internal").ap()
    gT = nc.dram_tensor("gT", [3 * D, TT], BF16, kind="Internal").ap()
    mT = nc.dram_tensor("mT", [D, TT], BF16, kind="Internal").ap()
    aT = nc.dram_tensor("aT", [FF, TT], BF16, kind="Internal").ap()
    x1_ch = chunks_of(x1T.t, DC, "x1T")
    y_ch = chunks_of(yT.t, MC, "yT")
    x2_ch = chunks_of(x2T, DC, "x2T")
    x3_ch = chunks_of(x3T, DC, "x3T")
    g_ch = chunks_of(gT, 3 * DC, "gT")
    m_ch = chunks_of(mT, DC, "mT")
    aT_ch = chunks_of(aT, FC, "aT")
    P.dma(E.vec[:, 0:15, :], vecs.t, [vecs], [E.vec])
    derive_gs(P, E, 16, 0, 3); derive_gs(P, E, 17, 0, 9); derive_gs(P, E, 18, 1, 6); derive_gs(P, E, 19, 1, 12)
    derive_half(P, E, 20, 7); derive_half(P, E, 21, 13)
    uv = norm_pass(P, E, x1_ch, NL, 16, 2, 17, 8)
    for n in range(3 * DC):
        if n % 2 == 0:
            wt = load_w(P, E, wg, 0, DC, n * 128, 256)
        wo = (n % 2) * 128
        pg = psum_group(E)
        mm_chunk(P, E, wt, wo, 128, DC, E.actA, uv, pg)
        go = rot(E, "ost", "oi")
        for ti, (t0, sz) in enumerate(E.tts):
            P.op("act", lambda e, ti=ti, t0=t0, sz=sz, pg=pg, go=go: e.activation(go[:, t0:t0 + sz], pg[ti][:, :sz], AF.Sigmoid), [pg[ti]], [go])
        P.dma(g_ch[n].t, go[:, :], [go], [g_ch[n]], eng="pool", key=f"st{go.name}")
    yv = E.actA.t[:, :MC * TT].rearrange("p (c t) -> p c t", t=TT)
    for c in range(MC):
        P.dma(yv[:, c, :], y_ch[c].t, [y_ch[c]], [E.actA], eng="pool", key="actA_ld")
    kb = [(0, WH // 128), (WH // 128, WA // 128), ((WH + WA) // 128, WR // 128)]
    for c in range(DC):
        mo = rot(E, "ost", "oi")
        for b, (k0, kc) in enumerate(kb):
            wt = load_w(P, E, wbr, k0 * 128, kc, c * 128, 128)
            pg = psum_group(E)
            mm_chunk(P, E, wt, 0, 128, kc, E.actA, yv, pg, kbase=k0)
            gl = rot(E, "tmp", "ti")
            glb = gl.t.bitcast(BF16)
            P.dma(glb[:, :TT], g_ch[b * DC + c].t, [g_ch[b * DC + c]], [gl])
            for ti, (t0, sz) in enumerate(E.tts):
                if b == 0:
                    P.op("dve", lambda e, ti=ti, t0=t0, sz=sz, pg=pg, glb=glb, gl=gl, mo=mo: e.tensor_tensor(mo[:, t0:t0 + sz], pg[ti][:, :sz], glb[:, t0:t0 + sz], ALU.mult), [pg[ti], gl], [mo])
                else:
                    t2 = rot(E, "xin", "xi")
                    P.op("dve", lambda e, ti=ti, t0=t0, sz=sz, pg=pg, glb=glb, gl=gl, t2=t2: e.tensor_tensor(t2[:, t0:t0 + sz], pg[ti][:, :sz], glb[:, t0:t0 + sz], ALU.mult), [pg[ti], gl], [t2])
                    P.op("dve", lambda e, t0=t0, sz=sz, t2=t2, mo=mo: e.tensor_tensor(mo[:, t0:t0 + sz], mo[:, t0:t0 + sz], t2[:, t0:t0 + sz], ALU.add), [t2, mo], [mo])
        P.dma(m_ch[c].t, mo[:, :], [mo], [m_ch[c]], eng="pool", key=f"st{mo.name}")
    mv = E.actA.t[:, :DC * TT].rearrange("p (c t) -> p c t", t=TT)
    for c in range(DC):
        P.dma(mv[:, c, :], m_ch[c].t, [m_ch[c]], [E.actA], key="actA_ld")
    for c in range(DC):
        if c % 2 == 0:
            wt = load_w(P, E, wo_, 0, DC, c * 128, 256)
        wo = (c % 2) * 128
        pg = psum_group(E)
        mm_chunk(P, E, wt, wo, 128, DC, E.actA, mv, pg)
        xc = rot(E, "xin", "xi")
        P.dma(xc[:, :], x1_ch[c].t, [x1_ch[c]], [xc])
        xo = rot(E, "ost", "oi")
        for ti, (t0, sz) in enumerate(E.tts):
            for (a, b, gi) in ((t0, min(t0 + sz, NL), 4), (max(t0, NL), t0 + sz, 10)):
                if b <= a:
                    continue
                P.op("dve", lambda e, ti=ti, t0=t0, a=a, b=b, gi=gi, pg=pg, xc=xc, xo=xo, c=c: e.scalar_tensor_tensor(
                    xo[:, a:b], pg[ti][:, a - t0:b - t0], E.vec[:, gi, c:c + 1], xc[:, a:b], ALU.mult, ALU.add),
                    [pg[ti], E.vec, xc], [xo])
        P.dma(x2_ch[c].t, xo[:, :], [xo], [x2_ch[c]], key=f"st{xo.name}")
    if not last:
        ffn_block(P, E, NL, x2_ch, x3_ch, wfi, wfo, FF, aT_ch, 18, 5, 19, 11, 20, 21)
    else:
        x3i = nc.dram_tensor("x3i", [D, TT], F32, kind="Internal").ap()
        x3i_ch = chunks_of(x3i, DC, "x3i")
        ffn_block(P, E, NL, x2_ch, x3i_ch, wfi, wfo, FF, aT_ch, 18, 5, 19, 11, 20, 21)
        for c in range(DC):
            xc = rot(E, "xin", "xi")
            P.dma(xc[:, :], x3i_ch[c].t, [x3i_ch[c]], [xc])
            if c == 0:
                P.op("act", lambda e, xc=xc: e.activation(E.acc[:, :], xc[:, :], AF.Square), [xc], [E.acc])
            else:
                sq = rot(E, "tmp", "ti")
                P.op("act", lambda e, xc=xc, sq=sq: e.activation(sq[:, :], xc[:, :], AF.Square), [xc], [sq])
                P.op("dve", lambda e, sq=sq: e.tensor_tensor(E.acc[:, :], E.acc[:, :], sq[:, :], ALU.add), [sq, E.acc], [E.acc])
        rms_stats(P, E, E.acc)
        for c in range(DC):
            xc = rot(E, "xin", "xi")
            P.dma(xc[:, :], x3i_ch[c].t, [x3i_ch[c]], [xc])
            xo = rot(E, "ost", "oi")
            P.op("dve", lambda e, xc=xc, xo=xo, c=c: e.scalar_tensor_tensor(
                xo[:, :], xc[:, :], E.vec[:, 14, c:c + 1], E.rstd[:, :], ALU.mult, ALU.mult), [xc, E.vec, E.rstd], [xo])
            P.dma(x3_ch[c].t, xo[:, :], [xo], [x3_ch[c]], key=f"st{xo.name}")
    return P.build()

import numpy as np

EPS = 1e-6


def build_at(nc, H, LQ, LC, lam_init, with_ctx_q):
    P = Prog(nc)
    LK = LQ + LC
    NKC = LK // 128
    q = P.dram("q", [H, 2, 64, LQ], F32, kind="ExternalInput")
    k = P.dram("k", [H, 2, 64, LQ], F32, kind="ExternalInput")
    kc = P.dram("kc", [H, 2, 64, LC], F32, kind="ExternalInput")
    v = P.dram("v", [H, LK, 128], F32, kind="ExternalInput")
    cs = P.dram("cs", [2, 64, LQ], F32, kind="ExternalInput")
    rt = P.dram("rt", [64, 64], F32, kind="ExternalInput")
    lamv = P.dram("lamv", [1, 256], F32, kind="ExternalInput")
    subln = P.dram("subln", [128, 1], F32, kind="ExternalInput")
    ya = nc.dram_tensor("ya", [H, 128, LQ], F32, kind="ExternalOutput").ap()
    if with_ctx_q:
        qc = P.dram("qc", [H, 2, 64, LC], F32, kind="ExternalInput")
        yac = nc.dram_tensor("yac", [H, 128, LC], F32, kind="ExternalOutput").ap()
    Qa = [P.sb([65, LK], BF16, name=f"Qa{m}") for m in range(2)]
    Ka = [P.sb([65, LK], BF16, name=f"Ka{m}") for m in range(2)]
    Vb = P.sb([128, NKC, 128], BF16, name="Vb")
    ps_s = [P.ps([128, 512], F32, name=f"pss{i}") for i in range(3)]
    ps_o = [P.ps([128, 512], F32, name=f"pso{i}") for i in range(2)]
    ps_l = [P.ps([128, 512], F32, name=f"psl{i}") for i in range(2)]
    ps_x = P.ps([128, 512], F32, name="psx")
    xin = [P.sb([64, 512], F32, name=f"xin{i}") for i in range(3)]
    csb = [P.sb([64, 2, 512], F32, name=f"csb{i}") for i in range(2)]
    t1 = [P.sb([64, 512], F32, name=f"t1_{i}") for i in range(2)]
    t2 = [P.sb([64, 512], F32, name=f"t2_{i}") for i in range(2)]
    sq = [P.sb([64, 512], F32, name=f"sq{i}") for i in range(2)]
    pt = [P.sb([128, 512], BF16, name=f"pt{i}") for i in range(3)]
    wk = [P.sb([128, 512], F32, name=f"wk{i}") for i in range(6)]
    yo = [P.sb([128, 512], F32, name=f"yo{i}") for i in range(2)]
    rtb = P.sb([64, 64], F32, name="rtb")
    ones65 = P.sb([64, 65], F32, name="ones65")
    onesf = P.sb([128, 128], F32, name="onesf")
    onesb = P.sb([128, 128], BF16, name="onesb")
    kmax2 = P.sb([65, 1], F32, name="kmax2")
    kmt = P.sb([65, 1], F32, name="kmt")
    lam = P.sb([128, 4], F32, name="lam")
    lrow = P.sb([1, 256], F32, name="lrow")
    lt = P.sb([1, 8], F32, name="lt")
    sub = P.sb([128, 1], F32, name="sub")
    cnt = {"x": 0, "c": 0, "t": 0, "s": 0, "p": 0, "w": 0, "y": 0, "ss": 0}

    def nxt(lst, key):
        b = lst[cnt[key] % len(lst)]
        cnt[key] += 1
        return b

    P.dma(rtb[:, :], rt.t, [rt], [rtb])
    P.dma(lrow[:, :], lamv.t, [lamv], [lrow])
    P.dma(sub[:, :], subln.t, [subln], [sub])
    P.op("dve", lambda e: e.memset(ones65[:, :], 1.0), [], [ones65])
    P.op("dve", lambda e: e.memset(onesf[:, :], 1.0), [], [onesf])
    P.op("dve", lambda e: e.memset(onesb[:, :], 1.0), [], [onesb])
    P.op("dve", lambda e: e.memset(lt[:, :], 0.0), [], [lt])
    lj = P.sb([1, 64], F32, name="lj")
    for i in range(2):
        P.op("dve", lambda e, i=i: e.scalar_tensor_tensor(lj[:, :], lrow[:, 128 * i:128 * i + 64], 1.0, lrow[:, 128 * i + 64:128 * i + 128],
                                                          ALU.mult, ALU.mult, accum_out=lt[:, i:i + 1]), [lrow, lt], [lj, lt])
    P.op("act", lambda e: e.activation(lt[:, 2:4], lt[:, 0:2], AF.Exp), [lt], [lt])
    P.op("dve", lambda e: e.tensor_tensor(lt[:, 4:5], lt[:, 3:4], lt[:, 2:3], ALU.subtract), [lt], [lt])
    P.op("dve", lambda e: e.tensor_scalar(lt[:, 4:5], lt[:, 4:5], -float(lam_init), None, ALU.add), [lt], [lt])
    P.op("pe", lambda e: e.matmul(ps_x[:, 0:1], onesf[0:1, :], lt[0:1, 4:5], start=True, stop=True), [onesf, lt], [ps_x])
    P.op("dve", lambda e: e.tensor_copy(lam[:, 0:1], ps_x[:, 0:1]), [ps_x], [lam])
    P.op("dve", lambda e: e.tensor_scalar(lam[:, 1:2], sub[:, 0:1], 1.0 - float(lam_init), None, ALU.mult), [sub], [lam])

    for h in range(H):
        P.dma(Vb[:, :, :], v.t[h].rearrange("(c p) e -> p c e", p=128), [v], [Vb], eng="pool")
        P.op("dve", lambda e: e.memset(kmax2[:, :], 0.0), [], [kmax2])
        for m in range(2):
            P.op("dve", lambda e, m=m: e.memset(Ka[m][64:65, :], -1.0), [], [Ka[m]])

        def rope_chunk(src_ap, src_buf, dst, c0, n, use_rope, c_tab0, is_q, m):
            x = nxt(xin, "x")
            P.dma(x[:, :n], src_ap, [src_buf], [x])
            s = nxt(sq, "ss")
            P.op("act", lambda e: e.activation(s[:, :n], x[:, :n], AF.Square), [x], [s])
            P.op("pe", lambda e: e.matmul(ps_x[:65, :n], ones65[:, :], s[:, :n], start=True, stop=True), [ones65, s], [ps_x])
            if use_rope:
                ct = nxt(csb, "c")
                P.dma(ct[:, :, :n], cs.t[:, :, c_tab0:c_tab0 + n].rearrange("a d t -> d a t"), [cs], [ct])
                pr = nxt(ps_s, "s")
                P.op("pe", lambda e: e.matmul(pr[:64, :n], rtb[:, :], x[:, :n], start=True, stop=True), [rtb, x], [pr])
                a = nxt(t1, "t")
                b = t2[(cnt["t"] - 1) % 2]
                P.op("dve", lambda e: e.tensor_tensor(a[:, :n], x[:, :n], ct[:, 0, :n], ALU.mult), [x, ct], [a])
                P.op("dve", lambda e: e.tensor_tensor(b[:, :n], pr[:64, :n], ct[:, 1, :n], ALU.mult), [pr, ct], [b])
                P.op("dve", lambda e: e.tensor_tensor(dst[0:64, c0:c0 + n], a[:, :n], b[:, :n], ALU.add), [a, b], [dst])
            else:
                P.op("dve", lambda e: e.tensor_copy(dst[0:64, c0:c0 + n], x[:, :n]), [x], [dst])
            if is_q:
                P.op("act", lambda e: e.activation(dst[64:65, c0:c0 + n], ps_x[64:65, :n], AF.Sqrt, scale=kmax2[64:65, 0:1]), [ps_x, kmax2], [dst])
            else:
                P.op("dve", lambda e: e.reduce_max(kmt[:, :], ps_x[:65, :n], axis=AX.X), [ps_x], [kmt])
                P.op("dve", lambda e: e.tensor_tensor(kmax2[:, :], kmax2[:, :], kmt[:, :], ALU.max), [kmt, kmax2], [kmax2])

        for m in range(2):
            for c0 in range(0, LQ, 512):
                n = min(512, LQ - c0)
                rope_chunk(k.t[h, m, :, c0:c0 + n], k, Ka[m], c0, n, True, c0, False, m)
            for c0 in range(0, LC, 512):
                n = min(512, LC - c0)
                rope_chunk(kc.t[h, m, :, c0:c0 + n], kc, Ka[m], LQ + c0, n, False, 0, False, m)
        for m in range(2):
            for c0 in range(0, LQ, 512):
                n = min(512, LQ - c0)
                rope_chunk(q.t[h, m, :, c0:c0 + n], q, Qa[m], c0, n, True, c0, True, m)
            if with_ctx_q:
                for c0 in range(0, LC, 512):
                    n = min(512, LC - c0)
                    rope_chunk(qc.t[h, m, :, c0:c0 + n], qc, Qa[m], LQ + c0, n, False, 0, True, m)

        def attend(qc0, n, kchunks, out_ap):
            for m in range(2):
                for j, kci in enumerate(kchunks):
                    ss = nxt(ps_s, "s")
                    P.op("pe", lambda e, ss=ss, kci=kci, m=m: e.matmul(ss[:, :n], Ka[m][0:65, kci * 128:(kci + 1) * 128], Qa[m][0:65, qc0:qc0 + n],
                                                                   start=True, stop=True), [Ka[m], Qa[m]], [ss])
                    p_ = nxt(pt, "p")
                    P.op("act", lambda e, ss=ss, p_=p_: e.activation(p_[:, :n], ss[:, :n], AF.Exp, scale=0.125), [ss], [p_])
                    P.op("pe", lambda e, p_=p_, kci=kci, j=j, m=m: e.matmul(ps_o[m][:, :n], Vb[:, kci, :], p_[:, :n], start=(j == 0), stop=(j == len(kchunks) - 1)),
                         [Vb, p_], [ps_o[m]])
                    P.op("pe", lambda e, p_=p_, j=j, m=m: e.matmul(ps_l[m][:, :n], onesb[:, :], p_[:, :n], start=(j == 0), stop=(j == len(kchunks) - 1)),
                         [onesb, p_], [ps_l[m]])
            rl = [nxt(wk, "w") for _ in range(2)]
            o = [nxt(wk, "w") for _ in range(2)]
            for m in range(2):
                P.op("dve", lambda e, m=m: e.reciprocal(rl[m][:, :n], ps_l[m][:, :n]), [ps_l[m]], [rl[m]])
                P.op("dve", lambda e, m=m: e.tensor_tensor(o[m][:, :n], ps_o[m][:, :n], rl[m][:, :n], ALU.mult), [ps_o[m], rl[m]], [o[m]])
            d = nxt(wk, "w")
            P.op("dve", lambda e: e.scalar_tensor_tensor(d[:, :n], o[1][:, :n], lam[:, 0:1], o[0][:, :n], ALU.mult, ALU.add), [o[0], o[1], lam], [d])
            s2 = nxt(wk, "w")
            P.op("act", lambda e: e.activation(s2[:, :n], d[:, :n], AF.Square), [d], [s2])
            P.op("pe", lambda e: e.matmul(ps_x[:, :n], onesf[:, :], s2[:, :n], start=True, stop=True), [onesf, s2], [ps_x])
            r = rl[0]
            P.op("dve", lambda e: e.tensor_scalar(r[:, :n], ps_x[:, :n], 1.0 / 128, EPS, ALU.mult, ALU.add), [ps_x], [r])
            P.op("act", lambda e: e.activation(r[:, :n], r[:, :n], AF.Sqrt), [r], [r])
            P.op("dve", lambda e: e.reciprocal(r[:, :n], r[:, :n]), [r], [r])
            y = nxt(yo, "y")
            P.op("dve", lambda e: e.scalar_tensor_tensor(y[:, :n], d[:, :n], lam[:, 1:2], r[:, :n], ALU.mult, ALU.mult), [d, lam, r], [y])
            ob = Buf("yout", out_ap)
            P.dma(out_ap, y[:, :n], [y], [ob], key=f"st{y.name}")

        allk = list(range(NKC))
        for c0 in range(0, LQ, 512):
            n = min(512, LQ - c0)
            attend(c0, n, allk, ya[h, :, c0:c0 + n])
        if with_ctx_q:
            ck = list(range(LQ // 128, NKC))
            for c0 in range(0, LC, 512):
                n = min(512, LC - c0)
                attend(LQ + c0, n, ck, yac[h, :, c0:c0 + n])
    return P.build()


def rope_tables(LQ, grid_w=64, theta=10000.0):
    t = np.arange(LQ)
    row = (t // grid_w).astype(np.float32)
    col = (t % grid_w).astype(np.float32)
    inv = (theta ** (-np.arange(16, dtype=np.float32) / 16)).astype(np.float32)
    ang = np.zeros((64, LQ), np.float32)
    for d in range(64):
        pos = row if d < 32 else col
        ang[d] = pos * inv[d % 16]
    cs = np.stack([np.cos(ang), np.sin(ang)]).astype(np.float32)
    rt = np.zeros((64, 64), np.float32)
    for d in range(64):
        if d % 32 < 16:
            rt[d + 16, d] = -1.0
        else:
            rt[d - 16, d] = 1.0
    return cs, rt

import numpy as np
import math

MAGIC = 12582912.0
TWO_PI = 2.0 * math.pi


def build_hy(nc, L, C=128, CHP=64):
    P = Prog(nc)
    NB = L // 128
    NI = 2 * L - 1
    NE = 2 * NB - 1
    TW = NE * 128
    hp = P.dram("hp", [L + 2, 3 * C], F32, kind="ExternalInput")
    swb_d = P.dram("swb", [128, 3, 3, C], F32, kind="ExternalInput")
    skipb_d = P.dram("skipb", [128, 2, C], F32, kind="ExternalInput")
    zT = P.dram("zT", [33, NI], F32, kind="ExternalInput")
    w1_d = P.dram("w1", [33, 64], F32, kind="ExternalInput")
    v1_d = P.dram("v1", [64, 4], F32, kind="ExternalInput")
    w2_d = P.dram("w2", [64, 64], F32, kind="ExternalInput")
    w3_d = P.dram("w3c", [64, 2, 2, C], F32, kind="ExternalInput")
    win_d = P.dram("win", [C, NI], F32, kind="ExternalInput")
    jm_d = P.dram("jm", [128, 128], F32, kind="ExternalInput")
    yh = nc.dram_tensor("yh", [128, C, NB], F32, kind="ExternalOutput").ap()
    Hd_t = [nc.dram_tensor(f"Hd{o}", [C, 2 * L], BF16, kind="Internal") for o in range(2)]
    Hd = [Buf(f"Hd{o}", Hd_t[o].ap()) for o in range(2)]

    ps = [P.ps([128, 512], F32, name=f"psb{i}") for i in range(8)]
    pc = {"i": 0}

    def nps():
        b = ps[pc["i"] % 8]
        pc["i"] += 1
        return b

    swb = P.sb([128, 3, 3, C], F32, name="swb_s")
    skipb = P.sb([128, 2, C], F32, name="skipb_s")
    w1 = P.sb([33, 64], F32, name="w1_s")
    v1 = P.sb([64, 8], F32, name="v1_s")
    w2 = P.sb([64, 64], F32, name="w2_s")
    w3 = P.sb([64, 2, 2, C], F32, name="w3_s")
    jf = P.sb([128, 128], F32, name="jf")
    jb = P.sb([128, 128], BF16, name="jb")
    P.dma(swb[:], swb_d.t, [swb_d], [swb])
    P.dma(skipb[:], skipb_d.t, [skipb_d], [skipb])
    P.dma(w1[:], w1_d.t, [w1_d], [w1])
    P.dma(v1[:, 0:4], v1_d.t, [v1_d], [v1])
    P.dma(w2[:], w2_d.t, [w2_d], [w2])
    P.dma(w3[:], w3_d.t, [w3_d], [w3])
    P.dma(jf[:], jm_d.t, [jm_d], [jf])
    P.op("dve", lambda e: e.tensor_copy(jb[:], jf[:]), [jf], [jb])

    zin = [P.sb([33, 512], F32, name=f"zin{i}") for i in range(2)]
    fa = [P.sb([64, 512], F32, name=f"fa{i}") for i in range(2)]
    fq = [P.sb([64, 512], F32, name=f"fq{i}") for i in range(2)]
    fh = [P.sb([64, 512], F32, name=f"fh{i}") for i in range(4)]
    wn = [P.sb([C, 512], F32, name=f"wn{i}") for i in range(2)]
    hb = [P.sb([C, 512], BF16, name=f"hb{i}") for i in range(4)]
    fc = {"a": 0, "h": 0, "b": 0}

    def sin_layer(psrc, n, bcol, fcol):
        a = fa[fc["a"] % 2]
        q_ = fq[fc["a"] % 2]
        fc["a"] += 1
        h = fh[fc["h"] % 4]
        fc["h"] += 1
        P.op("dve", lambda e: e.tensor_scalar(a[:, :n], psrc[:64, :n], v1[:, bcol:bcol + 1], v1[:, fcol:fcol + 1], ALU.add, ALU.mult), [psrc, v1], [a])
        P.op("dve", lambda e: e.tensor_scalar(q_[:, :n], a[:, :n], 1.0 / TWO_PI, MAGIC, ALU.mult, ALU.add), [a], [q_])
        P.op("dve", lambda e: e.tensor_scalar(q_[:, :n], q_[:, :n], MAGIC, -TWO_PI, ALU.subtract, ALU.mult), [q_], [q_])
        P.op("dve", lambda e: e.tensor_tensor(a[:, :n], a[:, :n], q_[:, :n], ALU.add), [a, q_], [a])
        P.op("dve", lambda e: e.tensor_scalar(a[:, :n], a[:, :n], 3.1415925, -3.1415925, ALU.min, ALU.max), [a], [a])
        P.op("act", lambda e: e.activation(h[:, :n], a[:, :n], AF.Sin), [a], [h])
        return h

    for ci, i0 in enumerate(range(0, NI, 512)):
        n = min(512, NI - i0)
        z = zin[ci % 2]
        P.dma(z[:, :n], zT.t[:, i0:i0 + n], [zT], [z])
        w_ = wn[ci % 2]
        P.dma(w_[:, :n], win_d.t[:, i0:i0 + n], [win_d], [w_])
        p1 = nps()
        P.op("pe", lambda e, p1=p1, z=z, n=n: e.matmul(p1[:64, :n], w1[:, :], z[:, :n], start=True, stop=True), [w1, z], [p1])
        h1 = sin_layer(p1, n, 0, 1)
        p2 = nps()
        P.op("pe", lambda e, p2=p2, h1=h1, n=n: e.matmul(p2[:64, :n], w2[:, :], h1[:, :n], start=True, stop=True), [w2, h1], [p2])
        h2 = sin_layer(p2, n, 2, 3)
        nbw = max(0, min(n, (L - 1) - i0))
        for o in range(2):
            p3 = nps()
            if nbw > 0:
                P.op("pe", lambda e, p3=p3, h2=h2, o=o, nbw=nbw: e.matmul(p3[:C, :nbw], w3[:, o, 1, :], h2[:, :nbw], start=True, stop=True), [w3, h2], [p3])
            if nbw < n:
                P.op("pe", lambda e, p3=p3, h2=h2, o=o, nbw=nbw, n=n: e.matmul(p3[:C, nbw:n], w3[:, o, 0, :], h2[:, nbw:n], start=True, stop=True), [w3, h2], [p3])
            hh = hb[fc["b"] % 4]
            fc["b"] += 1
            P.op("dve", lambda e, p3=p3, hh=hh, w_=w_, n=n: e.tensor_tensor(hh[:, :n], p3[:C, :n], w_[:, :n], ALU.mult), [p3, w_], [hh])
            P.dma(Hd[o].t[:, i0:i0 + n], hh[:, :n], [hh], [Hd[o]], key=f"st{hh.name}")

    FW = NB * CHP
    sh = [P.sb([128, NB, CHP], F32, name=f"sh{i}") for i in range(3)]
    vs = P.sb([128, NB, CHP], BF16, name="vs")
    vr = P.sb([128, NB, CHP], BF16, name="vr")
    zz = P.sb([128, NB, CHP], BF16, name="zz")
    xs = P.sb([128, NB, CHP], F32, name="xs")
    tt = [P.sb([128, TW], BF16, name=f"tt{i}") for i in range(3)]
    ep = [P.sb([128, 8, NB], F32, name=f"ep{i}") for i in range(3)]
    tc_ = {"i": 0, "e": 0}

    def short_conv(xi, c0, dst, dst_buf):
        for s in range(3):
            src = hp.t[s:s + L, xi * C + c0: xi * C + c0 + CHP].rearrange("(b j) c -> j b c", j=128)
            P.dma(sh[s][:, :, :], src, [hp], [sh[s]])
            wb = swb[:, xi, s, c0:c0 + CHP].unsqueeze(1).broadcast_to([128, NB, CHP])
            P.op("dve", lambda e, s=s, wb=wb: e.tensor_tensor(sh[s][:, :, :], sh[s][:, :, :], wb, ALU.mult), [sh[s], swb], [sh[s]])
        P.op("dve", lambda e: e.tensor_tensor(sh[0][:, :, :], sh[0][:, :, :], sh[1][:, :, :], ALU.add), [sh[0], sh[1]], [sh[0]])
        P.op("dve", lambda e: e.tensor_tensor(dst, sh[0][:, :, :], sh[2][:, :, :], ALU.add), [sh[0], sh[2]], [dst_buf])

    def reverse_j(src, dst):
        sv = src.t.rearrange("p b c -> p (b c)")
        dv = dst.t.rearrange("p b c -> p (b c)")
        for k, f0 in enumerate(range(0, FW, 512)):
            n = min(512, FW - f0)
            pr = nps()
            P.op("pe", lambda e, pr=pr, f0=f0, n=n: e.matmul(pr[:, :n], jb[:, :], sv[:, f0:f0 + n], start=True, stop=True), [jb, src], [pr])
            if k % 2 == 0:
                P.op("act", lambda e, pr=pr, f0=f0, n=n: e.copy(dv[:, f0:f0 + n], pr[:, :n]), [pr], [dst])
            else:
                P.op("dve", lambda e, pr=pr, f0=f0, n=n: e.tensor_copy(dv[:, f0:f0 + n], pr[:, :n]), [pr], [dst])

    def long_conv(o, c0, urev, epilogue):
        for g in range(CHP // 8):
            pg = nps()
            for c8 in range(8):
                c = g * 8 + c8
                T = tt[tc_["i"] % 3]
                tc_["i"] += 1
                src = bass.AP(Hd_t[o], (c0 + c) * 2 * L, [[1, 128], [1, TW]])
                P.dma(T[:, :], src, [Hd[o]], [T])
                order = [NB - 1] + [e_ for e_ in range(NE) if e_ != NB - 1]
                for k, e_ in enumerate(order):
                    d = e_ - (NB - 1)
                    bt0 = max(0, d)
                    n = NB - abs(d)
                    bs0 = bt0 - d
                    P.op("pe", lambda e, T=T, e_=e_, pg=pg, c8=c8, bt0=bt0, n=n, bs0=bs0, c=c, k=k: e.matmul(
                        pg[:, c8 * NB + bt0: c8 * NB + bt0 + n], T[:, e_ * 128:(e_ + 1) * 128], urev[:, bs0:bs0 + n, c],
                        start=(k == 0), stop=(k == NE - 1), skip_group_check=True), [T, urev], [pg])
            epilogue(g, pg)

    for c0 in range(0, C, CHP):
        short_conv(0, c0, vs[:, :, :], vs)
        short_conv(1, c0, xs[:, :, :], xs)
        reverse_j(vs, vr)

        def epi1(g, pg, c0=c0):
            t = ep[tc_["e"] % 3]
            tc_["e"] += 1
            vv = vs[:, :, g * 8:(g + 1) * 8].rearrange("p b c -> p c b")
            xv = xs[:, :, g * 8:(g + 1) * 8].rearrange("p b c -> p c b")
            zv = zz[:, :, g * 8:(g + 1) * 8].rearrange("p b c -> p c b")
            sk = skipb[:, 0, c0 + g * 8:c0 + (g + 1) * 8].unsqueeze(2).broadcast_to([128, 8, NB])
            pv = pg[:, :8 * NB].rearrange("p (c b) -> p c b", b=NB)
            P.op("dve", lambda e: e.tensor_tensor(t[:, :, :], vv, sk, ALU.mult), [vs, skipb], [t])
            P.op("dve", lambda e: e.tensor_tensor(t[:, :, :], t[:, :, :], pv, ALU.add), [t, pg], [t])
            P.op("dve", lambda e: e.tensor_tensor(zv, t[:, :, :], xv, ALU.mult), [t, xs], [zz])
        long_conv(0, c0, vr, epi1)
        short_conv(2, c0, xs[:, :, :], xs)
        reverse_j(zz, vr)

        def epi2(g, pg, c0=c0):
            t = ep[tc_["e"] % 3]
            tc_["e"] += 1
            zv = zz[:, :, g * 8:(g + 1) * 8].rearrange("p b c -> p c b")
            xv = xs[:, :, g * 8:(g + 1) * 8].rearrange("p b c -> p c b")
            sk = skipb[:, 1, c0 + g * 8:c0 + (g + 1) * 8].unsqueeze(2).broadcast_to([128, 8, NB])
            pv = pg[:, :8 * NB].rearrange("p (c b) -> p c b", b=NB)
            P.op("dve", lambda e: e.tensor_tensor(t[:, :, :], zv, sk, ALU.mult), [zz, skipb], [t])
            P.op("dve", lambda e: e.tensor_tensor(t[:, :, :], t[:, :, :], pv, ALU.add), [t, pg], [t])
            P.op("dve", lambda e: e.tensor_tensor(t[:, :, :], t[:, :, :], xv, ALU.mult), [t, xs], [t])
            ob = Buf("yhout", None)
            P.dma(yh[:, c0 + g * 8:c0 + (g + 1) * 8, :], t[:, :, :], [t], [ob], key=f"st{t.name}")
        long_conv(1, c0, vr, epi2)
    return P.build()


def hyena_consts(L, C, c_lo, hy_width=1024, bands_n=16):
    idx = np.arange(2 * L - 1)
    t = np.abs(idx - (L - 1)).astype(np.float32)
    t_norm = t / np.float32(L)
    bands = np.linspace(1e-4, bands_n - 1, bands_n, dtype=np.float32)
    ang = (np.float32(2.0 * math.pi / L) * t[:, None]) * bands[None, :]
    z = np.concatenate([t_norm[:, None], np.cos(ang), -np.sin(ang)], axis=-1).astype(np.float32)
    deltas = np.abs(np.linspace(math.log(1e-2) / 1.5, math.log(1e-2) / 0.3, hy_width, dtype=np.float32))
    win = np.exp(-t_norm[None, :] * deltas[c_lo:c_lo + C, None]).astype(np.float32)
    jm = np.eye(128, dtype=np.float32)[::-1].copy()
    return np.ascontiguousarray(z.T), win, jm

import numpy as np
import math

GN_EPS = 64e-5
NG = 7


def build_rw(nc, LQ, LC, want_ctx_out, CH=8):
    P = Prog(nc)
    LT = LC + LQ
    segs = [(0, LC), (LC, LQ)]
    rinA = P.dram("rinA", [2, 3, 128, LT + 4], F32, kind="ExternalInput")
    rinB = P.dram("rinB", [2, 4, 128, LT + 4], F32, kind="ExternalInput")
    GSRC = {0: (rinA, 0), 1: (rinA, 1), 4: (rinA, 2), 2: (rinB, 0), 3: (rinB, 1), 5: (rinB, 2), 6: (rinB, 3)}
    swt_d = P.dram("swt", [128, NG, 3], F32, kind="ExternalInput")
    pv_d = P.dram("pv", [128, 12], F32, kind="ExternalInput")
    w2_d = P.dram("w2t", [128, 128], F32, kind="ExternalInput")
    a2_d = P.dram("a2t", [128, 128], F32, kind="ExternalInput")
    g2a_d = P.dram("g2a", [128, 128], F32, kind="ExternalInput")
    g2b_d = P.dram("g2b", [32, 128], F32, kind="ExternalInput")
    cm_d = P.dram("cm", [3, 128, 128], F32, kind="ExternalInput")
    yr = nc.dram_tensor("yr", [128, LQ], F32, kind="ExternalOutput").ap()
    if want_ctx_out:
        yrc = nc.dram_tensor("yrc", [128, LC], F32, kind="ExternalOutput").ap()
    rows_t = [[nc.dram_tensor(f"rows{h}{d}", [LT, 320], F32, kind="Internal") for d in range(2)] for h in range(2)]
    rows = [[Buf(f"rows{h}{d}", rows_t[h][d].ap()) for d in range(2)] for h in range(2)]
    vT = [Buf(f"vT{h}", nc.dram_tensor(f"vT{h}", [128, LT], F32, kind="Internal").ap()) for h in range(2)]
    yT = [Buf(f"yT{h}", nc.dram_tensor(f"yT{h}", [128, LT], F32, kind="Internal").ap()) for h in range(2)]

    ps = [P.ps([128, 512], F32, name=f"psb{i}") for i in range(8)]
    pc = {"i": 0}

    def nps():
        b = ps[pc["i"] % 8]
        pc["i"] += 1
        return b

    swt = P.sb([128, NG, 3], F32, name="swt_s")
    pv = P.sb([128, 16], F32, name="pv_s")
    w2 = P.sb([128, 128], F32, name="w2_s")
    a2 = P.sb([128, 128], F32, name="a2_s")
    g2a = P.sb([128, 128], F32, name="g2a_s")
    g2b = P.sb([32, 128], F32, name="g2b_s")
    cm = P.sb([128, 3, 128], F32, name="cm_s")
    P.dma(swt[:], swt_d.t, [swt_d], [swt])
    P.dma(pv[:, 0:12], pv_d.t, [pv_d], [pv])
    P.dma(w2[:], w2_d.t, [w2_d], [w2])
    P.dma(a2[:], a2_d.t, [a2_d], [a2])
    P.dma(g2a[:], g2a_d.t, [g2a_d], [g2a])
    P.dma(g2b[:], g2b_d.t, [g2b_d], [g2b])
    P.dma(cm[:], cm_d.t.rearrange("a p n -> p a n"), [cm_d], [cm])
    ident, jm, bones = cm[:, 0, :], cm[:, 1, :], cm[:, 2, :]
    bonus = P.sb([128, LT], F32, name="bonus")
    gout = P.sb([128, LT], F32, name="gout")

    NW = 26
    xin = [P.sb([128, 516], F32, name=f"xin{i}") for i in range(NG + 2)]
    wkb = [P.sb([128, 512], F32, name=f"wk{i}") for i in range(NW)]
    stg = [P.sb([128, 5, 128], F32, name=f"stg{i}") for i in range(2)]
    cnt = {"x": 0, "w": 0, "s": 0}

    def nw():
        b = wkb[cnt["w"] % NW]
        cnt["w"] += 1
        return b

    def conv(d, g, col0, n):
        x = xin[cnt["x"] % len(xin)]
        cnt["x"] += 1
        rsrc, gi = GSRC[g]
        P.dma(x[:, :n + 2], rsrc.t[d, gi, :, col0:col0 + n + 2], [rsrc], [x])
        u = nw()
        taps = (0, 1, 2) if d == 0 else (2, 1, 0)
        P.op("act", lambda e: e.activation(u[:, :n], x[:, 0:n], AF.Identity, scale=swt[:, g, taps[0]:taps[0] + 1]), [x, swt], [u])
        P.op("dve", lambda e: e.scalar_tensor_tensor(u[:, :n], x[:, 1:n + 1], swt[:, g, taps[1]:taps[1] + 1], u[:, :n], ALU.mult, ALU.add), [x, swt, u], [u])
        P.op("dve", lambda e: e.scalar_tensor_tensor(u[:, :n], x[:, 2:n + 2], swt[:, g, taps[2]:taps[2] + 1], u[:, :n], ALU.mult, ALU.add), [x, swt, u], [u])
        return u

    def lowrank(u, wt, d, n, bias_col):
        p_ = nps()
        lo = d * 64
        P.op("pe", lambda e: e.matmul(p_[:, :n], wt[lo:lo + 64, :], u[lo:lo + 64, :n], start=True, stop=True), [wt, u], [p_])
        o = nw()
        P.op("act", lambda e: e.activation(o[:, :n], p_[:, :n], AF.Sigmoid, bias=pv[:, bias_col:bias_col + 1], scale=1.0), [p_, pv], [o])
        return o

    for d in range(2):
        for (s0, sl) in segs:
            for c0 in range(0, sl, 512):
                n = min(512, sl - c0)
                n0 = s0 + c0
                seg_i = 0 if s0 == 0 else 1
                pad0 = s0 + 2 * seg_i + c0
                uk = conv(d, 0, pad0, n)
                uv = conv(d, 1, pad0, n)
                uwd = conv(d, 2, pad0, n)
                uad = conv(d, 3, pad0, n)
                ur = conv(d, 4, pad0, n)
                P.op("act", lambda e, uwd=uwd, n=n: e.activation(uwd[:, :n], uwd[:, :n], AF.Tanh), [uwd], [uwd])
                dirs = (0, 1) if d == 0 else (1,)
                a_ = {}
                dec = None
                for dd in dirs:
                    a_[dd] = lowrank(uad, a2, dd, n, 2 + dd)
                sg = lowrank(uwd, w2, d, n, d)
                dec = nw()
                P.op("act", lambda e, dec=dec, sg=sg, n=n: e.activation(dec[:, :n], sg[:, :n], AF.Exp, scale=-math.exp(-0.5)), [sg], [dec])
                kkr = nw()
                P.op("dve", lambda e, kkr=kkr, uk=uk, n=n: e.tensor_scalar(kkr[:, :n], uk[:, :n], pv[:, 4:5], None, ALU.mult), [uk, pv], [kkr])
                sq = nw()
                P.op("act", lambda e, sq=sq, kkr=kkr, n=n: e.activation(sq[:, :n], kkr[:, :n], AF.Square), [kkr], [sq])
                pn = nps()
                P.op("pe", lambda e, pn=pn, sq=sq, n=n: e.matmul(pn[:, :n], bones, sq[:, :n], start=True, stop=True), [cm, sq], [pn])
                P.op("act", lambda e, pn=pn, sq=sq, n=n: e.activation(sq[:, :n], pn[:, :n], AF.Sqrt), [pn], [sq])
                P.op("dve", lambda e, sq=sq, n=n: e.tensor_scalar(sq[:, :n], sq[:, :n], 1e-12, None, ALU.max), [sq], [sq])
                P.op("dve", lambda e, sq=sq, n=n: e.reciprocal(sq[:, :n], sq[:, :n]), [sq], [sq])
                nkk = nw()
                P.op("dve", lambda e, nkk=nkk, kkr=kkr, sq=sq, n=n: e.scalar_tensor_tensor(nkk[:, :n], kkr[:, :n], -1.0, sq[:, :n], ALU.mult, ALU.mult), [kkr, sq], [nkk])
                bb = nw()
                P.op("dve", lambda e, bb=bb, nkk=nkk, ad=a_[d], n=n: e.scalar_tensor_tensor(bb[:, :n], nkk[:, :n], -1.0, ad[:, :n], ALU.mult, ALU.mult), [nkk, a_[d]], [bb])
                kd = {}
                for dd in dirs:
                    t_ = nw()
                    P.op("dve", lambda e, t_=t_, ad=a_[dd], n=n: e.tensor_scalar(t_[:, :n], ad[:, :n], -1.0, pv[:, 5:6], ALU.add, ALU.mult), [a_[dd], pv], [t_])
                    P.op("dve", lambda e, t_=t_, uk=uk, n=n: e.scalar_tensor_tensor(t_[:, :n], t_[:, :n], 1.0, uk[:, :n], ALU.add, ALU.mult), [t_, uk], [t_])
                    kd[dd] = t_
                if d == 0:
                    ks = nw()
                    P.op("dve", lambda e, ks=ks, kd=kd, n=n: e.tensor_tensor(ks[:, :n], kd[0][:, :n], kd[1][:, :n], ALU.add), [kd[0], kd[1]], [ks])
                    P.op("dve", lambda e, ks=ks, ur=ur, n=n: e.scalar_tensor_tensor(ks[:, :n], ks[:, :n], pv[:, 6:7], ur[:, :n], ALU.mult, ALU.mult), [ks, pv, ur], [ks])
                    pb = nps()
                    P.op("pe", lambda e, pb=pb, ks=ks, n=n: e.matmul(pb[:, :n], bones, ks[:, :n], start=True, stop=True), [cm, ks], [pb])
                    P.op("dve", lambda e, pb=pb, uv=uv, n=n, n0=n0: e.tensor_tensor(bonus[:, n0:n0 + n], pb[:, :n], uv[:, :n], ALU.mult), [pb, uv], [bonus])
                    ug = conv(d, 5, pad0, n)
                    ug2 = conv(d, 6, pad0, n)
                    P.op("act", lambda e, ug=ug, n=n: e.activation(ug[:, :n], ug[:, :n], AF.Sigmoid), [ug], [ug])
                    P.op("act", lambda e, ug2=ug2, n=n: e.activation(ug2[:32, :n], ug2[:32, :n], AF.Sigmoid), [ug2], [ug2])
                    pg_ = nps()
                    P.op("pe", lambda e, pg_=pg_, ug=ug, n=n: e.matmul(pg_[:, :n], g2a[:, :], ug[:, :n], start=True, stop=False), [g2a, ug], [pg_])
                    P.op("pe", lambda e, pg_=pg_, ug2=ug2, n=n: e.matmul(pg_[:, :n], g2b[:, :], ug2[:32, :n], start=False, stop=True), [g2b, ug2], [pg_])
                    P.op("act", lambda e, pg_=pg_, n=n, n0=n0: e.copy(gout[:, n0:n0 + n], pg_[:, :n]), [pg_], [gout])
                for h in range(2):
                    P.dma(vT[h].t[d * 64:(d + 1) * 64, n0:n0 + n], uv[h * 64:(h + 1) * 64, :n], [uv], [vT[h]], key=f"vst{h}")
                srcs = [dec, nkk, bb, kd[d], ur]
                for b0 in range(0, n, 128):
                    st = stg[cnt["s"] % 2]
                    cnt["s"] += 1
                    for qi, sb_ in enumerate(srcs):
                        if qi % 4 == 0:
                            pt_ = nps()
                        P.op("pe", lambda e, pt_=pt_, sb_=sb_, b0=b0, qi=qi: e.matmul(pt_[:, (qi % 4) * 128:(qi % 4 + 1) * 128], sb_[:, b0:b0 + 128], ident, start=True, stop=True),
                             [sb_, cm], [pt_])
                        if qi == 3 or qi == 4:
                            lo, hi = (0, 4) if qi == 3 else (4, 5)
                            if qi == 3:
                                P.op("dve", lambda e, pt_=pt_, st=st: e.tensor_copy(st[:, 0:4, :], pt_[:, 0:512].rearrange("p (q c) -> p q c", c=128)), [pt_], [st])
                            else:
                                P.op("act", lambda e, pt_=pt_, st=st: e.copy(st[:, 4, :], pt_[:, 0:128]), [pt_], [st])
                    for h in range(2):
                        dst = rows[h][d].t[n0 + b0:n0 + b0 + 128, :].rearrange("t (q k) -> t q k", k=64)
                        P.dma(dst, st[:, :, h * 64:(h + 1) * 64], [st], [rows[h][d]], key=f"rst{h}{d}")

    S = [P.sb([128, 64], F32, name=f"S{h}") for h in range(2)]
    junk = [P.sb([128, 64], F32, name=f"junk{h}") for h in range(2)]
    sa = [P.sb([128, 1], F32, name=f"sa{h}") for h in range(2)]
    RB = [[P.sb([128, CH, 320], F32, name=f"RB{h}{i}") for i in range(2)] for h in range(2)]
    VC = [[P.sb([128, CH], F32, name=f"VC{h}{i}") for i in range(2)] for h in range(2)]
    YC = [[P.sb([128, CH], F32, name=f"YC{h}{i}") for i in range(2)] for h in range(2)]
    for h in range(2):
        P.op("dve", lambda e, h=h: e.memset(S[h][:, :], 0.0), [], [S[h]])
    nchunk = LT // CH
    for ci in range(nchunk):
        n0 = ci * CH
        for h in range(2):
            rb = RB[h][ci % 2]
            for d in range(2):
                src = bass.AP(rows_t[h][d], n0 * 320, [[0, 64], [1, CH * 320]])
                P.dma(rb[d * 64:(d + 1) * 64, :, :].rearrange("p c r -> p (c r)"), src, [rows[h][d]], [rb], key=f"rb{h}{ci % 2}")
            P.dma(VC[h][ci % 2][:, :], vT[h].t[:, n0:n0 + CH], [vT[h]], [VC[h][ci % 2]], key=f"vc{h}{ci % 2}")
        for i in range(CH):
            for h in range(2):
                rb = RB[h][ci % 2]
                vc = VC[h][ci % 2]
                yc = YC[h][ci % 2]
                s_, j_, a_s = S[h], junk[h], sa[h]
                P.op("dve", lambda e, s_=s_, j_=j_, a_s=a_s, rb=rb, i=i: e.scalar_tensor_tensor(j_[:, :], s_[:, :], 1.0, rb[:, i, 64:128], ALU.mult, ALU.mult, accum_out=a_s[:, 0:1]),
                     [s_, rb], [j_, a_s])
                P.op("dve", lambda e, s_=s_, rb=rb, i=i: e.tensor_tensor(s_[:, :], s_[:, :], rb[:, i, 0:64], ALU.mult), [s_, rb], [s_])
                P.op("dve", lambda e, s_=s_, a_s=a_s, rb=rb, i=i: e.scalar_tensor_tensor(s_[:, :], rb[:, i, 128:192], a_s[:, 0:1], s_[:, :], ALU.mult, ALU.add), [s_, rb, a_s], [s_])
                P.op("dve", lambda e, s_=s_, vc=vc, rb=rb, i=i: e.scalar_tensor_tensor(s_[:, :], rb[:, i, 192:256], vc[:, i:i + 1], s_[:, :], ALU.mult, ALU.add), [s_, rb, vc], [s_])
                P.op("dve", lambda e, s_=s_, j_=j_, yc=yc, rb=rb, i=i: e.scalar_tensor_tensor(j_[:, :], s_[:, :], 1.0, rb[:, i, 256:320], ALU.mult, ALU.mult, accum_out=yc[:, i:i + 1]),
                     [s_, rb], [j_, yc])
        for h in range(2):
            P.dma(yT[h].t[:, n0:n0 + CH], YC[h][ci % 2][:, :], [YC[h][ci % 2]], [yT[h]], key=f"yst{h}{ci % 2}")

    yf = [P.sb([128, 128], F32, name=f"yf{i}") for i in range(2)]
    yb = [P.sb([128, 128], F32, name=f"yb{i}") for i in range(2)]
    yt = [P.sb([128, 128], F32, name=f"yt{i}") for i in range(2)]
    w3 = [P.sb([128, 128], F32, name=f"w3_{i}") for i in range(6)]
    k3 = {"i": 0, "w": 0}

    def n3():
        b = w3[k3["w"] % 6]
        k3["w"] += 1
        return b

    for (s0, sl) in segs:
        if s0 == 0 and not want_ctx_out:
            continue
        nb = sl // 128
        for b in range(nb):
            t0 = s0 + b * 128
            tb = s0 + (nb - 1 - b) * 128
            f_, b_, t_ = yf[k3["i"] % 2], yb[k3["i"] % 2], yt[k3["i"] % 2]
            k3["i"] += 1
            for h in range(2):
                P.dma(f_[h * 64:(h + 1) * 64, :], yT[h].t[0:64, t0:t0 + 128], [yT[h]], [f_], key=f"yf{k3['i'] % 2}")
                P.dma(b_[h * 64:(h + 1) * 64, :], yT[h].t[64:128, tb:tb + 128], [yT[h]], [b_], key=f"yb{k3['i'] % 2}")
            p1 = nps()
            P.op("pe", lambda e, p1=p1, b_=b_: e.matmul(p1[:, :128], b_[:, :], ident, start=True, stop=True), [b_, cm], [p1])
            P.op("act", lambda e, p1=p1, t_=t_: e.copy(t_[:, :], p1[:, :128]), [p1], [t_])
            p2 = nps()
            P.op("pe", lambda e, p2=p2, t_=t_: e.matmul(p2[:, :128], t_[:, :], jm, start=True, stop=True), [t_, cm], [p2])
            y = n3()
            P.op("dve", lambda e, y=y, p2=p2, f_=f_: e.tensor_tensor(y[:, :], p2[:, :128], f_[:, :], ALU.add), [p2, f_], [y])
            pm = nps()
            P.op("pe", lambda e, pm=pm, y=y: e.matmul(pm[:, :128], bones, y[:, :], start=True, stop=True), [cm, y], [pm])
            yc_ = n3()
            P.op("dve", lambda e, yc_=yc_, pm=pm, y=y: e.scalar_tensor_tensor(yc_[:, :], pm[:, :128], -1.0 / 64, y[:, :], ALU.mult, ALU.add), [pm, y], [yc_])
            sq = n3()
            P.op("act", lambda e, sq=sq, yc_=yc_: e.activation(sq[:, :], yc_[:, :], AF.Square), [yc_], [sq])
            pv_ = nps()
            P.op("pe", lambda e, pv_=pv_, sq=sq: e.matmul(pv_[:, :128], bones, sq[:, :], start=True, stop=True), [cm, sq], [pv_])
            P.op("dve", lambda e, pv_=pv_, sq=sq: e.tensor_scalar(sq[:, :], pv_[:, :128], 1.0 / 64, GN_EPS, ALU.mult, ALU.add), [pv_], [sq])
            P.op("act", lambda e, sq=sq: e.activation(sq[:, :], sq[:, :], AF.Sqrt), [sq], [sq])
            P.op("dve", lambda e, sq=sq: e.reciprocal(sq[:, :], sq[:, :]), [sq], [sq])
            P.op("dve", lambda e, yc_=yc_, sq=sq: e.scalar_tensor_tensor(yc_[:, :], yc_[:, :], pv[:, 7:8], sq[:, :], ALU.mult, ALU.mult), [yc_, pv, sq], [yc_])
            P.op("dve", lambda e, yc_=yc_, t0=t0: e.scalar_tensor_tensor(yc_[:, :], yc_[:, :], pv[:, 8:9], bonus[:, t0:t0 + 128], ALU.add, ALU.add), [yc_, pv, bonus], [yc_])
            o_ = n3()
            P.op("dve", lambda e, o_=o_, yc_=yc_, t0=t0: e.tensor_tensor(o_[:, :], yc_[:, :], gout[:, t0:t0 + 128], ALU.mult), [yc_, gout], [o_])
            dst = yr[:, t0 - LC:t0 - LC + 128] if s0 > 0 else yrc[:, t0:t0 + 128]
            P.dma(dst, o_[:, :], [o_], [Buf("yrout", None)], key=f"st{o_.name}")
    return P.build()


def build_M(nc, D, NCOL, NLAY):
    P = Prog(nc)
    DC = D // 128
    cc = P.dram("cc", [128, DC, 2], F32, kind="ExternalInput")
    wm = P.dram("wm", [NLAY, D, NCOL], F32, kind="ExternalInput")
    bm = P.dram("bm", [NLAY, 2, NCOL], F32, kind="ExternalInput")
    mo = nc.dram_tensor("mo", [NLAY, 2, NCOL], F32, kind="ExternalOutput").ap()
    cs = P.sb([128, DC, 2], F32, name="cs")
    P.dma(cs[:], cc.t, [cc], [cs])
    P.op("act", lambda e: e.activation(cs[:], cs[:], AF.Silu), [cs], [cs])
    wt = [P.sb([128, DC, 256], F32, name=f"wt{i}") for i in range(4)]
    ps = [P.ps([128, 512], F32, name=f"ps{i}") for i in range(4)]
    ob = [P.sb([2, 256], F32, name=f"ob{i}") for i in range(3)]
    bb = [P.sb([2, 256], F32, name=f"bb{i}") for i in range(3)]
    i = 0
    for l in range(NLAY):
        for c0 in range(0, NCOL, 256):
            n = min(256, NCOL - c0)
            w = wt[i % 4]; p_ = ps[i % 4]; o = ob[i % 3]; b = bb[i % 3]
            i += 1
            P.dma(w[:, :, :n], wm.t[l, :, c0:c0 + n].rearrange("(c p) n -> p c n", p=128), [wm], [w])
            P.dma(b[:, :n], bm.t[l, :, c0:c0 + n], [bm], [b])
            for k in range(DC):
                P.op("pe", lambda e, w=w, p_=p_, k=k, n=n: e.matmul(p_[:2, :n], cs[:, k, :], w[:, k, :n], start=(k == 0), stop=(k == DC - 1)), [cs, w], [p_])
            P.op("dve", lambda e, o=o, p_=p_, b=b, n=n: e.tensor_tensor(o[:, :n], p_[:2, :n], b[:, :n], ALU.add), [p_, b], [o])
            P.dma(mo[l, :, c0:c0 + n], o[:, :n], [o], [Buf("moout", None)], key=f"st{o.name}")
    return P.build()

from concourse.bass_utils import run_bass_kernel_spmd

NCORES = 8
D_MODEL = 4096
SEQ = 8192
CTX_LEN = 256
D_FF = 5632
DEPTH = 2
HY_W = 1024
DA_W = 2048
RW_W = 1024
O_HY, O_Q, O_K, O_V, O_RW = 0, 3072, 5120, 7168, 9216
RW_STATE = 2304
O_RW_RG = O_RW + RW_STATE
O_GATE = 12704
TLAT = SEQ // NCORES
TCTX = CTX_LEN // NCORES
TT = TLAT + TCTX
DC = D_MODEL // 128


def _launch(build, in_maps):
    nc = bass.Bass("TRN2", target_bir_lowering=False)
    build(nc)
    res = run_bass_kernel_spmd(nc, in_maps, core_ids=list(range(NCORES)))
    return res.results


def _pc(vec):
    return np.asarray(vec, np.float32).reshape(DC, 128).T


def _vecs(rows):
    return np.ascontiguousarray(np.stack([_pc(r) for r in rows], axis=1))


def kernel(x, c, ctx, c_ctx, w_mod, b_mod, norm_gain, w_ff_in, w_ff_out, w_in,
           hy_short, hy_w1, hy_b1, hy_f1, hy_w2, hy_b2, hy_f2, hy_w3, hy_skip,
           da_lambda, da_subln,
           rw_shift, rw_w0, rw_w2, rw_a0, rw_a2, rw_g2, rw_k_k, rw_k_a, rw_r_k, rw_gn_w, rw_gn_b,
           w_branch, w_out, final_gain):
    f32 = np.float32
    A = lambda a: np.asarray(a, f32)
    x, c, ctx, c_ctx = A(x), A(c), A(ctx), A(c_ctx)
    w_in = A(w_in)
    NCOL = 9 * D_MODEL // NCORES
    cc = np.ascontiguousarray(np.stack([c[0], c_ctx], 0).reshape(2, DC, 128).transpose(2, 1, 0))
    w_mod = A(w_mod); b_mod = A(b_mod)
    ims = []
    for i in range(NCORES):
        sl = slice(i * NCOL, (i + 1) * NCOL)
        ims.append({"cc": cc, "wm": np.ascontiguousarray(w_mod[:, :, sl]),
                    "bm": np.ascontiguousarray(np.broadcast_to(b_mod[:, None, sl], (DEPTH, 2, NCOL)))})
    res = _launch(lambda nc: build_M(nc, D_MODEL, NCOL, DEPTH), ims)
    mo = np.concatenate([r["mo"] for r in res], axis=2)
    del ims
    mod = mo[:, 0].reshape(DEPTH, 9, D_MODEL)
    modc = mo[:, 1].reshape(DEPTH, 9, D_MODEL)

    xT = [np.ascontiguousarray(np.concatenate([x[0, i * TLAT:(i + 1) * TLAT], ctx[0, i * TCTX:(i + 1) * TCTX]], 0).T) for i in range(NCORES)]
    cs_tab, rt = rope_tables(SEQ)
    bones = np.zeros((128, 128), f32); bones[:64, :64] = 1; bones[64:, 64:] = 1
    cm = np.stack([np.eye(128, dtype=f32), np.eye(128, dtype=f32)[::-1].copy(), bones])
    LT = CTX_LEN + SEQ

    for l in range(DEPTH):
        last = l == DEPTH - 1
        lam_init = 0.8 - 0.6 * math.exp(-0.3 * l)
        ng = A(norm_gain[l])
        vA = _vecs([ng[0], ng[1], mod[l, 0], mod[l, 1], mod[l, 2], mod[l, 3], mod[l, 4],
                    modc[l, 0], modc[l, 1], modc[l, 2], modc[l, 3], modc[l, 4]])
        wfi, wfo = A(w_ff_in[l, 0]), A(w_ff_out[l, 0])
        win = np.ascontiguousarray(w_in[l][:, :O_GATE])
        ims = [{"xT": xT[i], "vecs": vA, "wfi": wfi, "wfo": wfo, "win": win} for i in range(NCORES)]
        res = _launch(lambda nc: build_A(nc, D_MODEL, D_FF, O_GATE, TT, TLAT), ims)
        del ims, win
        x1T = [r["x1T"] for r in res]
        p_lat = np.concatenate([r["pT"][:, :TLAT].T for r in res], 0)
        p_ctx = np.concatenate([r["pT"][:, TLAT:].T for r in res], 0)
        del res

        hs = A(hy_short[l]); w3 = A(hy_w3[l]).reshape(64, 2, 2, HY_W); sk = A(hy_skip[l])
        v1 = np.ascontiguousarray(np.stack([A(hy_b1[l]), A(hy_f1[l]), A(hy_b2[l]), A(hy_f2[l])], 1))

        def hy_run(pp, L):
            ims = []
            for i in range(NCORES):
                c_lo = i * 128
                cols = np.concatenate([np.arange(c_lo, c_lo + 128) + k * HY_W for k in range(3)])
                hp = np.zeros((L + 2, 384), f32); hp[1:L + 1] = pp[:, O_HY + cols]
                zT, win_c, jm = hyena_consts(L, 128, c_lo)
                swb = np.ascontiguousarray(np.broadcast_to(hs[:, cols].reshape(3, 3, 128).transpose(1, 0, 2)[None], (128, 3, 3, 128)))
                skipb = np.ascontiguousarray(np.broadcast_to(sk[:, c_lo:c_lo + 128][None], (128, 2, 128)))
                ims.append({"hp": hp, "swb": swb, "skipb": skipb, "zT": zT, "w1": A(hy_w1[l]), "v1": v1, "w2": A(hy_w2[l]),
                            "w3c": np.ascontiguousarray(w3[:, :, :, c_lo:c_lo + 128]), "win": win_c, "jm": jm})
            res = _launch(lambda nc: build_hy(nc, L, 128, 32 if L > 1024 else 64), ims)
            return np.concatenate([r["yh"].transpose(2, 0, 1).reshape(L, 128) for r in res], 1)
        y_h = hy_run(p_lat, SEQ)
        yc_h = hy_run(p_ctx, CTX_LEN) if not last else None

        ims = []
        for i in range(NCORES):
            hc = slice(i * 256, (i + 1) * 256)

            def fm(pp, off):
                return np.ascontiguousarray(pp[:, off:off + DA_W][:, hc].reshape(-1, 2, 2, 64).transpose(1, 2, 3, 0))
            vv = np.concatenate([p_lat[:, O_V:O_V + DA_W][:, hc], p_ctx[:, O_V:O_V + DA_W][:, hc]], 0)
            im = {"q": fm(p_lat, O_Q), "k": fm(p_lat, O_K), "kc": fm(p_ctx, O_K),
                  "v": np.ascontiguousarray(vv.reshape(LT, 2, 128).transpose(1, 0, 2)), "cs": cs_tab, "rt": rt,
                  "lamv": np.ascontiguousarray(A(da_lambda[l]).reshape(1, 256)), "subln": np.ascontiguousarray(A(da_subln[l]).reshape(128, 1))}
            if not last:
                im["qc"] = fm(p_ctx, O_Q)
            ims.append(im)
        res = _launch(lambda nc: build_at(nc, 2, SEQ, CTX_LEN, lam_init, not last), ims)
        del ims
        y_a = np.concatenate([r["ya"].transpose(2, 0, 1).reshape(SEQ, 256) for r in res], 1)
        yc_a = np.concatenate([r["yac"].transpose(2, 0, 1).reshape(CTX_LEN, 256) for r in res], 1) if not last else None
        del res

        rs = A(rw_shift[l]); w0 = A(rw_w0[l]); w2 = A(rw_w2[l]); a0 = A(rw_a0[l]); a2 = A(rw_a2[l]); g2 = A(rw_g2[l])

        def padded(rows_c, rows_l):
            n = rows_c.shape[0]
            o = np.zeros((2, n, 128, LT + 4), f32)
            o[0, :, :, 1:1 + CTX_LEN] = rows_c; o[0, :, :, CTX_LEN + 3:CTX_LEN + 3 + SEQ] = rows_l
            o[1, :, :, 1:1 + CTX_LEN] = rows_c[:, :, ::-1]; o[1, :, :, CTX_LEN + 3:CTX_LEN + 3 + SEQ] = rows_l[:, :, ::-1]
            return o

        def grpB(pp):
            r = pp[:, O_RW:O_GATE]
            gd2 = np.zeros((pp.shape[0], 128), f32); gd2[:, :32] = r[:, RW_STATE + RW_W + 128:]
            return np.stack([r[:, 2 * RW_W:2 * RW_W + 128].T, r[:, 2 * RW_W + 128:2 * RW_W + 256].T, r[:, RW_STATE + RW_W:RW_STATE + RW_W + 128].T, gd2.T])
        rinB = padded(grpB(p_ctx), grpB(p_lat))
        ims = []
        for i in range(NCORES):
            cs_ = slice(i * 128, (i + 1) * 128)

            def grpA(pp):
                r = pp[:, O_RW:O_GATE]
                return np.stack([r[:, 0:RW_W][:, cs_].T, r[:, RW_W:2 * RW_W][:, cs_].T, r[:, RW_STATE:RW_STATE + RW_W][:, cs_].T])
            sw = lambda cols: rs[:, cols].T
            swt = np.zeros((128, 7, 3), f32)
            swt[:, 0] = sw(np.arange(0, RW_W)[cs_]); swt[:, 1] = sw(np.arange(RW_W, 2 * RW_W)[cs_])
            swt[:, 2] = sw(np.arange(2 * RW_W, 2 * RW_W + 128)); swt[:, 3] = sw(np.arange(2 * RW_W + 128, 2 * RW_W + 256))
            swt[:, 4] = sw(np.arange(RW_STATE, RW_STATE + RW_W)[cs_]); swt[:, 5] = sw(np.arange(RW_STATE + RW_W, RW_STATE + RW_W + 128))
            swt[:32, 6] = sw(np.arange(RW_STATE + RW_W + 128, RW_STATE + RW_W + 160))
            pvv = np.zeros((128, 12), f32)
            pvv[:, 0] = w0[0, cs_]; pvv[:, 1] = w0[1, cs_]; pvv[:, 2] = a0[0, cs_]; pvv[:, 3] = a0[1, cs_]
            pvv[:, 4] = A(rw_k_k[l])[cs_]; pvv[:, 5] = A(rw_k_a[l])[cs_]; pvv[:, 6] = A(rw_r_k[l]).reshape(-1)[cs_]
            pvv[:, 7] = A(rw_gn_w[l])[cs_]; pvv[:, 8] = A(rw_gn_b[l])[cs_]
            ims.append({"rinA": padded(grpA(p_ctx), grpA(p_lat)), "rinB": rinB, "swt": swt, "pv": pvv,
                        "w2t": np.ascontiguousarray(np.concatenate([w2[0][:, cs_], w2[1][:, cs_]], 0)),
                        "a2t": np.ascontiguousarray(np.concatenate([a2[0][:, cs_], a2[1][:, cs_]], 0)),
                        "g2a": np.ascontiguousarray(g2[:128, cs_]), "g2b": np.ascontiguousarray(g2[128:, cs_]), "cm": cm})
        res = _launch(lambda nc: build_rw(nc, SEQ, CTX_LEN, not last), ims)
        del ims, rinB
        y_r = np.concatenate([r["yr"].T for r in res], 1)
        yc_r = np.concatenate([r["yrc"].T for r in res], 1) if not last else None
        del res, p_lat, p_ctx

        y_lat = np.concatenate([y_h, y_a, y_r], 1)
        y_ctx = np.concatenate([yc_h, yc_a, yc_r], 1) if not last else np.zeros((CTX_LEN, 4096), f32)
        vC = _vecs([ng[1], ng[2], mod[l, 3], mod[l, 4], mod[l, 5], mod[l, 6], mod[l, 7], mod[l, 8],
                    modc[l, 3], modc[l, 4], modc[l, 5], modc[l, 6], modc[l, 7], modc[l, 8], A(final_gain)])
        wg = np.ascontiguousarray(w_in[l][:, O_GATE:])
        ims = []
        for i in range(NCORES):
            yT = np.ascontiguousarray(np.concatenate([y_lat[i * TLAT:(i + 1) * TLAT], y_ctx[i * TCTX:(i + 1) * TCTX]], 0).T)
            ims.append({"x1T": x1T[i], "yT": yT, "vecs": vC, "wg": wg, "wbr": A(w_branch[l]), "wo": A(w_out[l]),
                        "wfi": A(w_ff_in[l, 1]), "wfo": A(w_ff_out[l, 1])})
        res = _launch(lambda nc: build_C(nc, D_MODEL, D_FF, TT, TLAT, HY_W, DA_W, RW_W, last), ims)
        del ims, wg
        xT = [r["x3T"] for r in res]
        del res
    out = np.concatenate([t[:, :TLAT].T for t in xT], 0)[None]
    return np.ascontiguousarray(out.astype(np.float32))
```

```python
import math
import numpy as np

from contextlib import ExitStack
import numpy as np
import concourse.bass as bass
import concourse.mybir as mybir

F32 = mybir.dt.float32
BF16 = mybir.dt.bfloat16
AF = mybir.ActivationFunctionType
ALU = mybir.AluOpType
AX = mybir.AxisListType


class Buf:
    __slots__ = ("name", "t", "lw", "rd")

    def __init__(self, name, t=None):
        self.name = name
        self.t = t
        self.lw = None
        self.rd = []

    def __getitem__(self, k):
        return self.t[k]


class Op:
    __slots__ = ("eng", "fn", "dma", "deps", "sig", "sem", "val", "waits", "idx")

    def __init__(self, eng, fn, dma):
        self.eng, self.fn, self.dma = eng, fn, dma
        self.deps = []
        self.sig = False
        self.sem = None
        self.val = 0
        self.waits = []


class Prog:
    ENGS = ("pe", "dve", "act", "pool", "sp")

    def __init__(self, nc):
        self.nc = nc
        self.es = ExitStack()
        self.ops = []
        self.dma_keys = {}
        self.nbuf = 0

    def sb(self, shape, dt=F32, name=None):
        self.nbuf += 1
        name = name or f"sb{self.nbuf}"
        t = self.es.enter_context(self.nc.sbuf_tensor(name, list(shape), dt))
        return Buf(name, t)

    def ps(self, shape, dt=F32, name=None):
        self.nbuf += 1
        name = name or f"ps{self.nbuf}"
        t = self.es.enter_context(self.nc.psum_tensor(name, list(shape), dt))
        return Buf(name, t)

    def dram(self, name, shape, dt=F32, kind="Internal"):
        t = self.nc.dram_tensor(name, list(shape), dt, kind=kind)
        return Buf(name, t.ap())

    def _rec(self, eng, fn, reads, writes, dma=False, key=None):
        op = Op(eng, fn, dma)
        op.idx = len(self.ops)
        if dma:
            op.sem = key if key is not None else writes[0].name
        for b in reads:
            if b.lw is not None:
                op.deps.append((b.lw, "raw"))
        for b in writes:
            if b.lw is not None:
                op.deps.append((b.lw, "waw"))
            for r in b.rd:
                op.deps.append((r, "war"))
        for b in reads:
            b.rd.append(op)
        for b in writes:
            b.lw = op
            b.rd = []
        self.ops.append(op)
        return op

    def op(self, eng, fn, reads=(), writes=()):
        return self._rec(eng, fn, list(reads), list(writes))

    def dma(self, out_ap, in_ap, reads, writes, eng="sp", key=None, **kw):
        def fn(e):
            return e.dma_start(out=out_ap, in_=in_ap, **kw)
        return self._rec(eng, fn, list(reads), list(writes), dma=True, key=key)

    def build(self):
        nc = self.nc
        need = []
        for x in self.ops:
            for (p, kind) in x.deps:
                if p is x:
                    continue
                if not p.dma and not x.dma and p.eng == x.eng:
                    if p.eng == "pe" or kind != "raw":
                        continue
                need.append((x, p))
                p.sig = True
        cnt = {e: 0 for e in self.ENGS}
        dcnt = {}
        for o in self.ops:
            if o.dma:
                dcnt[o.sem] = dcnt.get(o.sem, 0) + 16
                o.val = dcnt[o.sem]
                o.sig = True
            elif o.sig:
                cnt[o.eng] += 1
                o.val = cnt[o.eng]
                o.sem = "eng_" + o.eng
        semnames = ["eng_" + e for e in self.ENGS if cnt[e] > 0] + list(dcnt.keys())
        sems = {}
        for i, n in enumerate(semnames):
            sems[n] = self.es.enter_context(nc.semaphore(f"s{i}"))
        self.nsems = len(sems)
        waited = {e: {} for e in self.ENGS}
        per_eng = {e: [] for e in self.ENGS}
        needmap = {}
        for (x, p) in need:
            needmap.setdefault(id(x), []).append(p)
        for x in self.ops:
            w = waited[x.eng]
            best = {}
            for p in needmap.get(id(x), []):
                if w.get(p.sem, 0) >= p.val:
                    continue
                if best.get(p.sem, 0) < p.val:
                    best[p.sem] = p.val
            for s, v in best.items():
                w[s] = v
                x.waits.append((s, v))
            per_eng[x.eng].append(x)
        finals = [(s, v) for s, v in dcnt.items()]
        engobj = {"pe": "tensor", "dve": "vector", "act": "scalar", "pool": "gpsimd", "sp": "sync"}
        with nc.Block() as block:
            for e in self.ENGS:
                lst = per_eng[e]
                if not lst and e != "sp":
                    continue

                def body(eng, lst=lst, e=e):
                    for x in lst:
                        for (s, v) in x.waits:
                            eng.wait_ge(sems[s], v)
                        ins = x.fn(eng)
                        if x.sig:
                            ins.then_inc(sems[x.sem], 16 if x.dma else 1)
                    if e == "sp":
                        for (s, v) in finals:
                            eng.wait_ge(sems[s], v)
                getattr(block, engobj[e])(body)
        self.es.close()
        return cnt, dcnt


EPS = 1e-6


def split_tokens(TT):
    n = (TT + 511) // 512
    sz = TT // n
    assert sz * n == TT
    return [(i * sz, sz) for i in range(n)]


class Env:
    pass


def setup_env(P, D, TT, KCmax, nvec):
    E = Env()
    E.D, E.TT = D, TT
    E.DC = D // 128
    E.tts = split_tokens(TT)
    E.ps = [P.ps([128, 512], F32, name=f"psb{i}") for i in range(8)]
    E.psi = 0
    E.wt = [P.sb([128, KCmax, 256], BF16, name=f"wt{i}") for i in range(3)]
    E.wi = 0
    E.actA = P.sb([128, KCmax * TT], BF16, name="actA")
    E.xin = [P.sb([128, TT], F32, name=f"xin{i}") for i in range(2)]
    E.xi = 0
    E.tmp = [P.sb([128, TT], F32, name=f"tmp{i}") for i in range(2)]
    E.ti = 0
    E.ost = [P.sb([128, TT], F32, name=f"ost{i}") for i in range(3)]
    E.oi = 0
    E.acc = P.sb([128, TT], F32, name="acc")
    E.rstd = P.sb([128, TT], F32, name="rstd")
    E.ones = P.sb([128, 128], F32, name="ones")
    P.op("dve", lambda e: e.memset(E.ones[:], 1.0), [], [E.ones])
    E.vec = P.sb([128, nvec, E.DC], F32, name="vec")
    return E


def rot(E, name, idx):
    lst = getattr(E, name)
    i = getattr(E, idx)
    setattr(E, idx, i + 1)
    return lst[i % len(lst)]


def psum_group(E):
    g = []
    for _ in E.tts:
        g.append(E.ps[E.psi % 8])
        E.psi += 1
    return g


def load_w(P, E, Wd, r0, KC, c0, width):
    wt = rot(E, "wt", "wi")
    src = Wd.t[r0:r0 + KC * 128, c0:c0 + width].rearrange("(c p) n -> p c n", p=128)
    P.dma(wt[:, :KC, :width], src, [Wd], [wt], eng="pool")
    return wt


def mm_chunk(P, E, wt, wo, width, KC, Xb, xview, psg, first=True, last=True, kbase=0):
    for k in range(KC):
        for ti, (t0, sz) in enumerate(E.tts):
            ps = psg[ti]
            P.op("pe", lambda e, ps=ps, k=k, t0=t0, sz=sz: e.matmul(
                ps[:width, :sz], wt[:, k, wo:wo + width], xview[:, kbase + k, t0:t0 + sz],
                start=(first and k == 0), stop=(last and k == KC - 1)), [wt, Xb], [ps])


def rms_stats(P, E, acc_ready_buf):
    D = E.D
    psg = psum_group(E)
    for ti, (t0, sz) in enumerate(E.tts):
        ps = psg[ti]
        P.op("pe", lambda e, ps=ps, t0=t0, sz=sz: e.matmul(ps[:, :sz], E.ones[:, :], E.acc[:, t0:t0 + sz], start=True, stop=True),
             [E.ones, E.acc], [ps])
        P.op("dve", lambda e, ps=ps, t0=t0, sz=sz: e.tensor_scalar(E.rstd[:, t0:t0 + sz], ps[:, :sz], 1.0 / D, EPS, ALU.mult, ALU.add),
             [ps], [E.rstd])
    P.op("act", lambda e: e.activation(E.rstd[:, :], E.rstd[:, :], AF.Sqrt), [E.rstd], [E.rstd])
    P.op("dve", lambda e: e.reciprocal(E.rstd[:, :], E.rstd[:, :]), [E.rstd], [E.rstd])


def norm_pass(P, E, src_chunks, NL, gs, sh, gsc, shc):
    TT, DC = E.TT, E.DC
    for c in range(DC):
        xc = rot(E, "xin", "xi")
        P.dma(xc[:, :], src_chunks[c].t, [src_chunks[c]], [xc])
        if c == 0:
            P.op("act", lambda e, xc=xc: e.activation(E.acc[:, :], xc[:, :], AF.Square), [xc], [E.acc])
        else:
            sq = rot(E, "tmp", "ti")
            P.op("act", lambda e, xc=xc, sq=sq: e.activation(sq[:, :], xc[:, :], AF.Square), [xc], [sq])
            P.op("dve", lambda e, sq=sq: e.tensor_tensor(E.acc[:, :], E.acc[:, :], sq[:, :], ALU.add), [sq, E.acc], [E.acc])
    rms_stats(P, E, E.acc)
    xv = E.actA.t[:, :DC * TT].rearrange("p (c t) -> p c t", t=TT)
    for c in range(DC):
        xc = rot(E, "xin", "xi")
        P.dma(xc[:, :], src_chunks[c].t, [src_chunks[c]], [xc])
        tm = rot(E, "tmp", "ti")
        for (a, b, g_, s_) in ((0, NL, gs, sh), (NL, TT, gsc, shc)):
            if b <= a:
                continue
            P.op("dve", lambda e, xc=xc, tm=tm, a=a, b=b, g_=g_, c=c: e.scalar_tensor_tensor(
                tm[:, a:b], xc[:, a:b], E.vec[:, g_, c:c + 1], E.rstd[:, a:b], ALU.mult, ALU.mult), [xc, E.vec, E.rstd], [tm])
            P.op("act", lambda e, tm=tm, a=a, b=b, s_=s_, c=c: e.activation(
                xv[:, c, a:b], tm[:, a:b], AF.Identity, bias=E.vec[:, s_, c:c + 1], scale=1.0), [tm, E.vec], [E.actA])
    return xv


def derive_gs(P, E, dst, g, sc):
    P.op("dve", lambda e: e.scalar_tensor_tensor(E.vec[:, dst, :], E.vec[:, sc, :], 1.0, E.vec[:, g, :], ALU.add, ALU.mult), [E.vec], [E.vec])


def derive_half(P, E, dst, src):
    P.op("dve", lambda e: e.tensor_scalar(E.vec[:, dst, :], E.vec[:, src, :], 0.5, None, ALU.mult), [E.vec], [E.vec])


def ffn_block(P, E, NL, src_ch, dst_ch, wfi, wfo, FF, aT_ch, gs, sh, gsc, shc, hg, hgc):
    DC, TT = E.DC, E.TT
    FC = FF // 128
    hv = norm_pass(P, E, src_ch, NL, gs, sh, gsc, shc)
    for f in range(FC):
        if f % 2 == 0:
            nb = min(2, FC - f) * 128
            wg = load_w(P, E, wfi, 0, DC, f * 128, nb)
            wu = load_w(P, E, wfi, 0, DC, FF + f * 128, nb)
        wo = (f % 2) * 128
        pg = psum_group(E)
        mm_chunk(P, E, wg, wo, 128, DC, E.actA, hv, pg)
        pu = psum_group(E)
        mm_chunk(P, E, wu, wo, 128, DC, E.actA, hv, pu)
        sg = rot(E, "tmp", "ti")
        ao = rot(E, "ost", "oi")
        for ti, (t0, sz) in enumerate(E.tts):
            P.op("act", lambda e, ti=ti, t0=t0, sz=sz, pg=pg, sg=sg: e.activation(sg[:, t0:t0 + sz], pg[ti][:, :sz], AF.Silu), [pg[ti]], [sg])
            P.op("dve", lambda e, ti=ti, t0=t0, sz=sz, pu=pu, sg=sg, ao=ao: e.tensor_tensor(ao[:, t0:t0 + sz], sg[:, t0:t0 + sz], pu[ti][:, :sz], ALU.mult), [pu[ti], sg], [ao])
        P.dma(aT_ch[f].t, ao[:, :], [ao], [aT_ch[f]], eng="pool", key=f"st{ao.name}")
    av = E.actA.t[:, :FC * TT].rearrange("p (c t) -> p c t", t=TT)
    for f in range(FC):
        P.dma(av[:, f, :], aT_ch[f].t, [aT_ch[f]], [E.actA], key="actA_ld")
    for c in range(DC):
        if c % 2 == 0:
            wt = load_w(P, E, wfo, 0, FC, c * 128, 256)
        wo = (c % 2) * 128
        pg = psum_group(E)
        mm_chunk(P, E, wt, wo, 128, FC, E.actA, av, pg)
        xc = rot(E, "xin", "xi")
        P.dma(xc[:, :], src_ch[c].t, [src_ch[c]], [xc])
        xo = rot(E, "ost", "oi")
        for ti, (t0, sz) in enumerate(E.tts):
            for (a, b, hgi) in ((t0, min(t0 + sz, NL), hg), (max(t0, NL), t0 + sz, hgc)):
                if b <= a:
                    continue
                P.op("dve", lambda e, ti=ti, t0=t0, a=a, b=b, hgi=hgi, pg=pg, xc=xc, xo=xo, c=c: e.scalar_tensor_tensor(
                    xo[:, a:b], pg[ti][:, a - t0:b - t0], E.vec[:, hgi, c:c + 1], xc[:, a:b], ALU.mult, ALU.add),
                    [pg[ti], E.vec, xc], [xo])
        P.dma(dst_ch[c].t, xo[:, :], [xo], [dst_ch[c]], key=f"st{xo.name}")


def chunks_of(ap, n, pfx):
    return [Buf(f"{pfx}{c}", ap[c * 128:(c + 1) * 128, :]) for c in range(n)]


def build_A(nc, D, FF, NP, TT, NL):
    P = Prog(nc)
    DC, FC = D // 128, FF // 128
    E = setup_env(P, D, TT, max(DC, FC), 18)
    xT = P.dram("xT", [D, TT], F32, kind="ExternalInput")
    vecs = P.dram("vecs", [128, 12, DC], F32, kind="ExternalInput")
    wfi = P.dram("wfi", [D, 2 * FF], F32, kind="ExternalInput")
    wfo = P.dram("wfo", [FF, D], F32, kind="ExternalInput")
    win = P.dram("win", [D, NP], F32, kind="ExternalInput")
    x1T = nc.dram_tensor("x1T", [D, TT], F32, kind="ExternalOutput").ap()
    pT = nc.dram_tensor("pT", [NP, TT], F32, kind="ExternalOutput").ap()
    aT = nc.dram_tensor("aT", [FF, TT], BF16, kind="Internal").ap()
    xT_ch = chunks_of(xT.t, DC, "xT")
    x1_ch = chunks_of(x1T, DC, "x1T")
    aT_ch = chunks_of(aT, FC, "aT")
    P.dma(E.vec[:, 0:12, :], vecs.t, [vecs], [E.vec])
    derive_gs(P, E, 12, 0, 3); derive_gs(P, E, 13, 0, 8); derive_gs(P, E, 16, 1, 6); derive_gs(P, E, 17, 1, 11)
    derive_half(P, E, 14, 4); derive_half(P, E, 15, 9)
    ffn_block(P, E, NL, xT_ch, x1_ch, wfi, wfo, FF, aT_ch, 12, 2, 13, 7, 14, 15)
    uv = norm_pass(P, E, x1_ch, NL, 16, 5, 17, 10)
    nchunks = (NP + 127) // 128
    for n in range(nchunks):
        if n % 2 == 0:
            nb = min(256, NP - n * 128)
            wt = load_w(P, E, win, 0, DC, n * 128, nb)
        wo = (n % 2) * 128
        width = min(128, NP - n * 128)
        pg = psum_group(E)
        mm_chunk(P, E, wt, wo, width, DC, E.actA, uv, pg)
        po = rot(E, "ost", "oi")
        for ti, (t0, sz) in enumerate(E.tts):
            if (ti + n) % 2 == 0:
                P.op("act", lambda e, ti=ti, t0=t0, sz=sz, pg=pg, po=po, width=width: e.copy(po[:width, t0:t0 + sz], pg[ti][:width, :sz]), [pg[ti]], [po])
            else:
                P.op("dve", lambda e, ti=ti, t0=t0, sz=sz, pg=pg, po=po, width=width: e.tensor_copy(po[:width, t0:t0 + sz], pg[ti][:width, :sz]), [pg[ti]], [po])
        pch = Buf(f"pT{n}", pT[n * 128:n * 128 + width, :])
        P.dma(pch.t, po[:width, :], [po], [pch], key=f"st{po.name}")
    return P.build()


def build_C(nc, D, FF, TT, NL, WH, WA, WR, last):
    P = Prog(nc)
    DC, FC = D // 128, FF // 128
    MIX = WH + WA + WR
    MC = MIX // 128
    E = setup_env(P, D, TT, max(DC, FC, MC), 24)
    x1T = P.dram("x1T", [D, TT], F32, kind="ExternalInput")
    yT = P.dram("yT", [MIX, TT], F32, kind="ExternalInput")
    vecs = P.dram("vecs", [128, 15, DC], F32, kind="ExternalInput")
    wg = P.dram("wg", [D, 3 * D], F32, kind="ExternalInput")
    wbr = P.dram("wbr", [MIX, D], F32, kind="ExternalInput")
    wo_ = P.dram("wo", [D, D], F32, kind="ExternalInput")
    wfi = P.dram("wfi", [D, 2 * FF], F32, kind="ExternalInput")
    wfo = P.dram("wfo", [FF, D], F32, kind="ExternalInput")
    x3T = nc.dram_tensor("x3T", [D, TT], F32, kind="ExternalOutput").ap()
    x2T = nc.dram_tensor("x2T", [D, TT], F32, kind="Internal").ap()
    gT = nc.dram_tensor("gT", [3 * D, TT], BF16, kind="Internal").ap()
    mT = nc.dram_tensor("mT", [D, TT], BF16, kind="Internal").ap()
    aT = nc.dram_tensor("aT", [FF, TT], BF16, kind="Internal").ap()
    x1_ch = chunks_of(x1T.t, DC, "x1T")
    y_ch = chunks_of(yT.t, MC, "yT")
    x2_ch = chunks_of(x2T, DC, "x2T")
    x3_ch = chunks_of(x3T, DC, "x3T")
    g_ch = chunks_of(gT, 3 * DC, "gT")
    m_ch = chunks_of(mT, DC, "mT")
    aT_ch = chunks_of(aT, FC, "aT")
    P.dma(E.vec[:, 0:15, :], vecs.t, [vecs], [E.vec])
    derive_gs(P, E, 16, 0, 3); derive_gs(P, E, 17, 0, 9); derive_gs(P, E, 18, 1, 6); derive_gs(P, E, 19, 1, 12)
    derive_half(P, E, 20, 7); derive_half(P, E, 21, 13)
    uv = norm_pass(P, E, x1_ch, NL, 16, 2, 17, 8)
    for n in range(3 * DC):
        if n % 2 == 0:
            wt = load_w(P, E, wg, 0, DC, n * 128, 256)
        wo = (n % 2) * 128
        pg = psum_group(E)
        mm_chunk(P, E, wt, wo, 128, DC, E.actA, uv, pg)
        go = rot(E, "ost", "oi")
        for ti, (t0, sz) in enumerate(E.tts):
            P.op("act", lambda e, ti=ti, t0=t0, sz=sz, pg=pg, go=go: e.activation(go[:, t0:t0 + sz], pg[ti][:, :sz], AF.Sigmoid), [pg[ti]], [go])
        P.dma(g_ch[n].t, go[:, :], [go], [g_ch[n]], eng="pool", key=f"st{go.name}")
    yv = E.actA.t[:, :MC * TT].rearrange("p (c t) -> p c t", t=TT)
    for c in range(MC):
        P.dma(yv[:, c, :], y_ch[c].t, [y_ch[c]], [E.actA], eng="pool", key="actA_ld")
    kb = [(0, WH // 128), (WH // 128, WA // 128), ((WH + WA) // 128, WR // 128)]
    for c in range(DC):
        mo = rot(E, "ost", "oi")
        for b, (k0, kc) in enumerate(kb):
            wt = load_w(P, E, wbr, k0 * 128, kc, c * 128, 128)
            pg = psum_group(E)
            mm_chunk(P, E, wt, 0, 128, kc, E.actA, yv, pg, kbase=k0)
            gl = rot(E, "tmp", "ti")
            glb = gl.t.bitcast(BF16)
            P.dma(glb[:, :TT], g_ch[b * DC + c].t, [g_ch[b * DC + c]], [gl])
            for ti, (t0, sz) in enumerate(E.tts):
                if b == 0:
                    P.op("dve", lambda e, ti=ti, t0=t0, sz=sz, pg=pg, glb=glb, gl=gl, mo=mo: e.tensor_tensor(mo[:, t0:t0 + sz], pg[ti][:, :sz], glb[:, t0:t0 + sz], ALU.mult), [pg[ti], gl], [mo])
                else:
                    t2 = rot(E, "xin", "xi")
                    P.op("dve", lambda e, ti=ti, t0=t0, sz=sz, pg=pg, glb=glb, gl=gl, t2=t2: e.tensor_tensor(t2[:, t0:t0 + sz], pg[ti][:, :sz], glb[:, t0:t0 + sz], ALU.mult), [pg[ti], gl], [t2])
                    P.op("dve", lambda e, t0=t0, sz=sz, t2=t2, mo=mo: e.tensor_tensor(mo[:, t0:t0 + sz], mo[:, t0:t0 + sz], t2[:, t0:t0 + sz], ALU.add), [t2, mo], [mo])
        P.dma(m_ch[c].t, mo[:, :], [mo], [m_ch[c]], eng="pool", key=f"st{mo.name}")
    mv = E.actA.t[:, :DC * TT].rearrange("p (c t) -> p c t", t=TT)
    for c in range(DC):
        P.dma(mv[:, c, :], m_ch[c].t, [m_ch[c]], [E.actA], key="actA_ld")
    for c in range(DC):
        if c % 2 == 0:
            wt = load_w(P, E, wo_, 0, DC, c * 128, 256)
        wo = (c % 2) * 128
        pg = psum_group(E)
        mm_chunk(P, E, wt, wo, 128, DC, E.actA, mv, pg)
        xc = rot(E, "xin", "xi")
        P.dma(xc[:, :], x1_ch[c].t, [x1_ch[c]], [xc])
        xo = rot(E, "ost", "oi")
        for ti, (t0, sz) in enumerate(E.tts):
            for (a, b, gi) in ((t0, min(t0 + sz, NL), 4), (max(t0, NL), t0 + sz, 10)):
                if b <= a:
                    continue
                P.op("dve", lambda e, ti=ti, t0=t0, a=a, b=b, gi=gi, pg=pg, xc=xc, xo=xo, c=c: e.scalar_tensor_tensor(
                    xo[:, a:b], pg[ti][:, a - t0:b - t0], E.vec[:, gi, c:c + 1], xc[:, a:b], ALU.mult, ALU.add),
                    [pg[ti], E.vec, xc], [xo])
        P.dma(x2_ch[c].t, xo[:, :], [xo], [x2_ch[c]], key=f"st{xo.name}")
    if not last:
        ffn_block(P, E, NL, x2_ch, x3_ch, wfi, wfo, FF, aT_ch, 18, 5, 19, 11, 20, 21)
    else:
        x3i = nc.dram_tensor("x3i", [D, TT], F32, kind="Internal").ap()
        x3i_ch = chunks_of(x3i, DC, "x3i")
        ffn_block(P, E, NL, x2_ch, x3i_ch, wfi, wfo, FF, aT_ch, 18, 5, 19, 11, 20, 21)
        for c in range(DC):
            xc = rot(E, "xin", "xi")
            P.dma(xc[:, :], x3i_ch[c].t, [x3i_ch[c]], [xc])
            if c == 0:
                P.op("act", lambda e, xc=xc: e.activation(E.acc[:, :], xc[:, :], AF.Square), [xc], [E.acc])
            else:
                sq = rot(E, "tmp", "ti")
                P.op("act", lambda e, xc=xc, sq=sq: e.activation(sq[:, :], xc[:, :], AF.Square), [xc], [sq])
                P.op("dve", lambda e, sq=sq: e.tensor_tensor(E.acc[:, :], E.acc[:, :], sq[:, :], ALU.add), [sq, E.acc], [E.acc])
        rms_stats(P, E, E.acc)
        for c in range(DC):
            xc = rot(E, "xin", "xi")
            P.dma(xc[:, :], x3i_ch[c].t, [x3i_ch[c]], [xc])
            xo = rot(E, "ost", "oi")
            P.op("dve", lambda e, xc=xc, xo=xo, c=c: e.scalar_tensor_tensor(
                xo[:, :], xc[:, :], E.vec[:, 14, c:c + 1], E.rstd[:, :], ALU.mult, ALU.mult), [xc, E.vec, E.rstd], [xo])
            P.dma(x3_ch[c].t, xo[:, :], [xo], [x3_ch[c]], key=f"st{xo.name}")
    return P.build()

import numpy as np

EPS = 1e-6


def build_at(nc, H, LQ, LC, lam_init, with_ctx_q):
    P = Prog(nc)
    LK = LQ + LC
    NKC = LK // 128
    q = P.dram("q", [H, 2, 64, LQ], F32, kind="ExternalInput")
    k = P.dram("k", [H, 2, 64, LQ], F32, kind="ExternalInput")
    kc = P.dram("kc", [H, 2, 64, LC], F32, kind="ExternalInput")
    v = P.dram("v", [H, LK, 128], F32, kind="ExternalInput")
    cs = P.dram("cs", [2, 64, LQ], F32, kind="ExternalInput")
    rt = P.dram("rt", [64, 64], F32, kind="ExternalInput")
    lamv = P.dram("lamv", [1, 256], F32, kind="ExternalInput")
    subln = P.dram("subln", [128, 1], F32, kind="ExternalInput")
    ya = nc.dram_tensor("ya", [H, 128, LQ], F32, kind="ExternalOutput").ap()
    if with_ctx_q:
        qc = P.dram("qc", [H, 2, 64, LC], F32, kind="ExternalInput")
        yac = nc.dram_tensor("yac", [H, 128, LC], F32, kind="ExternalOutput").ap()
    Qa = [P.sb([65, LK], BF16, name=f"Qa{m}") for m in range(2)]
    Ka = [P.sb([65, LK], BF16, name=f"Ka{m}") for m in range(2)]
    Vb = P.sb([128, NKC, 128], BF16, name="Vb")
    ps_s = [P.ps([128, 512], F32, name=f"pss{i}") for i in range(3)]
    ps_o = [P.ps([128, 512], F32, name=f"pso{i}") for i in range(2)]
    ps_l = [P.ps([128, 512], F32, name=f"psl{i}") for i in range(2)]
    ps_x = P.ps([128, 512], F32, name="psx")
    xin = [P.sb([64, 512], F32, name=f"xin{i}") for i in range(3)]
    csb = [P.sb([64, 2, 512], F32, name=f"csb{i}") for i in range(2)]
    t1 = [P.sb([64, 512], F32, name=f"t1_{i}") for i in range(2)]
    t2 = [P.sb([64, 512], F32, name=f"t2_{i}") for i in range(2)]
    sq = [P.sb([64, 512], F32, name=f"sq{i}") for i in range(2)]
    pt = [P.sb([128, 512], BF16, name=f"pt{i}") for i in range(3)]
    wk = [P.sb([128, 512], F32, name=f"wk{i}") for i in range(6)]
    yo = [P.sb([128, 512], F32, name=f"yo{i}") for i in range(2)]
    rtb = P.sb([64, 64], F32, name="rtb")
    ones65 = P.sb([64, 65], F32, name="ones65")
    onesf = P.sb([128, 128], F32, name="onesf")
    onesb = P.sb([128, 128], BF16, name="onesb")
    kmax2 = P.sb([65, 1], F32, name="kmax2")
    kmt = P.sb([65, 1], F32, name="kmt")
    lam = P.sb([128, 4], F32, name="lam")
    lrow = P.sb([1, 256], F32, name="lrow")
    lt = P.sb([1, 8], F32, name="lt")
    sub = P.sb([128, 1], F32, name="sub")
    cnt = {"x": 0, "c": 0, "t": 0, "s": 0, "p": 0, "w": 0, "y": 0, "ss": 0}

    def nxt(lst, key):
        b = lst[cnt[key] % len(lst)]
        cnt[key] += 1
        return b

    P.dma(rtb[:, :], rt.t, [rt], [rtb])
    P.dma(lrow[:, :], lamv.t, [lamv], [lrow])
    P.dma(sub[:, :], subln.t, [subln], [sub])
    P.op("dve", lambda e: e.memset(ones65[:, :], 1.0), [], [ones65])
    P.op("dve", lambda e: e.memset(onesf[:, :], 1.0), [], [onesf])
    P.op("dve", lambda e: e.memset(onesb[:, :], 1.0), [], [onesb])
    P.op("dve", lambda e: e.memset(lt[:, :], 0.0), [], [lt])
    lj = P.sb([1, 64], F32, name="lj")
    for i in range(2):
        P.op("dve", lambda e, i=i: e.scalar_tensor_tensor(lj[:, :], lrow[:, 128 * i:128 * i + 64], 1.0, lrow[:, 128 * i + 64:128 * i + 128],
                                                          ALU.mult, ALU.mult, accum_out=lt[:, i:i + 1]), [lrow, lt], [lj, lt])
    P.op("act", lambda e: e.activation(lt[:, 2:4], lt[:, 0:2], AF.Exp), [lt], [lt])
    P.op("dve", lambda e: e.tensor_tensor(lt[:, 4:5], lt[:, 3:4], lt[:, 2:3], ALU.subtract), [lt], [lt])
    P.op("dve", lambda e: e.tensor_scalar(lt[:, 4:5], lt[:, 4:5], -float(lam_init), None, ALU.add), [lt], [lt])
    P.op("pe", lambda e: e.matmul(ps_x[:, 0:1], onesf[0:1, :], lt[0:1, 4:5], start=True, stop=True), [onesf, lt], [ps_x])
    P.op("dve", lambda e: e.tensor_copy(lam[:, 0:1], ps_x[:, 0:1]), [ps_x], [lam])
    P.op("dve", lambda e: e.tensor_scalar(lam[:, 1:2], sub[:, 0:1], 1.0 - float(lam_init), None, ALU.mult), [sub], [lam])

    for h in range(H):
        P.dma(Vb[:, :, :], v.t[h].rearrange("(c p) e -> p c e", p=128), [v], [Vb], eng="pool")
        P.op("dve", lambda e: e.memset(kmax2[:, :], 0.0), [], [kmax2])
        for m in range(2):
            P.op("dve", lambda e, m=m: e.memset(Ka[m][64:65, :], -1.0), [], [Ka[m]])

        def rope_chunk(src_ap, src_buf, dst, c0, n, use_rope, c_tab0, is_q, m):
            x = nxt(xin, "x")
            P.dma(x[:, :n], src_ap, [src_buf], [x])
            s = nxt(sq, "ss")
            P.op("act", lambda e: e.activation(s[:, :n], x[:, :n], AF.Square), [x], [s])
            P.op("pe", lambda e: e.matmul(ps_x[:65, :n], ones65[:, :], s[:, :n], start=True, stop=True), [ones65, s], [ps_x])
            if use_rope:
                ct = nxt(csb, "c")
                P.dma(ct[:, :, :n], cs.t[:, :, c_tab0:c_tab0 + n].rearrange("a d t -> d a t"), [cs], [ct])
                pr = nxt(ps_s, "s")
                P.op("pe", lambda e: e.matmul(pr[:64, :n], rtb[:, :], x[:, :n], start=True, stop=True), [rtb, x], [pr])
                a = nxt(t1, "t")
                b = t2[(cnt["t"] - 1) % 2]
                P.op("dve", lambda e: e.tensor_tensor(a[:, :n], x[:, :n], ct[:, 0, :n], ALU.mult), [x, ct], [a])
                P.op("dve", lambda e: e.tensor_tensor(b[:, :n], pr[:64, :n], ct[:, 1, :n], ALU.mult), [pr, ct], [b])
                P.op("dve", lambda e: e.tensor_tensor(dst[0:64, c0:c0 + n], a[:, :n], b[:, :n], ALU.add), [a, b], [dst])
            else:
                P.op("dve", lambda e: e.tensor_copy(dst[0:64, c0:c0 + n], x[:, :n]), [x], [dst])
            if is_q:
                P.op("act", lambda e: e.activation(dst[64:65, c0:c0 + n], ps_x[64:65, :n], AF.Sqrt, scale=kmax2[64:65, 0:1]), [ps_x, kmax2], [dst])
            else:
                P.op("dve", lambda e: e.reduce_max(kmt[:, :], ps_x[:65, :n], axis=AX.X), [ps_x], [kmt])
                P.op("dve", lambda e: e.tensor_tensor(kmax2[:, :], kmax2[:, :], kmt[:, :], ALU.max), [kmt, kmax2], [kmax2])

        for m in range(2):
            for c0 in range(0, LQ, 512):
                n = min(512, LQ - c0)
                rope_chunk(k.t[h, m, :, c0:c0 + n], k, Ka[m], c0, n, True, c0, False, m)
            for c0 in range(0, LC, 512):
                n = min(512, LC - c0)
                rope_chunk(kc.t[h, m, :, c0:c0 + n], kc, Ka[m], LQ + c0, n, False, 0, False, m)
        for m in range(2):
            for c0 in range(0, LQ, 512):
                n = min(512, LQ - c0)
                rope_chunk(q.t[h, m, :, c0:c0 + n], q, Qa[m], c0, n, True, c0, True, m)
            if with_ctx_q:
                for c0 in range(0, LC, 512):
                    n = min(512, LC - c0)
                    rope_chunk(qc.t[h, m, :, c0:c0 + n], qc, Qa[m], LQ + c0, n, False, 0, True, m)

        def attend(qc0, n, kchunks, out_ap):
            LA = 2
            nk = len(kchunks)
            for m in range(2):
                sl = {}

                def emit_s(j, m=m, sl=sl):
                    ss = nxt(ps_s, "s")
                    kci = kchunks[j]
                    P.op("pe", lambda e, ss=ss, kci=kci, m=m: e.matmul(ss[:, :n], Ka[m][0:65, kci * 128:(kci + 1) * 128], Qa[m][0:65, qc0:qc0 + n],
                                                                   start=True, stop=True), [Ka[m], Qa[m]], [ss])
                    sl[j] = ss
                for j in range(min(LA, nk)):
                    emit_s(j)
                for j, kci in enumerate(kchunks):
                    if j + LA < nk:
                        emit_s(j + LA)
                    ss = sl.pop(j)
                    p_ = nxt(pt, "p")
                    P.op("act", lambda e, ss=ss, p_=p_: e.activation(p_[:, :n], ss[:, :n], AF.Exp, scale=0.125), [ss], [p_])
                    P.op("pe", lambda e, p_=p_, kci=kci, j=j, m=m: e.matmul(ps_o[m][:, :n], Vb[:, kci, :], p_[:, :n], start=(j == 0), stop=(j == nk - 1)),
                         [Vb, p_], [ps_o[m]])
                    P.op("pe", lambda e, p_=p_, j=j, m=m: e.matmul(ps_l[m][:, :n], onesb[:, :], p_[:, :n], start=(j == 0), stop=(j == nk - 1)),
                         [onesb, p_], [ps_l[m]])
            rl = [nxt(wk, "w") for _ in range(2)]
            o = [nxt(wk, "w") for _ in range(2)]
            for m in range(2):
                P.op("dve", lambda e, m=m: e.reciprocal(rl[m][:, :n], ps_l[m][:, :n]), [ps_l[m]], [rl[m]])
                P.op("dve", lambda e, m=m: e.tensor_tensor(o[m][:, :n], ps_o[m][:, :n], rl[m][:, :n], ALU.mult), [ps_o[m], rl[m]], [o[m]])
            d = nxt(wk, "w")
            P.op("dve", lambda e: e.scalar_tensor_tensor(d[:, :n], o[1][:, :n], lam[:, 0:1], o[0][:, :n], ALU.mult, ALU.add), [o[0], o[1], lam], [d])
            s2 = nxt(wk, "w")
            P.op("act", lambda e: e.activation(s2[:, :n], d[:, :n], AF.Square), [d], [s2])
            P.op("pe", lambda e: e.matmul(ps_x[:, :n], onesf[:, :], s2[:, :n], start=True, stop=True), [onesf, s2], [ps_x])
            r = rl[0]
            P.op("dve", lambda e: e.tensor_scalar(r[:, :n], ps_x[:, :n], 1.0 / 128, EPS, ALU.mult, ALU.add), [ps_x], [r])
            P.op("act", lambda e: e.activation(r[:, :n], r[:, :n], AF.Sqrt), [r], [r])
            P.op("dve", lambda e: e.reciprocal(r[:, :n], r[:, :n]), [r], [r])
            y = nxt(yo, "y")
            P.op("dve", lambda e: e.scalar_tensor_tensor(y[:, :n], d[:, :n], lam[:, 1:2], r[:, :n], ALU.mult, ALU.mult), [d, lam, r], [y])
            ob = Buf("yout", out_ap)
            P.dma(out_ap, y[:, :n], [y], [ob], key=f"st{y.name}")

        allk = list(range(NKC))
        for c0 in range(0, LQ, 512):
            n = min(512, LQ - c0)
            attend(c0, n, allk, ya[h, :, c0:c0 + n])
        if with_ctx_q:
            ck = list(range(LQ // 128, NKC))
            for c0 in range(0, LC, 512):
                n = min(512, LC - c0)
                attend(LQ + c0, n, ck, yac[h, :, c0:c0 + n])
    return P.build()


def rope_tables(LQ, grid_w=64, theta=10000.0):
    t = np.arange(LQ)
    row = (t // grid_w).astype(np.float32)
    col = (t % grid_w).astype(np.float32)
    inv = (theta ** (-np.arange(16, dtype=np.float32) / 16)).astype(np.float32)
    ang = np.zeros((64, LQ), np.float32)
    for d in range(64):
        pos = row if d < 32 else col
        ang[d] = pos * inv[d % 16]
    cs = np.stack([np.cos(ang), np.sin(ang)]).astype(np.float32)
    rt = np.zeros((64, 64), np.float32)
    for d in range(64):
        if d % 32 < 16:
            rt[d + 16, d] = -1.0
        else:
            rt[d - 16, d] = 1.0
    return cs, rt

import numpy as np
import math

MAGIC = 12582912.0
TWO_PI = 2.0 * math.pi


def build_hy(nc, L, C=128, CHP=64):
    P = Prog(nc)
    NB = L // 128
    NI = 2 * L - 1
    NE = 2 * NB - 1
    TW = NE * 128
    hp = P.dram("hp", [L + 2, 3 * C], F32, kind="ExternalInput")
    swb_d = P.dram("swb", [128, 3, 3, C], F32, kind="ExternalInput")
    skipb_d = P.dram("skipb", [128, 2, C], F32, kind="ExternalInput")
    zT = P.dram("zT", [33, NI], F32, kind="ExternalInput")
    w1_d = P.dram("w1", [33, 64], F32, kind="ExternalInput")
    v1_d = P.dram("v1", [64, 4], F32, kind="ExternalInput")
    w2_d = P.dram("w2", [64, 64], F32, kind="ExternalInput")
    w3_d = P.dram("w3c", [64, 2, 2, C], F32, kind="ExternalInput")
    win_d = P.dram("win", [C, NI], F32, kind="ExternalInput")
    jm_d = P.dram("jm", [128, 128], F32, kind="ExternalInput")
    yh = nc.dram_tensor("yh", [128, C, NB], F32, kind="ExternalOutput").ap()
    Hd_t = [nc.dram_tensor(f"Hd{o}", [C, 2 * L], BF16, kind="Internal") for o in range(2)]
    Hd = [Buf(f"Hd{o}", Hd_t[o].ap()) for o in range(2)]

    ps = [P.ps([128, 512], F32, name=f"psb{i}") for i in range(8)]
    pc = {"i": 0}

    def nps():
        b = ps[pc["i"] % 8]
        pc["i"] += 1
        return b

    swb = P.sb([128, 3, 3, C], F32, name="swb_s")
    skipb = P.sb([128, 2, C], F32, name="skipb_s")
    w1 = P.sb([33, 64], F32, name="w1_s")
    v1 = P.sb([64, 8], F32, name="v1_s")
    w2 = P.sb([64, 64], F32, name="w2_s")
    w3 = P.sb([64, 2, 2, C], F32, name="w3_s")
    jf = P.sb([128, 128], F32, name="jf")
    jb = P.sb([128, 128], BF16, name="jb")
    P.dma(swb[:], swb_d.t, [swb_d], [swb])
    P.dma(skipb[:], skipb_d.t, [skipb_d], [skipb])
    P.dma(w1[:], w1_d.t, [w1_d], [w1])
    P.dma(v1[:, 0:4], v1_d.t, [v1_d], [v1])
    P.dma(w2[:], w2_d.t, [w2_d], [w2])
    P.dma(w3[:], w3_d.t, [w3_d], [w3])
    P.dma(jf[:], jm_d.t, [jm_d], [jf])
    P.op("dve", lambda e: e.tensor_copy(jb[:], jf[:]), [jf], [jb])

    zin = [P.sb([33, 512], F32, name=f"zin{i}") for i in range(2)]
    fa = [P.sb([64, 512], F32, name=f"fa{i}") for i in range(2)]
    fq = [P.sb([64, 512], F32, name=f"fq{i}") for i in range(2)]
    fh = [P.sb([64, 512], F32, name=f"fh{i}") for i in range(4)]
    wn = [P.sb([C, 512], F32, name=f"wn{i}") for i in range(2)]
    hb = [P.sb([C, 512], BF16, name=f"hb{i}") for i in range(4)]
    fc = {"a": 0, "h": 0, "b": 0}

    def sin_layer(psrc, n, bcol, fcol):
        a = fa[fc["a"] % 2]
        q_ = fq[fc["a"] % 2]
        fc["a"] += 1
        h = fh[fc["h"] % 4]
        fc["h"] += 1
        P.op("dve", lambda e: e.tensor_scalar(a[:, :n], psrc[:64, :n], v1[:, bcol:bcol + 1], v1[:, fcol:fcol + 1], ALU.add, ALU.mult), [psrc, v1], [a])
        P.op("dve", lambda e: e.tensor_scalar(q_[:, :n], a[:, :n], 1.0 / TWO_PI, MAGIC, ALU.mult, ALU.add), [a], [q_])
        P.op("dve", lambda e: e.tensor_scalar(q_[:, :n], q_[:, :n], MAGIC, -TWO_PI, ALU.subtract, ALU.mult), [q_], [q_])
        P.op("dve", lambda e: e.tensor_tensor(a[:, :n], a[:, :n], q_[:, :n], ALU.add), [a, q_], [a])
        P.op("dve", lambda e: e.tensor_scalar(a[:, :n], a[:, :n], 3.1415925, -3.1415925, ALU.min, ALU.max), [a], [a])
        P.op("act", lambda e: e.activation(h[:, :n], a[:, :n], AF.Sin), [a], [h])
        return h

    for ci, i0 in enumerate(range(0, NI, 512)):
        n = min(512, NI - i0)
        z = zin[ci % 2]
        P.dma(z[:, :n], zT.t[:, i0:i0 + n], [zT], [z])
        w_ = wn[ci % 2]
        P.dma(w_[:, :n], win_d.t[:, i0:i0 + n], [win_d], [w_])
        p1 = nps()
        P.op("pe", lambda e, p1=p1, z=z, n=n: e.matmul(p1[:64, :n], w1[:, :], z[:, :n], start=True, stop=True), [w1, z], [p1])
        h1 = sin_layer(p1, n, 0, 1)
        p2 = nps()
        P.op("pe", lambda e, p2=p2, h1=h1, n=n: e.matmul(p2[:64, :n], w2[:, :], h1[:, :n], start=True, stop=True), [w2, h1], [p2])
        h2 = sin_layer(p2, n, 2, 3)
        nbw = max(0, min(n, (L - 1) - i0))
        for o in range(2):
            p3 = nps()
            if nbw > 0:
                P.op("pe", lambda e, p3=p3, h2=h2, o=o, nbw=nbw: e.matmul(p3[:C, :nbw], w3[:, o, 1, :], h2[:, :nbw], start=True, stop=True), [w3, h2], [p3])
            if nbw < n:
                P.op("pe", lambda e, p3=p3, h2=h2, o=o, nbw=nbw, n=n: e.matmul(p3[:C, nbw:n], w3[:, o, 0, :], h2[:, nbw:n], start=True, stop=True), [w3, h2], [p3])
            hh = hb[fc["b"] % 4]
            fc["b"] += 1
            P.op("dve", lambda e, p3=p3, hh=hh, w_=w_, n=n: e.tensor_tensor(hh[:, :n], p3[:C, :n], w_[:, :n], ALU.mult), [p3, w_], [hh])
            P.dma(Hd[o].t[:, i0:i0 + n], hh[:, :n], [hh], [Hd[o]], key=f"st{hh.name}")

    FW = NB * CHP
    sh = [P.sb([128, NB, CHP], F32, name=f"sh{i}") for i in range(3)]
    vs = P.sb([128, NB, CHP], BF16, name="vs")
    vr = P.sb([128, NB, CHP], BF16, name="vr")
    zz = P.sb([128, NB, CHP], BF16, name="zz")
    xs = P.sb([128, NB, CHP], F32, name="xs")
    tt = [P.sb([128, TW], BF16, name=f"tt{i}") for i in range(3)]
    ep = [P.sb([128, 8, NB], F32, name=f"ep{i}") for i in range(3)]
    tc_ = {"i": 0, "e": 0}

    def short_conv(xi, c0, dst, dst_buf):
        for s in range(3):
            src = hp.t[s:s + L, xi * C + c0: xi * C + c0 + CHP].rearrange("(b j) c -> j b c", j=128)
            P.dma(sh[s][:, :, :], src, [hp], [sh[s]])
            wb = swb[:, xi, s, c0:c0 + CHP].unsqueeze(1).broadcast_to([128, NB, CHP])
            P.op("dve", lambda e, s=s, wb=wb: e.tensor_tensor(sh[s][:, :, :], sh[s][:, :, :], wb, ALU.mult), [sh[s], swb], [sh[s]])
        P.op("dve", lambda e: e.tensor_tensor(sh[0][:, :, :], sh[0][:, :, :], sh[1][:, :, :], ALU.add), [sh[0], sh[1]], [sh[0]])
        P.op("dve", lambda e: e.tensor_tensor(dst, sh[0][:, :, :], sh[2][:, :, :], ALU.add), [sh[0], sh[2]], [dst_buf])

    def reverse_j(src, dst):
        sv = src.t.rearrange("p b c -> p (b c)")
        dv = dst.t.rearrange("p b c -> p (b c)")
        for k, f0 in enumerate(range(0, FW, 512)):
            n = min(512, FW - f0)
            pr = nps()
            P.op("pe", lambda e, pr=pr, f0=f0, n=n: e.matmul(pr[:, :n], jb[:, :], sv[:, f0:f0 + n], start=True, stop=True), [jb, src], [pr])
            if k % 2 == 0:
                P.op("act", lambda e, pr=pr, f0=f0, n=n: e.copy(dv[:, f0:f0 + n], pr[:, :n]), [pr], [dst])
            else:
                P.op("dve", lambda e, pr=pr, f0=f0, n=n: e.tensor_copy(dv[:, f0:f0 + n], pr[:, :n]), [pr], [dst])

    def long_conv(o, c0, urev, epilogue):
        for g in range(CHP // 8):
            pg = nps()
            for c8 in range(8):
                c = g * 8 + c8
                T = tt[tc_["i"] % 3]
                tc_["i"] += 1
                src = bass.AP(Hd_t[o], (c0 + c) * 2 * L, [[1, 128], [1, TW]])
                P.dma(T[:, :], src, [Hd[o]], [T])
                order = [NB - 1] + [e_ for e_ in range(NE) if e_ != NB - 1]
                for k, e_ in enumerate(order):
                    d = e_ - (NB - 1)
                    bt0 = max(0, d)
                    n = NB - abs(d)
                    bs0 = bt0 - d
                    P.op("pe", lambda e, T=T, e_=e_, pg=pg, c8=c8, bt0=bt0, n=n, bs0=bs0, c=c, k=k: e.matmul(
                        pg[:, c8 * NB + bt0: c8 * NB + bt0 + n], T[:, e_ * 128:(e_ + 1) * 128], urev[:, bs0:bs0 + n, c],
                        start=(k == 0), stop=(k == NE - 1), skip_group_check=True), [T, urev], [pg])
            epilogue(g, pg)

    for c0 in range(0, C, CHP):
        short_conv(0, c0, vs[:, :, :], vs)
        short_conv(1, c0, xs[:, :, :], xs)
        reverse_j(vs, vr)

        def epi1(g, pg, c0=c0):
            t = ep[tc_["e"] % 3]
            tc_["e"] += 1
            vv = vs[:, :, g * 8:(g + 1) * 8].rearrange("p b c -> p c b")
            xv = xs[:, :, g * 8:(g + 1) * 8].rearrange("p b c -> p c b")
            zv = zz[:, :, g * 8:(g + 1) * 8].rearrange("p b c -> p c b")
            sk = skipb[:, 0, c0 + g * 8:c0 + (g + 1) * 8].unsqueeze(2).broadcast_to([128, 8, NB])
            pv = pg[:, :8 * NB].rearrange("p (c b) -> p c b", b=NB)
            P.op("dve", lambda e: e.tensor_tensor(t[:, :, :], vv, sk, ALU.mult), [vs, skipb], [t])
            P.op("dve", lambda e: e.tensor_tensor(t[:, :, :], t[:, :, :], pv, ALU.add), [t, pg], [t])
            P.op("dve", lambda e: e.tensor_tensor(zv, t[:, :, :], xv, ALU.mult), [t, xs], [zz])
        long_conv(0, c0, vr, epi1)
        short_conv(2, c0, xs[:, :, :], xs)
        reverse_j(zz, vr)

        def epi2(g, pg, c0=c0):
            t = ep[tc_["e"] % 3]
            tc_["e"] += 1
            zv = zz[:, :, g * 8:(g + 1) * 8].rearrange("p b c -> p c b")
            xv = xs[:, :, g * 8:(g + 1) * 8].rearrange("p b c -> p c b")
            sk = skipb[:, 1, c0 + g * 8:c0 + (g + 1) * 8].unsqueeze(2).broadcast_to([128, 8, NB])
            pv = pg[:, :8 * NB].rearrange("p (c b) -> p c b", b=NB)
            P.op("dve", lambda e: e.tensor_tensor(t[:, :, :], zv, sk, ALU.mult), [zz, skipb], [t])
            P.op("dve", lambda e: e.tensor_tensor(t[:, :, :], t[:, :, :], pv, ALU.add), [t, pg], [t])
            P.op("dve", lambda e: e.tensor_tensor(t[:, :, :], t[:, :, :], xv, ALU.mult), [t, xs], [t])
            ob = Buf("yhout", None)
            P.dma(yh[:, c0 + g * 8:c0 + (g + 1) * 8, :], t[:, :, :], [t], [ob], key=f"st{t.name}")
        long_conv(1, c0, vr, epi2)
    return P.build()


def hyena_consts(L, C, c_lo, hy_width=1024, bands_n=16):
    idx = np.arange(2 * L - 1)
    t = np.abs(idx - (L - 1)).astype(np.float32)
    t_norm = t / np.float32(L)
    bands = np.linspace(1e-4, bands_n - 1, bands_n, dtype=np.float32)
    ang = (np.float32(2.0 * math.pi / L) * t[:, None]) * bands[None, :]
    z = np.concatenate([t_norm[:, None], np.cos(ang), -np.sin(ang)], axis=-1).astype(np.float32)
    deltas = np.abs(np.linspace(math.log(1e-2) / 1.5, math.log(1e-2) / 0.3, hy_width, dtype=np.float32))
    win = np.exp(-t_norm[None, :] * deltas[c_lo:c_lo + C, None]).astype(np.float32)
    jm = np.eye(128, dtype=np.float32)[::-1].copy()
    return np.ascontiguousarray(z.T), win, jm

import numpy as np
import math

GN_EPS = 64e-5
NG = 7


def build_rw(nc, LQ, LC, want_ctx_out, CH=16):
    P = Prog(nc)
    LT = LC + LQ
    segs = [(0, LC), (LC, LQ)]
    rinA = P.dram("rinA", [2, 3, 128, LT + 4], F32, kind="ExternalInput")
    rinB = P.dram("rinB", [2, 4, 128, LT + 4], F32, kind="ExternalInput")
    GSRC = {0: (rinA, 0), 1: (rinA, 1), 4: (rinA, 2), 2: (rinB, 0), 3: (rinB, 1), 5: (rinB, 2), 6: (rinB, 3)}
    swt_d = P.dram("swt", [128, NG, 3], F32, kind="ExternalInput")
    pv_d = P.dram("pv", [128, 12], F32, kind="ExternalInput")
    w2_d = P.dram("w2t", [128, 128], F32, kind="ExternalInput")
    a2_d = P.dram("a2t", [128, 128], F32, kind="ExternalInput")
    g2a_d = P.dram("g2a", [128, 128], F32, kind="ExternalInput")
    g2b_d = P.dram("g2b", [32, 128], F32, kind="ExternalInput")
    cm_d = P.dram("cm", [3, 128, 128], F32, kind="ExternalInput")
    yr = nc.dram_tensor("yr", [128, LQ], F32, kind="ExternalOutput").ap()
    if want_ctx_out:
        yrc = nc.dram_tensor("yrc", [128, LC], F32, kind="ExternalOutput").ap()
    cols = [Buf(f"cols{h}", nc.dram_tensor(f"cols{h}", [4, 128, LT], F32, kind="Internal").ap()) for h in range(2)]
    kv_t = [nc.dram_tensor(f"kv{h}", [2, 2, LT, 64], F32, kind="Internal") for h in range(2)]
    kv = [Buf(f"kv{h}", kv_t[h].ap()) for h in range(2)]
    yT = [Buf(f"yT{h}", nc.dram_tensor(f"yT{h}", [128, LT], F32, kind="Internal").ap()) for h in range(2)]

    ps = [P.ps([128, 512], F32, name=f"psb{i}") for i in range(8)]
    pc = {"i": 0}

    def nps():
        b = ps[pc["i"] % 8]
        pc["i"] += 1
        return b

    swt = P.sb([128, NG, 3], F32, name="swt_s")
    pv = P.sb([128, 16], F32, name="pv_s")
    w2 = P.sb([128, 128], F32, name="w2_s")
    a2 = P.sb([128, 128], F32, name="a2_s")
    g2a = P.sb([128, 128], F32, name="g2a_s")
    g2b = P.sb([32, 128], F32, name="g2b_s")
    cm = P.sb([128, 3, 128], F32, name="cm_s")
    P.dma(swt[:], swt_d.t, [swt_d], [swt])
    P.dma(pv[:, 0:12], pv_d.t, [pv_d], [pv])
    P.dma(w2[:], w2_d.t, [w2_d], [w2])
    P.dma(a2[:], a2_d.t, [a2_d], [a2])
    P.dma(g2a[:], g2a_d.t, [g2a_d], [g2a])
    P.dma(g2b[:], g2b_d.t, [g2b_d], [g2b])
    P.dma(cm[:], cm_d.t.rearrange("a p n -> p a n"), [cm_d], [cm])
    ident, jm, bones = cm[:, 0, :], cm[:, 1, :], cm[:, 2, :]
    bonus = P.sb([128, LT], F32, name="bonus")
    gout = P.sb([128, LT], F32, name="gout")

    NW = 26
    xin = [P.sb([128, 516], F32, name=f"xin{i}") for i in range(NG + 2)]
    wkb = [P.sb([128, 512], F32, name=f"wk{i}") for i in range(NW)]
    stg = [P.sb([128, 2, 128], F32, name=f"stg{i}") for i in range(2)]
    cnt = {"x": 0, "w": 0, "s": 0}

    def nw():
        b = wkb[cnt["w"] % NW]
        cnt["w"] += 1
        return b

    def conv(d, g, col0, n):
        x = xin[cnt["x"] % len(xin)]
        cnt["x"] += 1
        rsrc, gi = GSRC[g]
        P.dma(x[:, :n + 2], rsrc.t[d, gi, :, col0:col0 + n + 2], [rsrc], [x])
        u = nw()
        taps = (0, 1, 2) if d == 0 else (2, 1, 0)
        P.op("act", lambda e: e.activation(u[:, :n], x[:, 0:n], AF.Identity, scale=swt[:, g, taps[0]:taps[0] + 1]), [x, swt], [u])
        P.op("dve", lambda e: e.scalar_tensor_tensor(u[:, :n], x[:, 1:n + 1], swt[:, g, taps[1]:taps[1] + 1], u[:, :n], ALU.mult, ALU.add), [x, swt, u], [u])
        P.op("dve", lambda e: e.scalar_tensor_tensor(u[:, :n], x[:, 2:n + 2], swt[:, g, taps[2]:taps[2] + 1], u[:, :n], ALU.mult, ALU.add), [x, swt, u], [u])
        return u

    def lowrank(u, wt, d, n, bias_col):
        p_ = nps()
        lo = d * 64
        P.op("pe", lambda e: e.matmul(p_[:, :n], wt[lo:lo + 64, :], u[lo:lo + 64, :n], start=True, stop=True), [wt, u], [p_])
        o = nw()
        P.op("act", lambda e: e.activation(o[:, :n], p_[:, :n], AF.Sigmoid, bias=pv[:, bias_col:bias_col + 1], scale=1.0), [p_, pv], [o])
        return o

    for d in range(2):
        for (s0, sl) in segs:
            for c0 in range(0, sl, 512):
                n = min(512, sl - c0)
                n0 = s0 + c0
                seg_i = 0 if s0 == 0 else 1
                pad0 = s0 + 2 * seg_i + c0
                uk = conv(d, 0, pad0, n)
                uv = conv(d, 1, pad0, n)
                uwd = conv(d, 2, pad0, n)
                uad = conv(d, 3, pad0, n)
                ur = conv(d, 4, pad0, n)
                P.op("act", lambda e, uwd=uwd, n=n: e.activation(uwd[:, :n], uwd[:, :n], AF.Tanh), [uwd], [uwd])
                dirs = (0, 1) if d == 0 else (1,)
                a_ = {}
                dec = None
                for dd in dirs:
                    a_[dd] = lowrank(uad, a2, dd, n, 2 + dd)
                sg = lowrank(uwd, w2, d, n, d)
                dec = nw()
                P.op("act", lambda e, dec=dec, sg=sg, n=n: e.activation(dec[:, :n], sg[:, :n], AF.Exp, scale=-math.exp(-0.5)), [sg], [dec])
                kkr = nw()
                P.op("dve", lambda e, kkr=kkr, uk=uk, n=n: e.tensor_scalar(kkr[:, :n], uk[:, :n], pv[:, 4:5], None, ALU.mult), [uk, pv], [kkr])
                sq = nw()
                P.op("act", lambda e, sq=sq, kkr=kkr, n=n: e.activation(sq[:, :n], kkr[:, :n], AF.Square), [kkr], [sq])
                pn = nps()
                P.op("pe", lambda e, pn=pn, sq=sq, n=n: e.matmul(pn[:, :n], bones, sq[:, :n], start=True, stop=True), [cm, sq], [pn])
                P.op("act", lambda e, pn=pn, sq=sq, n=n: e.activation(sq[:, :n], pn[:, :n], AF.Sqrt), [pn], [sq])
                P.op("dve", lambda e, sq=sq, n=n: e.tensor_scalar(sq[:, :n], sq[:, :n], 1e-12, None, ALU.max), [sq], [sq])
                P.op("dve", lambda e, sq=sq, n=n: e.reciprocal(sq[:, :n], sq[:, :n]), [sq], [sq])
                nkk = nw()
                P.op("dve", lambda e, nkk=nkk, kkr=kkr, sq=sq, n=n: e.scalar_tensor_tensor(nkk[:, :n], kkr[:, :n], -1.0, sq[:, :n], ALU.mult, ALU.mult), [kkr, sq], [nkk])
                bb = nw()
                P.op("dve", lambda e, bb=bb, nkk=nkk, ad=a_[d], n=n: e.scalar_tensor_tensor(bb[:, :n], nkk[:, :n], -1.0, ad[:, :n], ALU.mult, ALU.mult), [nkk, a_[d]], [bb])
                kd = {}
                for dd in dirs:
                    t_ = nw()
                    P.op("dve", lambda e, t_=t_, ad=a_[dd], n=n: e.tensor_scalar(t_[:, :n], ad[:, :n], -1.0, pv[:, 5:6], ALU.add, ALU.mult), [a_[dd], pv], [t_])
                    P.op("dve", lambda e, t_=t_, uk=uk, n=n: e.scalar_tensor_tensor(t_[:, :n], t_[:, :n], 1.0, uk[:, :n], ALU.add, ALU.mult), [t_, uk], [t_])
                    kd[dd] = t_
                if d == 0:
                    ks = nw()
                    P.op("dve", lambda e, ks=ks, kd=kd, n=n: e.tensor_tensor(ks[:, :n], kd[0][:, :n], kd[1][:, :n], ALU.add), [kd[0], kd[1]], [ks])
                    P.op("dve", lambda e, ks=ks, ur=ur, n=n: e.scalar_tensor_tensor(ks[:, :n], ks[:, :n], pv[:, 6:7], ur[:, :n], ALU.mult, ALU.mult), [ks, pv, ur], [ks])
                    pb = nps()
                    P.op("pe", lambda e, pb=pb, ks=ks, n=n: e.matmul(pb[:, :n], bones, ks[:, :n], start=True, stop=True), [cm, ks], [pb])
                    P.op("dve", lambda e, pb=pb, uv=uv, n=n, n0=n0: e.tensor_tensor(bonus[:, n0:n0 + n], pb[:, :n], uv[:, :n], ALU.mult), [pb, uv], [bonus])
                    ug = conv(d, 5, pad0, n)
                    ug2 = conv(d, 6, pad0, n)
                    P.op("act", lambda e, ug=ug, n=n: e.activation(ug[:, :n], ug[:, :n], AF.Sigmoid), [ug], [ug])
                    P.op("act", lambda e, ug2=ug2, n=n: e.activation(ug2[:32, :n], ug2[:32, :n], AF.Sigmoid), [ug2], [ug2])
                    pg_ = nps()
                    P.op("pe", lambda e, pg_=pg_, ug=ug, n=n: e.matmul(pg_[:, :n], g2a[:, :], ug[:, :n], start=True, stop=False), [g2a, ug], [pg_])
                    P.op("pe", lambda e, pg_=pg_, ug2=ug2, n=n: e.matmul(pg_[:, :n], g2b[:, :], ug2[:32, :n], start=False, stop=True), [g2b, ug2], [pg_])
                    P.op("act", lambda e, pg_=pg_, n=n, n0=n0: e.copy(gout[:, n0:n0 + n], pg_[:, :n]), [pg_], [gout])
                for h in range(2):
                    for qi, sb_ in enumerate([dec, nkk, bb, ur]):
                        P.dma(cols[h].t[qi, d * 64:(d + 1) * 64, n0:n0 + n], sb_[h * 64:(h + 1) * 64, :n], [sb_], [cols[h]], key=f"cst{h}")
                for b0 in range(0, n, 128):
                    st = stg[cnt["s"] % 2]
                    cnt["s"] += 1
                    pt_ = nps()
                    for qi, sb_ in enumerate([kd[d], uv]):
                        P.op("pe", lambda e, pt_=pt_, sb_=sb_, b0=b0, qi=qi: e.matmul(pt_[:, qi * 128:(qi + 1) * 128], sb_[:, b0:b0 + 128], ident, start=True, stop=True),
                             [sb_, cm], [pt_])
                    P.op("act", lambda e, pt_=pt_, st=st: e.copy(st[:, 0:2, :], pt_[:, 0:256].rearrange("p (q c) -> p q c", c=128)), [pt_], [st])
                    for h in range(2):
                        dst = kv_t[h].ap()[:, d, n0 + b0:n0 + b0 + 128, :].rearrange("q t k -> t q k")
                        P.dma(dst, st[:, 0:2, h * 64:(h + 1) * 64], [st], [kv[h]], key=f"kst{h}")

    ST = [[P.sb([128, 64], F32, name=f"ST{h}{i}") for i in range(2)] for h in range(2)]
    Tm = [P.sb([128, 64], F32, name=f"Tm{h}") for h in range(2)]
    CB = [[P.sb([128, 4, CH], F32, name=f"CB{h}{i}") for i in range(2)] for h in range(2)]
    AR = [[P.sb([128, CH, 64], F32, name=f"AR{h}{i}") for i in range(2)] for h in range(2)]
    KV = [[P.sb([128, 2, CH, 64], F32, name=f"KV{h}{i}") for i in range(2)] for h in range(2)]
    YC = [[P.sb([64, 2, CH], F32, name=f"YC{h}{i}") for i in range(2)] for h in range(2)]
    psab = [[Buf(f"psab{h}{i}", ps[h].t[:, i * 64:(i + 1) * 64]) for i in range(2)] for h in range(2)]
    pvk = [[Buf(f"pvk{h}{i}", ps[2 + h].t[:, i * 64:(i + 1) * 64]) for i in range(2)] for h in range(2)]
    py = [[Buf(f"py{h}{i}", ps[4 + h].t[0:64, i * 2 * CH:(i + 1) * 2 * CH].rearrange("p (d c) -> p d c", d=2)) for i in range(2)] for h in range(2)]
    for h in range(2):
        P.op("dve", lambda e, h=h: e.memset(ST[h][0][:, :], 0.0), [ps[h], ps[2 + h], ps[4 + h]], [ST[h][0]])

    def flush(h, pci):
        yc = YC[h][pci % 2]
        P.op("act", lambda e, yc=yc, yp=py[h][pci % 2]: e.copy(yc[:, :, :], yp[:, :, :]), [py[h][pci % 2]], [yc])
        for d in range(2):
            P.dma(yT[h].t[d * 64:(d + 1) * 64, pci * CH:(pci + 1) * CH], yc[:, d, :], [yc], [yT[h]], key=f"yst{h}{pci % 2}")

    nchunk = LT // CH
    step = 0
    for ci in range(nchunk):
        n0 = ci * CH
        for h in range(2):
            cb = CB[h][ci % 2]; ar = AR[h][ci % 2]; kvb = KV[h][ci % 2]
            P.dma(cb[:, :, :], cols[h].t[:, :, n0:n0 + CH].rearrange("q p t -> p q t"), [cols[h]], [cb], key=f"cb{h}{ci % 2}")
            for d in range(2):
                P.dma(kvb[d * 64:d * 64 + 1, :, :, :].rearrange("p q t k -> p q (t k)"),
                      kv_t[h].ap()[:, d:d + 1, n0:n0 + CH, :].rearrange("q o t k -> o q (t k)"), [kv[h]], [kvb], key=f"kv{h}{ci % 2}")
            P.op("act", lambda e, ar=ar, cb=cb: e.activation(ar[:, :, :], cb[:, 1, :].unsqueeze(2).broadcast_to([128, CH, 64]), AF.Identity), [cb], [ar])
        for i in range(CH):
            for h in range(2):
                cb = CB[h][ci % 2]; ar = AR[h][ci % 2]; kvb = KV[h][ci % 2]
                so = ST[h][step % 2]; sn = ST[h][(step + 1) % 2]; tm = Tm[h]
                sab = psab[h][step % 2]; vk = pvk[h][step % 2]
                for d in range(2):
                    lo = d * 64
                    P.op("pe", lambda e, ar=ar, so=so, sab=sab, lo=lo, i=i: e.matmul(sab[lo:lo + 64, :], ar[lo:lo + 64, i, :], so[lo:lo + 64, :], start=True, stop=True), [ar, so], [sab])
                for d in range(2):
                    lo = d * 64
                    P.op("pe", lambda e, kvb=kvb, vk=vk, lo=lo, i=i: e.matmul(vk[lo:lo + 64, :], kvb[lo:lo + 1, 0, i, :], kvb[lo:lo + 1, 1, i, :], start=True, stop=True), [kvb], [vk])
                if step > 0:
                    pi = (i - 1) % CH
                    pci = ci if i > 0 else ci - 1
                    ypp = py[h][pci % 2]; cbp = CB[h][pci % 2]
                    for d in range(2):
                        lo = d * 64
                        P.op("pe", lambda e, so=so, cbp=cbp, ypp=ypp, lo=lo, d=d, pi=pi: e.matmul(ypp[:, d, pi:pi + 1], so[lo:lo + 64, :], cbp[lo:lo + 64, 3, pi:pi + 1], start=True, stop=True), [so, cbp], [ypp])
                    if i == 0:
                        flush(h, pci)
                P.op("dve", lambda e, tm=tm, so=so, cb=cb, vk=vk, i=i: e.scalar_tensor_tensor(tm[:, :], so[:, :], cb[:, 0, i:i + 1], vk[:, :], ALU.mult, ALU.add), [so, cb, vk], [tm])
                P.op("dve", lambda e, tm=tm, sn=sn, cb=cb, sab=sab, i=i: e.scalar_tensor_tensor(sn[:, :], sab[:, :], cb[:, 2, i:i + 1], tm[:, :], ALU.mult, ALU.add), [sab, cb, tm], [sn])
            step += 1
    lastc = nchunk - 1
    for h in range(2):
        so = ST[h][step % 2]; cbp = CB[h][lastc % 2]; ypp = py[h][lastc % 2]
        for d in range(2):
            lo = d * 64
            P.op("pe", lambda e, so=so, cbp=cbp, ypp=ypp, lo=lo, d=d: e.matmul(ypp[:, d, CH - 1:CH], so[lo:lo + 64, :], cbp[lo:lo + 64, 3, CH - 1:CH], start=True, stop=True), [so, cbp], [ypp])
        flush(h, lastc)

    yf = [P.sb([128, 128], F32, name=f"yf{i}") for i in range(2)]
    yb = [P.sb([128, 128], F32, name=f"yb{i}") for i in range(2)]
    yt = [P.sb([128, 128], F32, name=f"yt{i}") for i in range(2)]
    w3 = [P.sb([128, 128], F32, name=f"w3_{i}") for i in range(6)]
    k3 = {"i": 0, "w": 0}

    def n3():
        b = w3[k3["w"] % 6]
        k3["w"] += 1
        return b

    for (s0, sl) in segs:
        if s0 == 0 and not want_ctx_out:
            continue
        nb = sl // 128
        for b in range(nb):
            t0 = s0 + b * 128
            tb = s0 + (nb - 1 - b) * 128
            f_, b_, t_ = yf[k3["i"] % 2], yb[k3["i"] % 2], yt[k3["i"] % 2]
            k3["i"] += 1
            for h in range(2):
                P.dma(f_[h * 64:(h + 1) * 64, :], yT[h].t[0:64, t0:t0 + 128], [yT[h]], [f_], key=f"yf{k3['i'] % 2}")
                P.dma(b_[h * 64:(h + 1) * 64, :], yT[h].t[64:128, tb:tb + 128], [yT[h]], [b_], key=f"yb{k3['i'] % 2}")
            p1 = nps()
            P.op("pe", lambda e, p1=p1, b_=b_: e.matmul(p1[:, :128], b_[:, :], ident, start=True, stop=True), [b_, cm], [p1])
            P.op("act", lambda e, p1=p1, t_=t_: e.copy(t_[:, :], p1[:, :128]), [p1], [t_])
            p2 = nps()
            P.op("pe", lambda e, p2=p2, t_=t_: e.matmul(p2[:, :128], t_[:, :], jm, start=True, stop=True), [t_, cm], [p2])
            y = n3()
            P.op("dve", lambda e, y=y, p2=p2, f_=f_: e.tensor_tensor(y[:, :], p2[:, :128], f_[:, :], ALU.add), [p2, f_], [y])
            pm = nps()
            P.op("pe", lambda e, pm=pm, y=y: e.matmul(pm[:, :128], bones, y[:, :], start=True, stop=True), [cm, y], [pm])
            yc_ = n3()
            P.op("dve", lambda e, yc_=yc_, pm=pm, y=y: e.scalar_tensor_tensor(yc_[:, :], pm[:, :128], -1.0 / 64, y[:, :], ALU.mult, ALU.add), [pm, y], [yc_])
            sq = n3()
            P.op("act", lambda e, sq=sq, yc_=yc_: e.activation(sq[:, :], yc_[:, :], AF.Square), [yc_], [sq])
            pv_ = nps()
            P.op("pe", lambda e, pv_=pv_, sq=sq: e.matmul(pv_[:, :128], bones, sq[:, :], start=True, stop=True), [cm, sq], [pv_])
            P.op("dve", lambda e, pv_=pv_, sq=sq: e.tensor_scalar(sq[:, :], pv_[:, :128], 1.0 / 64, GN_EPS, ALU.mult, ALU.add), [pv_], [sq])
            P.op("act", lambda e, sq=sq: e.activation(sq[:, :], sq[:, :], AF.Sqrt), [sq], [sq])
            P.op("dve", lambda e, sq=sq: e.reciprocal(sq[:, :], sq[:, :]), [sq], [sq])
            P.op("dve", lambda e, yc_=yc_, sq=sq: e.scalar_tensor_tensor(yc_[:, :], yc_[:, :], pv[:, 7:8], sq[:, :], ALU.mult, ALU.mult), [yc_, pv, sq], [yc_])
            P.op("dve", lambda e, yc_=yc_, t0=t0: e.scalar_tensor_tensor(yc_[:, :], yc_[:, :], pv[:, 8:9], bonus[:, t0:t0 + 128], ALU.add, ALU.add), [yc_, pv, bonus], [yc_])
            o_ = n3()
            P.op("dve", lambda e, o_=o_, yc_=yc_, t0=t0: e.tensor_tensor(o_[:, :], yc_[:, :], gout[:, t0:t0 + 128], ALU.mult), [yc_, gout], [o_])
            dst = yr[:, t0 - LC:t0 - LC + 128] if s0 > 0 else yrc[:, t0:t0 + 128]
            P.dma(dst, o_[:, :], [o_], [Buf("yrout", None)], key=f"st{o_.name}")
    return P.build()


def build_M(nc, D, NCOL, NLAY):
    P = Prog(nc)
    DC = D // 128
    cc = P.dram("cc", [128, DC, 2], F32, kind="ExternalInput")
    wm = P.dram("wm", [NLAY, D, NCOL], F32, kind="ExternalInput")
    bm = P.dram("bm", [NLAY, 2, NCOL], F32, kind="ExternalInput")
    mo = nc.dram_tensor("mo", [NLAY, 2, NCOL], F32, kind="ExternalOutput").ap()
    cs = P.sb([128, DC, 2], F32, name="cs")
    P.dma(cs[:], cc.t, [cc], [cs])
    P.op("act", lambda e: e.activation(cs[:], cs[:], AF.Silu), [cs], [cs])
    wt = [P.sb([128, DC, 256], F32, name=f"wt{i}") for i in range(4)]
    ps = [P.ps([128, 512], F32, name=f"ps{i}") for i in range(4)]
    ob = [P.sb([2, 256], F32, name=f"ob{i}") for i in range(3)]
    bb = [P.sb([2, 256], F32, name=f"bb{i}") for i in range(3)]
    i = 0
    for l in range(NLAY):
        for c0 in range(0, NCOL, 256):
            n = min(256, NCOL - c0)
            w = wt[i % 4]; p_ = ps[i % 4]; o = ob[i % 3]; b = bb[i % 3]
            i += 1
            P.dma(w[:, :, :n], wm.t[l, :, c0:c0 + n].rearrange("(c p) n -> p c n", p=128), [wm], [w])
            P.dma(b[:, :n], bm.t[l, :, c0:c0 + n], [bm], [b])
            for k in range(DC):
                P.op("pe", lambda e, w=w, p_=p_, k=k, n=n: e.matmul(p_[:2, :n], cs[:, k, :], w[:, k, :n], start=(k == 0), stop=(k == DC - 1)), [cs, w], [p_])
            P.op("dve", lambda e, o=o, p_=p_, b=b, n=n: e.tensor_tensor(o[:, :n], p_[:2, :n], b[:, :n], ALU.add), [p_, b], [o])
            P.dma(mo[l, :, c0:c0 + n], o[:, :n], [o], [Buf("moout", None)], key=f"st{o.name}")
    return P.build()

from concourse.bass_utils import run_bass_kernel_spmd

NCORES = 8
D_MODEL = 4096
SEQ = 8192
CTX_LEN = 256
D_FF = 5632
DEPTH = 2
HY_W = 1024
DA_W = 2048
RW_W = 1024
O_HY, O_Q, O_K, O_V, O_RW = 0, 3072, 5120, 7168, 9216
RW_STATE = 2304
O_RW_RG = O_RW + RW_STATE
O_GATE = 12704
TLAT = SEQ // NCORES
TCTX = CTX_LEN // NCORES
TT = TLAT + TCTX
DC = D_MODEL // 128


def _launch(build, in_maps):
    nc = bass.Bass("TRN2", target_bir_lowering=False)
    build(nc)
    res = run_bass_kernel_spmd(nc, in_maps, core_ids=list(range(NCORES)))
    return res.results


def _pc(vec):
    return np.asarray(vec, np.float32).reshape(DC, 128).T


def _vecs(rows):
    return np.ascontiguousarray(np.stack([_pc(r) for r in rows], axis=1))


def kernel(x, c, ctx, c_ctx, w_mod, b_mod, norm_gain, w_ff_in, w_ff_out, w_in,
           hy_short, hy_w1, hy_b1, hy_f1, hy_w2, hy_b2, hy_f2, hy_w3, hy_skip,
           da_lambda, da_subln,
           rw_shift, rw_w0, rw_w2, rw_a0, rw_a2, rw_g2, rw_k_k, rw_k_a, rw_r_k, rw_gn_w, rw_gn_b,
           w_branch, w_out, final_gain):
    f32 = np.float32
    A = lambda a: np.asarray(a, f32)
    x, c, ctx, c_ctx = A(x), A(c), A(ctx), A(c_ctx)
    w_in = A(w_in)
    NCOL = 9 * D_MODEL // NCORES
    cc = np.ascontiguousarray(np.stack([c[0], c_ctx], 0).reshape(2, DC, 128).transpose(2, 1, 0))
    w_mod = A(w_mod); b_mod = A(b_mod)
    ims = []
    for i in range(NCORES):
        sl = slice(i * NCOL, (i + 1) * NCOL)
        ims.append({"cc": cc, "wm": np.ascontiguousarray(w_mod[:, :, sl]),
                    "bm": np.ascontiguousarray(np.broadcast_to(b_mod[:, None, sl], (DEPTH, 2, NCOL)))})
    res = _launch(lambda nc: build_M(nc, D_MODEL, NCOL, DEPTH), ims)
    mo = np.concatenate([r["mo"] for r in res], axis=2)
    del ims
    mod = mo[:, 0].reshape(DEPTH, 9, D_MODEL)
    modc = mo[:, 1].reshape(DEPTH, 9, D_MODEL)

    xT = [np.ascontiguousarray(np.concatenate([x[0, i * TLAT:(i + 1) * TLAT], ctx[0, i * TCTX:(i + 1) * TCTX]], 0).T) for i in range(NCORES)]
    cs_tab, rt = rope_tables(SEQ)
    bones = np.zeros((128, 128), f32); bones[:64, :64] = 1; bones[64:, 64:] = 1
    cm = np.stack([np.eye(128, dtype=f32), np.eye(128, dtype=f32)[::-1].copy(), bones])
    LT = CTX_LEN + SEQ

    for l in range(DEPTH):
        last = l == DEPTH - 1
        lam_init = 0.8 - 0.6 * math.exp(-0.3 * l)
        ng = A(norm_gain[l])
        vA = _vecs([ng[0], ng[1], mod[l, 0], mod[l, 1], mod[l, 2], mod[l, 3], mod[l, 4],
                    modc[l, 0], modc[l, 1], modc[l, 2], modc[l, 3], modc[l, 4]])
        wfi, wfo = A(w_ff_in[l, 0]), A(w_ff_out[l, 0])
        win = np.ascontiguousarray(w_in[l][:, :O_GATE])
        ims = [{"xT": xT[i], "vecs": vA, "wfi": wfi, "wfo": wfo, "win": win} for i in range(NCORES)]
        res = _launch(lambda nc: build_A(nc, D_MODEL, D_FF, O_GATE, TT, TLAT), ims)
        del ims, win
        x1T = [r["x1T"] for r in res]
        p_lat = np.concatenate([r["pT"][:, :TLAT].T for r in res], 0)
        p_ctx = np.concatenate([r["pT"][:, TLAT:].T for r in res], 0)
        del res

        hs = A(hy_short[l]); w3 = A(hy_w3[l]).reshape(64, 2, 2, HY_W); sk = A(hy_skip[l])
        v1 = np.ascontiguousarray(np.stack([A(hy_b1[l]), A(hy_f1[l]), A(hy_b2[l]), A(hy_f2[l])], 1))

        def hy_run(pp, L):
            ims = []
            for i in range(NCORES):
                c_lo = i * 128
                cols = np.concatenate([np.arange(c_lo, c_lo + 128) + k * HY_W for k in range(3)])
                hp = np.zeros((L + 2, 384), f32); hp[1:L + 1] = pp[:, O_HY + cols]
                zT, win_c, jm = hyena_consts(L, 128, c_lo)
                swb = np.ascontiguousarray(np.broadcast_to(hs[:, cols].reshape(3, 3, 128).transpose(1, 0, 2)[None], (128, 3, 3, 128)))
                skipb = np.ascontiguousarray(np.broadcast_to(sk[:, c_lo:c_lo + 128][None], (128, 2, 128)))
                ims.append({"hp": hp, "swb": swb, "skipb": skipb, "zT": zT, "w1": A(hy_w1[l]), "v1": v1, "w2": A(hy_w2[l]),
                            "w3c": np.ascontiguousarray(w3[:, :, :, c_lo:c_lo + 128]), "win": win_c, "jm": jm})
            res = _launch(lambda nc: build_hy(nc, L, 128, 32 if L > 1024 else 64), ims)
            return np.concatenate([r["yh"].transpose(2, 0, 1).reshape(L, 128) for r in res], 1)
        y_h = hy_run(p_lat, SEQ)
        yc_h = hy_run(p_ctx, CTX_LEN) if not last else None

        ims = []
        for i in range(NCORES):
            hc = slice(i * 256, (i + 1) * 256)

            def fm(pp, off):
                return np.ascontiguousarray(pp[:, off:off + DA_W][:, hc].reshape(-1, 2, 2, 64).transpose(1, 2, 3, 0))
            vv = np.concatenate([p_lat[:, O_V:O_V + DA_W][:, hc], p_ctx[:, O_V:O_V + DA_W][:, hc]], 0)
            im = {"q": fm(p_lat, O_Q), "k": fm(p_lat, O_K), "kc": fm(p_ctx, O_K),
                  "v": np.ascontiguousarray(vv.reshape(LT, 2, 128).transpose(1, 0, 2)), "cs": cs_tab, "rt": rt,
                  "lamv": np.ascontiguousarray(A(da_lambda[l]).reshape(1, 256)), "subln": np.ascontiguousarray(A(da_subln[l]).reshape(128, 1))}
            if not last:
                im["qc"] = fm(p_ctx, O_Q)
            ims.append(im)
        res = _launch(lambda nc: build_at(nc, 2, SEQ, CTX_LEN, lam_init, not last), ims)
        del ims
        y_a = np.concatenate([r["ya"].transpose(2, 0, 1).reshape(SEQ, 256) for r in res], 1)
        yc_a = np.concatenate([r["yac"].transpose(2, 0, 1).reshape(CTX_LEN, 256) for r in res], 1) if not last else None
        del res

        rs = A(rw_shift[l]); w0 = A(rw_w0[l]); w2 = A(rw_w2[l]); a0 = A(rw_a0[l]); a2 = A(rw_a2[l]); g2 = A(rw_g2[l])

        def padded(rows_c, rows_l):
            n = rows_c.shape[0]
            o = np.zeros((2, n, 128, LT + 4), f32)
            o[0, :, :, 1:1 + CTX_LEN] = rows_c; o[0, :, :, CTX_LEN + 3:CTX_LEN + 3 + SEQ] = rows_l
            o[1, :, :, 1:1 + CTX_LEN] = rows_c[:, :, ::-1]; o[1, :, :, CTX_LEN + 3:CTX_LEN + 3 + SEQ] = rows_l[:, :, ::-1]
            return o

        def grpB(pp):
            r = pp[:, O_RW:O_GATE]
            gd2 = np.zeros((pp.shape[0], 128), f32); gd2[:, :32] = r[:, RW_STATE + RW_W + 128:]
            return np.stack([r[:, 2 * RW_W:2 * RW_W + 128].T, r[:, 2 * RW_W + 128:2 * RW_W + 256].T, r[:, RW_STATE + RW_W:RW_STATE + RW_W + 128].T, gd2.T])
        rinB = padded(grpB(p_ctx), grpB(p_lat))
        ims = []
        for i in range(NCORES):
            cs_ = slice(i * 128, (i + 1) * 128)

            def grpA(pp):
                r = pp[:, O_RW:O_GATE]
                return np.stack([r[:, 0:RW_W][:, cs_].T, r[:, RW_W:2 * RW_W][:, cs_].T, r[:, RW_STATE:RW_STATE + RW_W][:, cs_].T])
            sw = lambda cols: rs[:, cols].T
            swt = np.zeros((128, 7, 3), f32)
            swt[:, 0] = sw(np.arange(0, RW_W)[cs_]); swt[:, 1] = sw(np.arange(RW_W, 2 * RW_W)[cs_])
            swt[:, 2] = sw(np.arange(2 * RW_W, 2 * RW_W + 128)); swt[:, 3] = sw(np.arange(2 * RW_W + 128, 2 * RW_W + 256))
            swt[:, 4] = sw(np.arange(RW_STATE, RW_STATE + RW_W)[cs_]); swt[:, 5] = sw(np.arange(RW_STATE + RW_W, RW_STATE + RW_W + 128))
            swt[:32, 6] = sw(np.arange(RW_STATE + RW_W + 128, RW_STATE + RW_W + 160))
            pvv = np.zeros((128, 12), f32)
            pvv[:, 0] = w0[0, cs_]; pvv[:, 1] = w0[1, cs_]; pvv[:, 2] = a0[0, cs_]; pvv[:, 3] = a0[1, cs_]
            pvv[:, 4] = A(rw_k_k[l])[cs_]; pvv[:, 5] = A(rw_k_a[l])[cs_]; pvv[:, 6] = A(rw_r_k[l]).reshape(-1)[cs_]
            pvv[:, 7] = A(rw_gn_w[l])[cs_]; pvv[:, 8] = A(rw_gn_b[l])[cs_]
            ims.append({"rinA": padded(grpA(p_ctx), grpA(p_lat)), "rinB": rinB, "swt": swt, "pv": pvv,
                        "w2t": np.ascontiguousarray(np.concatenate([w2[0][:, cs_], w2[1][:, cs_]], 0)),
                        "a2t": np.ascontiguousarray(np.concatenate([a2[0][:, cs_], a2[1][:, cs_]], 0)),
                        "g2a": np.ascontiguousarray(g2[:128, cs_]), "g2b": np.ascontiguousarray(g2[128:, cs_]), "cm": cm})
        res = _launch(lambda nc: build_rw(nc, SEQ, CTX_LEN, not last), ims)
        del ims, rinB
        y_r = np.concatenate([r["yr"].T for r in res], 1)
        yc_r = np.concatenate([r["yrc"].T for r in res], 1) if not last else None
        del res, p_lat, p_ctx

        y_lat = np.concatenate([y_h, y_a, y_r], 1)
        y_ctx = np.concatenate([yc_h, yc_a, yc_r], 1) if not last else np.zeros((CTX_LEN, 4096), f32)
        vC = _vecs([ng[1], ng[2], mod[l, 3], mod[l, 4], mod[l, 5], mod[l, 6], mod[l, 7], mod[l, 8],
                    modc[l, 3], modc[l, 4], modc[l, 5], modc[l, 6], modc[l, 7], modc[l, 8], A(final_gain)])
        wg = np.ascontiguousarray(w_in[l][:, O_GATE:])
        ims = []
        for i in range(NCORES):
            yT = np.ascontiguousarray(np.concatenate([y_lat[i * TLAT:(i + 1) * TLAT], y_ctx[i * TCTX:(i + 1) * TCTX]], 0).T)
            ims.append({"x1T": x1T[i], "yT": yT, "vecs": vC, "wg": wg, "wbr": A(w_branch[l]), "wo": A(w_out[l]),
                        "wfi": A(w_ff_in[l, 1]), "wfo": A(w_ff_out[l, 1])})
        res = _launch(lambda nc: build_C(nc, D_MODEL, D_FF, TT, TLAT, HY_W, DA_W, RW_W, last), ims)
        del ims, wg
        xT = [r["x3T"] for r in res]
        del res
    out = np.concatenate([t[:, :TLAT].T for t in xT], 0)[None]
    return np.ascontiguousarray(out.astype(np.float32))
```

```python
import math
import numpy as np

from contextlib import ExitStack
import numpy as np
import concourse.bass as bass
import concourse.mybir as mybir

F32 = mybir.dt.float32
BF16 = mybir.dt.bfloat16
AF = mybir.ActivationFunctionType
ALU = mybir.AluOpType
AX = mybir.AxisListType


class Buf:
    __slots__ = ("name", "t", "lw", "rd")

    def __init__(self, name, t=None):
        self.name = name
        self.t = t
        self.lw = None
        self.rd = []

    def __getitem__(self, k):
        return self.t[k]


class Op:
    __slots__ = ("eng", "fn", "dma", "deps", "sig", "sem", "val", "waits", "idx")

    def __init__(self, eng, fn, dma):
        self.eng, self.fn, self.dma = eng, fn, dma
        self.deps = []
        self.sig = False
        self.sem = None
        self.val = 0
        self.waits = []


class Prog:
    ENGS = ("pe", "dve", "act", "pool", "sp")

    def __init__(self, nc):
        self.nc = nc
        self.es = ExitStack()
        self.ops = []
        self.dma_keys = {}
        self.nbuf = 0

    def sb(self, shape, dt=F32, name=None):
        self.nbuf += 1
        name = name or f"sb{self.nbuf}"
        t = self.es.enter_context(self.nc.sbuf_tensor(name, list(shape), dt))
        return Buf(name, t)

    def ps(self, shape, dt=F32, name=None):
        self.nbuf += 1
        name = name or f"ps{self.nbuf}"
        t = self.es.enter_context(self.nc.psum_tensor(name, list(shape), dt))
        return Buf(name, t)

    def dram(self, name, shape, dt=F32, kind="Internal"):
        t = self.nc.dram_tensor(name, list(shape), dt, kind=kind)
        return Buf(name, t.ap())

    def _rec(self, eng, fn, reads, writes, dma=False, key=None):
        op = Op(eng, fn, dma)
        op.idx = len(self.ops)
        if dma:
            op.sem = key if key is not None else writes[0].name
        for b in reads:
            if b.lw is not None:
                op.deps.append((b.lw, "raw"))
        for b in writes:
            if b.lw is not None:
                op.deps.append((b.lw, "waw"))
            for r in b.rd:
                op.deps.append((r, "war"))
        for b in reads:
            b.rd.append(op)
        for b in writes:
            b.lw = op
            b.rd = []
        self.ops.append(op)
        return op

    def op(self, eng, fn, reads=(), writes=()):
        return self._rec(eng, fn, list(reads), list(writes))

    def dma(self, out_ap, in_ap, reads, writes, eng="sp", key=None, **kw):
        def fn(e):
            return e.dma_start(out=out_ap, in_=in_ap, **kw)
        return self._rec(eng, fn, list(reads), list(writes), dma=True, key=key)

    def build(self):
        nc = self.nc
        need = []
        for x in self.ops:
            for (p, kind) in x.deps:
                if p is x:
                    continue
                if not p.dma and not x.dma and p.eng == x.eng:
                    if p.eng == "pe" or kind != "raw":
                        continue
                need.append((x, p))
                p.sig = True
        cnt = {e: 0 for e in self.ENGS}
        dcnt = {}
        for o in self.ops:
            if o.dma:
                dcnt[o.sem] = dcnt.get(o.sem, 0) + 16
                o.val = dcnt[o.sem]
                o.sig = True
            elif o.sig:
                cnt[o.eng] += 1
                o.val = cnt[o.eng]
                o.sem = "eng_" + o.eng
        semnames = ["eng_" + e for e in self.ENGS if cnt[e] > 0] + list(dcnt.keys())
        sems = {}
        for i, n in enumerate(semnames):
            sems[n] = self.es.enter_context(nc.semaphore(f"s{i}"))
        self.nsems = len(sems)
        waited = {e: {} for e in self.ENGS}
        per_eng = {e: [] for e in self.ENGS}
        needmap = {}
        for (x, p) in need:
            needmap.setdefault(id(x), []).append(p)
        for x in self.ops:
            w = waited[x.eng]
            best = {}
            for p in needmap.get(id(x), []):
                if w.get(p.sem, 0) >= p.val:
                    continue
                if best.get(p.sem, 0) < p.val:
                    best[p.sem] = p.val
            for s, v in best.items():
                w[s] = v
                x.waits.append((s, v))
            per_eng[x.eng].append(x)
        finals = [(s, v) for s, v in dcnt.items()]
        engobj = {"pe": "tensor", "dve": "vector", "act": "scalar", "pool": "gpsimd", "sp": "sync"}
        with nc.Block() as block:
            for e in self.ENGS:
                lst = per_eng[e]
                if not lst and e != "sp":
                    continue

                def body(eng, lst=lst, e=e):
                    for x in lst:
                        for (s, v) in x.waits:
                            eng.wait_ge(sems[s], v)
                        ins = x.fn(eng)
                        if x.sig:
                            ins.then_inc(sems[x.sem], 16 if x.dma else 1)
                    if e == "sp":
                        for (s, v) in finals:
                            eng.wait_ge(sems[s], v)
                getattr(block, engobj[e])(body)
        self.es.close()
        return cnt, dcnt


EPS = 1e-6


def split_tokens(TT):
    n = (TT + 511) // 512
    sz = TT // n
    assert sz * n == TT
    return [(i * sz, sz) for i in range(n)]


class Env:
    pass


def setup_env(P, D, TT, KCmax, nvec):
    E = Env()
    E.D, E.TT = D, TT
    E.DC = D // 128
    E.tts = split_tokens(TT)
    E.ps = [P.ps([128, 512], F32, name=f"psb{i}") for i in range(8)]
    E.psi = 0
    E.wt = [P.sb([128, KCmax, 256], BF16, name=f"wt{i}") for i in range(3)]
    E.wi = 0
    E.actA = P.sb([128, KCmax * TT], BF16, name="actA")
    E.xin = [P.sb([128, TT], F32, name=f"xin{i}") for i in range(2)]
    E.xi = 0
    E.tmp = [P.sb([128, TT], F32, name=f"tmp{i}") for i in range(2)]
    E.ti = 0
    E.ost = [P.sb([128, TT], F32, name=f"ost{i}") for i in range(3)]
    E.oi = 0
    E.obf = [P.sb([128, TT], BF16, name=f"obf{i}") for i in range(3)]
    E.bi = 0
    E.acc = P.sb([128, TT], F32, name="acc")
    E.rstd = P.sb([128, TT], F32, name="rstd")
    E.ones = P.sb([128, 128], F32, name="ones")
    P.op("dve", lambda e: e.memset(E.ones[:], 1.0), [], [E.ones])
    E.vec = P.sb([128, nvec, E.DC], F32, name="vec")
    return E


def rot(E, name, idx):
    lst = getattr(E, name)
    i = getattr(E, idx)
    setattr(E, idx, i + 1)
    return lst[i % len(lst)]


def psum_group(E):
    g = []
    for _ in E.tts:
        g.append(E.ps[E.psi % 8])
        E.psi += 1
    return g


def load_w(P, E, Wd, r0, KC, c0, width):
    wt = rot(E, "wt", "wi")
    src = Wd.t[r0:r0 + KC * 128, c0:c0 + width].rearrange("(c p) n -> p c n", p=128)
    P.dma(wt[:, :KC, :width], src, [Wd], [wt], eng="pool")
    return wt


def mm_chunk(P, E, wt, wo, width, KC, Xb, xview, psg, first=True, last=True, kbase=0):
    for k in range(KC):
        for ti, (t0, sz) in enumerate(E.tts):
            ps = psg[ti]
            P.op("pe", lambda e, ps=ps, k=k, t0=t0, sz=sz: e.matmul(
                ps[:width, :sz], wt[:, k, wo:wo + width], xview[:, kbase + k, t0:t0 + sz],
                start=(first and k == 0), stop=(last and k == KC - 1)), [wt, Xb], [ps])


def rms_stats(P, E, acc_ready_buf):
    D = E.D
    psg = psum_group(E)
    for ti, (t0, sz) in enumerate(E.tts):
        ps = psg[ti]
        P.op("pe", lambda e, ps=ps, t0=t0, sz=sz: e.matmul(ps[:, :sz], E.ones[:, :], E.acc[:, t0:t0 + sz], start=True, stop=True),
             [E.ones, E.acc], [ps])
        P.op("dve", lambda e, ps=ps, t0=t0, sz=sz: e.tensor_scalar(E.rstd[:, t0:t0 + sz], ps[:, :sz], 1.0 / D, EPS, ALU.mult, ALU.add),
             [ps], [E.rstd])
    P.op("act", lambda e: e.activation(E.rstd[:, :], E.rstd[:, :], AF.Sqrt), [E.rstd], [E.rstd])
    P.op("dve", lambda e: e.reciprocal(E.rstd[:, :], E.rstd[:, :]), [E.rstd], [E.rstd])


def norm_pass(P, E, src_chunks, NL, gs, sh, gsc, shc):
    TT, DC = E.TT, E.DC
    for c in range(DC):
        xc = rot(E, "xin", "xi")
        P.dma(xc[:, :], src_chunks[c].t, [src_chunks[c]], [xc])
        if c == 0:
            P.op("act", lambda e, xc=xc: e.activation(E.acc[:, :], xc[:, :], AF.Square), [xc], [E.acc])
        else:
            sq = rot(E, "tmp", "ti")
            P.op("act", lambda e, xc=xc, sq=sq: e.activation(sq[:, :], xc[:, :], AF.Square), [xc], [sq])
            P.op("dve", lambda e, sq=sq: e.tensor_tensor(E.acc[:, :], E.acc[:, :], sq[:, :], ALU.add), [sq, E.acc], [E.acc])
    rms_stats(P, E, E.acc)
    xv = E.actA.t[:, :DC * TT].rearrange("p (c t) -> p c t", t=TT)
    for c in range(DC):
        xc = rot(E, "xin", "xi")
        P.dma(xc[:, :], src_chunks[c].t, [src_chunks[c]], [xc])
        tm = rot(E, "tmp", "ti")
        for (a, b, g_, s_) in ((0, NL, gs, sh), (NL, TT, gsc, shc)):
            if b <= a:
                continue
            P.op("dve", lambda e, xc=xc, tm=tm, a=a, b=b, g_=g_, c=c: e.scalar_tensor_tensor(
                tm[:, a:b], xc[:, a:b], E.vec[:, g_, c:c + 1], E.rstd[:, a:b], ALU.mult, ALU.mult), [xc, E.vec, E.rstd], [tm])
            P.op("act", lambda e, tm=tm, a=a, b=b, s_=s_, c=c: e.activation(
                xv[:, c, a:b], tm[:, a:b], AF.Identity, bias=E.vec[:, s_, c:c + 1], scale=1.0), [tm, E.vec], [E.actA])
    return xv


def derive_gs(P, E, dst, g, sc):
    P.op("dve", lambda e: e.scalar_tensor_tensor(E.vec[:, dst, :], E.vec[:, sc, :], 1.0, E.vec[:, g, :], ALU.add, ALU.mult), [E.vec], [E.vec])


def derive_half(P, E, dst, src):
    P.op("dve", lambda e: e.tensor_scalar(E.vec[:, dst, :], E.vec[:, src, :], 0.5, None, ALU.mult), [E.vec], [E.vec])


def ffn_block(P, E, NL, src_ch, dst_ch, wfi, wfo, FF, aT_ch, gs, sh, gsc, shc, hg, hgc):
    DC, TT = E.DC, E.TT
    FC = FF // 128
    hv = norm_pass(P, E, src_ch, NL, gs, sh, gsc, shc)
    for f in range(FC):
        if f % 2 == 0:
            nb = min(2, FC - f) * 128
            wg = load_w(P, E, wfi, 0, DC, f * 128, nb)
            wu = load_w(P, E, wfi, 0, DC, FF + f * 128, nb)
        wo = (f % 2) * 128
        pg = psum_group(E)
        mm_chunk(P, E, wg, wo, 128, DC, E.actA, hv, pg)
        pu = psum_group(E)
        mm_chunk(P, E, wu, wo, 128, DC, E.actA, hv, pu)
        sg = rot(E, "tmp", "ti")
        ao = rot(E, "obf", "bi")
        for ti, (t0, sz) in enumerate(E.tts):
            P.op("act", lambda e, ti=ti, t0=t0, sz=sz, pg=pg, sg=sg: e.activation(sg[:, t0:t0 + sz], pg[ti][:, :sz], AF.Silu), [pg[ti]], [sg])
            P.op("dve", lambda e, ti=ti, t0=t0, sz=sz, pu=pu, sg=sg, ao=ao: e.tensor_tensor(ao[:, t0:t0 + sz], sg[:, t0:t0 + sz], pu[ti][:, :sz], ALU.mult), [pu[ti], sg], [ao])
        P.dma(aT_ch[f].t, ao[:, :], [ao], [aT_ch[f]], eng="sp", key=f"st{ao.name}")
    av = E.actA.t[:, :FC * TT].rearrange("p (c t) -> p c t", t=TT)
    for f in range(FC):
        P.dma(av[:, f, :], aT_ch[f].t, [aT_ch[f]], [E.actA], key="actA_ld")
    for c in range(DC):
        if c % 2 == 0:
            wt = load_w(P, E, wfo, 0, FC, c * 128, 256)
        wo = (c % 2) * 128
        pg = psum_group(E)
        mm_chunk(P, E, wt, wo, 128, FC, E.actA, av, pg)
        xc = rot(E, "xin", "xi")
        P.dma(xc[:, :], src_ch[c].t, [src_ch[c]], [xc])
        xo = rot(E, "ost", "oi")
        for ti, (t0, sz) in enumerate(E.tts):
            for (a, b, hgi) in ((t0, min(t0 + sz, NL), hg), (max(t0, NL), t0 + sz, hgc)):
                if b <= a:
                    continue
                P.op("dve", lambda e, ti=ti, t0=t0, a=a, b=b, hgi=hgi, pg=pg, xc=xc, xo=xo, c=c: e.scalar_tensor_tensor(
                    xo[:, a:b], pg[ti][:, a - t0:b - t0], E.vec[:, hgi, c:c + 1], xc[:, a:b], ALU.mult, ALU.add),
                    [pg[ti], E.vec, xc], [xo])
        P.dma(dst_ch[c].t, xo[:, :], [xo], [dst_ch[c]], key=f"st{xo.name}")


def chunks_of(ap, n, pfx):
    return [Buf(f"{pfx}{c}", ap[c * 128:(c + 1) * 128, :]) for c in range(n)]


def build_A(nc, D, FF, NP, TT, NL):
    P = Prog(nc)
    DC, FC = D // 128, FF // 128
    E = setup_env(P, D, TT, max(DC, FC), 18)
    xT = P.dram("xT", [D, TT], F32, kind="ExternalInput")
    vecs = P.dram("vecs", [128, 12, DC], F32, kind="ExternalInput")
    wfi = P.dram("wfi", [D, 2 * FF], F32, kind="ExternalInput")
    wfo = P.dram("wfo", [FF, D], F32, kind="ExternalInput")
    win = P.dram("win", [D, NP], F32, kind="ExternalInput")
    x1T = nc.dram_tensor("x1T", [D, TT], F32, kind="ExternalOutput").ap()
    pT = nc.dram_tensor("pT", [NP, TT], F32, kind="ExternalOutput").ap()
    aT = nc.dram_tensor("aT", [FF, TT], BF16, kind="Internal").ap()
    xT_ch = chunks_of(xT.t, DC, "xT")
    x1_ch = chunks_of(x1T, DC, "x1T")
    aT_ch = chunks_of(aT, FC, "aT")
    P.dma(E.vec[:, 0:12, :], vecs.t, [vecs], [E.vec])
    derive_gs(P, E, 12, 0, 3); derive_gs(P, E, 13, 0, 8); derive_gs(P, E, 16, 1, 6); derive_gs(P, E, 17, 1, 11)
    derive_half(P, E, 14, 4); derive_half(P, E, 15, 9)
    ffn_block(P, E, NL, xT_ch, x1_ch, wfi, wfo, FF, aT_ch, 12, 2, 13, 7, 14, 15)
    uv = norm_pass(P, E, x1_ch, NL, 16, 5, 17, 10)
    nchunks = (NP + 127) // 128
    for n in range(nchunks):
        if n % 2 == 0:
            nb = min(256, NP - n * 128)
            wt = load_w(P, E, win, 0, DC, n * 128, nb)
        wo = (n % 2) * 128
        width = min(128, NP - n * 128)
        pg = psum_group(E)
        mm_chunk(P, E, wt, wo, width, DC, E.actA, uv, pg)
        po = rot(E, "ost", "oi")
        for ti, (t0, sz) in enumerate(E.tts):
            if (ti + n) % 2 == 0:
                P.op("act", lambda e, ti=ti, t0=t0, sz=sz, pg=pg, po=po, width=width: e.copy(po[:width, t0:t0 + sz], pg[ti][:width, :sz]), [pg[ti]], [po])
            else:
                P.op("dve", lambda e, ti=ti, t0=t0, sz=sz, pg=pg, po=po, width=width: e.tensor_copy(po[:width, t0:t0 + sz], pg[ti][:width, :sz]), [pg[ti]], [po])
        pch = Buf(f"pT{n}", pT[n * 128:n * 128 + width, :])
        P.dma(pch.t, po[:width, :], [po], [pch], key=f"st{po.name}")
    return P.build()


def build_C(nc, D, FF, TT, NL, WH, WA, WR, last):
    P = Prog(nc)
    DC, FC = D // 128, FF // 128
    MIX = WH + WA + WR
    MC = MIX // 128
    E = setup_env(P, D, TT, max(DC, FC, MC), 24)
    x1T = P.dram("x1T", [D, TT], F32, kind="ExternalInput")
    yT = P.dram("yT", [MIX, TT], F32, kind="ExternalInput")
    vecs = P.dram("vecs", [128, 15, DC], F32, kind="ExternalInput")
    wg = P.dram("wg", [D, 3 * D], F32, kind="ExternalInput")
    wbr = P.dram("wbr", [MIX, D], F32, kind="ExternalInput")
    wo_ = P.dram("wo", [D, D], F32, kind="ExternalInput")
    wfi = P.dram("wfi", [D, 2 * FF], F32, kind="ExternalInput")
    wfo = P.dram("wfo", [FF, D], F32, kind="ExternalInput")
    x3T = nc.dram_tensor("x3T", [D, TT], F32, kind="ExternalOutput").ap()
    x2T = nc.dram_tensor("x2T", [D, TT], F32, kind="Internal").ap()
    gT = nc.dram_tensor("gT", [3 * D, TT], BF16, kind="Internal").ap()
    mT = nc.dram_tensor("mT", [D, TT], BF16, kind="Internal").ap()
    aT = nc.dram_tensor("aT", [FF, TT], BF16, kind="Internal").ap()
    x1_ch = chunks_of(x1T.t, DC, "x1T")
    y_ch = chunks_of(yT.t, MC, "yT")
    x2_ch = chunks_of(x2T, DC, "x2T")
    x3_ch = chunks_of(x3T, DC, "x3T")
    g_ch = chunks_of(gT, 3 * DC, "gT")
    m_ch = chunks_of(mT, DC, "mT")
    aT_ch = chunks_of(aT, FC, "aT")
    P.dma(E.vec[:, 0:15, :], vecs.t, [vecs], [E.vec])
    derive_gs(P, E, 16, 0, 3); derive_gs(P, E, 17, 0, 9); derive_gs(P, E, 18, 1, 6); derive_gs(P, E, 19, 1, 12)
    derive_half(P, E, 20, 7); derive_half(P, E, 21, 13)
    uv = norm_pass(P, E, x1_ch, NL, 16, 2, 17, 8)
    for n in range(3 * DC):
        if n % 2 == 0:
            wt = load_w(P, E, wg, 0, DC, n * 128, 256)
        wo = (n % 2) * 128
        pg = psum_group(E)
        mm_chunk(P, E, wt, wo, 128, DC, E.actA, uv, pg)
        go = rot(E, "obf", "bi")
        for ti, (t0, sz) in enumerate(E.tts):
            P.op("act", lambda e, ti=ti, t0=t0, sz=sz, pg=pg, go=go: e.activation(go[:, t0:t0 + sz], pg[ti][:, :sz], AF.Sigmoid), [pg[ti]], [go])
        P.dma(g_ch[n].t, go[:, :], [go], [g_ch[n]], eng="sp", key=f"st{go.name}")
    yv = E.actA.t[:, :MC * TT].rearrange("p (c t) -> p c t", t=TT)
    for c in range(MC):
        P.dma(yv[:, c, :], y_ch[c].t, [y_ch[c]], [E.actA], eng="pool", key="actA_ld")
    kb = [(0, WH // 128), (WH // 128, WA // 128), ((WH + WA) // 128, WR // 128)]
    for c in range(DC):
        mo = rot(E, "ost", "oi")
        mb = rot(E, "obf", "bi")
        for b, (k0, kc) in enumerate(kb):
            wt = load_w(P, E, wbr, k0 * 128, kc, c * 128, 128)
            pg = psum_group(E)
            mm_chunk(P, E, wt, 0, 128, kc, E.actA, yv, pg, kbase=k0)
            gl = rot(E, "tmp", "ti")
            glb = gl.t.bitcast(BF16)
            P.dma(glb[:, :TT], g_ch[b * DC + c].t, [g_ch[b * DC + c]], [gl])
            for ti, (t0, sz) in enumerate(E.tts):
                if b == 0:
                    P.op("dve", lambda e, ti=ti, t0=t0, sz=sz, pg=pg, glb=glb, gl=gl, mo=mo: e.tensor_tensor(mo[:, t0:t0 + sz], pg[ti][:, :sz], glb[:, t0:t0 + sz], ALU.mult), [pg[ti], gl], [mo])
                else:
                    t2 = rot(E, "xin", "xi")
                    dst = mo if b == 1 else mb
                    P.op("dve", lambda e, ti=ti, t0=t0, sz=sz, pg=pg, glb=glb, gl=gl, t2=t2: e.tensor_tensor(t2[:, t0:t0 + sz], pg[ti][:, :sz], glb[:, t0:t0 + sz], ALU.mult), [pg[ti], gl], [t2])
                    P.op("dve", lambda e, t0=t0, sz=sz, t2=t2, mo=mo, dst=dst: e.tensor_tensor(dst[:, t0:t0 + sz], mo[:, t0:t0 + sz], t2[:, t0:t0 + sz], ALU.add), [t2, mo], [dst])
        P.dma(m_ch[c].t, mb[:, :], [mb], [m_ch[c]], eng="sp", key=f"st{mb.name}")
    mv = E.actA.t[:, :DC * TT].rearrange("p (c t) -> p c t", t=TT)
    for c in range(DC):
        P.dma(mv[:, c, :], m_ch[c].t, [m_ch[c]], [E.actA], key="actA_ld")
    for c in range(DC):
        if c % 2 == 0:
            wt = load_w(P, E, wo_, 0, DC, c * 128, 256)
        wo = (c % 2) * 128
        pg = psum_group(E)
        mm_chunk(P, E, wt, wo, 128, DC, E.actA, mv, pg)
        xc = rot(E, "xin", "xi")
        P.dma(xc[:, :], x1_ch[c].t, [x1_ch[c]], [xc])
        xo = rot(E, "ost", "oi")
        for ti, (t0, sz) in enumerate(E.tts):
            for (a, b, gi) in ((t0, min(t0 + sz, NL), 4), (max(t0, NL), t0 + sz, 10)):
                if b <= a:
                    continue
                P.op("dve", lambda e, ti=ti, t0=t0, a=a, b=b, gi=gi, pg=pg, xc=xc, xo=xo, c=c: e.scalar_tensor_tensor(
                    xo[:, a:b], pg[ti][:, a - t0:b - t0], E.vec[:, gi, c:c + 1], xc[:, a:b], ALU.mult, ALU.add),
                    [pg[ti], E.vec, xc], [xo])
        P.dma(x2_ch[c].t, xo[:, :], [xo], [x2_ch[c]], key=f"st{xo.name}")
    if not last:
        ffn_block(P, E, NL, x2_ch, x3_ch, wfi, wfo, FF, aT_ch, 18, 5, 19, 11, 20, 21)
    else:
        x3i = nc.dram_tensor("x3i", [D, TT], F32, kind="Internal").ap()
        x3i_ch = chunks_of(x3i, DC, "x3i")
        ffn_block(P, E, NL, x2_ch, x3i_ch, wfi, wfo, FF, aT_ch, 18, 5, 19, 11, 20, 21)
        for c in range(DC):
            xc = rot(E, "xin", "xi")
            P.dma(xc[:, :], x3i_ch[c].t, [x3i_ch[c]], [xc])
            if c == 0:
                P.op("act", lambda e, xc=xc: e.activation(E.acc[:, :], xc[:, :], AF.Square), [xc], [E.acc])
            else:
                sq = rot(E, "tmp", "ti")
                P.op("act", lambda e, xc=xc, sq=sq: e.activation(sq[:, :], xc[:, :], AF.Square), [xc], [sq])
                P.op("dve", lambda e, sq=sq: e.tensor_tensor(E.acc[:, :], E.acc[:, :], sq[:, :], ALU.add), [sq, E.acc], [E.acc])
        rms_stats(P, E, E.acc)
        for c in range(DC):
            xc = rot(E, "xin", "xi")
            P.dma(xc[:, :], x3i_ch[c].t, [x3i_ch[c]], [xc])
            xo = rot(E, "ost", "oi")
            P.op("dve", lambda e, xc=xc, xo=xo, c=c: e.scalar_tensor_tensor(
                xo[:, :], xc[:, :], E.vec[:, 14, c:c + 1], E.rstd[:, :], ALU.mult, ALU.mult), [xc, E.vec, E.rstd], [xo])
            P.dma(x3_ch[c].t, xo[:, :], [xo], [x3_ch[c]], key=f"st{xo.name}")
    return P.build()

import numpy as np

EPS = 1e-6


def build_at(nc, H, LQ, LC, lam_init, with_ctx_q):
    P = Prog(nc)
    LK = LQ + LC
    NKC = LK // 128
    q = P.dram("q", [H, 2, 64, LQ], F32, kind="ExternalInput")
    k = P.dram("k", [H, 2, 64, LQ], F32, kind="ExternalInput")
    kc = P.dram("kc", [H, 2, 64, LC], F32, kind="ExternalInput")
    v = P.dram("v", [H, LK, 128], F32, kind="ExternalInput")
    cs = P.dram("cs", [2, 64, LQ], F32, kind="ExternalInput")
    rt = P.dram("rt", [64, 64], F32, kind="ExternalInput")
    lamv = P.dram("lamv", [1, 256], F32, kind="ExternalInput")
    subln = P.dram("subln", [128, 1], F32, kind="ExternalInput")
    ya = nc.dram_tensor("ya", [H, 128, LQ], F32, kind="ExternalOutput").ap()
    if with_ctx_q:
        qc = P.dram("qc", [H, 2, 64, LC], F32, kind="ExternalInput")
        yac = nc.dram_tensor("yac", [H, 128, LC], F32, kind="ExternalOutput").ap()
    Qa = [P.sb([65, LK], BF16, name=f"Qa{m}") for m in range(2)]
    Ka = [P.sb([65, LK], BF16, name=f"Ka{m}") for m in range(2)]
    Vb = P.sb([128, NKC, 128], BF16, name="Vb")
    ps_s = [P.ps([128, 512], F32, name=f"pss{i}") for i in range(3)]
    ps_o = [P.ps([128, 512], F32, name=f"pso{i}") for i in range(2)]
    ps_l = [P.ps([128, 512], F32, name=f"psl{i}") for i in range(2)]
    ps_x = P.ps([128, 512], F32, name="psx")
    xin = [P.sb([64, 512], F32, name=f"xin{i}") for i in range(3)]
    csb = [P.sb([64, 2, 512], F32, name=f"csb{i}") for i in range(2)]
    t1 = [P.sb([64, 512], F32, name=f"t1_{i}") for i in range(2)]
    t2 = [P.sb([64, 512], F32, name=f"t2_{i}") for i in range(2)]
    sq = [P.sb([64, 512], F32, name=f"sq{i}") for i in range(2)]
    pt = [P.sb([128, 512], BF16, name=f"pt{i}") for i in range(3)]
    wk = [P.sb([128, 512], F32, name=f"wk{i}") for i in range(6)]
    yo = [P.sb([128, 512], F32, name=f"yo{i}") for i in range(2)]
    rtb = P.sb([64, 64], F32, name="rtb")
    ones65 = P.sb([64, 65], F32, name="ones65")
    onesf = P.sb([128, 128], F32, name="onesf")
    onesb = P.sb([128, 128], BF16, name="onesb")
    kmax2 = P.sb([65, 1], F32, name="kmax2")
    kmt = P.sb([65, 1], F32, name="kmt")
    lam = P.sb([128, 4], F32, name="lam")
    lrow = P.sb([1, 256], F32, name="lrow")
    lt = P.sb([1, 8], F32, name="lt")
    sub = P.sb([128, 1], F32, name="sub")
    cnt = {"x": 0, "c": 0, "t": 0, "s": 0, "p": 0, "w": 0, "y": 0, "ss": 0}

    def nxt(lst, key):
        b = lst[cnt[key] % len(lst)]
        cnt[key] += 1
        return b

    P.dma(rtb[:, :], rt.t, [rt], [rtb])
    P.dma(lrow[:, :], lamv.t, [lamv], [lrow])
    P.dma(sub[:, :], subln.t, [subln], [sub])
    P.op("dve", lambda e: e.memset(ones65[:, :], 1.0), [], [ones65])
    P.op("dve", lambda e: e.memset(onesf[:, :], 1.0), [], [onesf])
    P.op("dve", lambda e: e.memset(onesb[:, :], 1.0), [], [onesb])
    P.op("dve", lambda e: e.memset(lt[:, :], 0.0), [], [lt])
    lj = P.sb([1, 64], F32, name="lj")
    for i in range(2):
        P.op("dve", lambda e, i=i: e.scalar_tensor_tensor(lj[:, :], lrow[:, 128 * i:128 * i + 64], 1.0, lrow[:, 128 * i + 64:128 * i + 128],
                                                          ALU.mult, ALU.mult, accum_out=lt[:, i:i + 1]), [lrow, lt], [lj, lt])
    P.op("act", lambda e: e.activation(lt[:, 2:4], lt[:, 0:2], AF.Exp), [lt], [lt])
    P.op("dve", lambda e: e.tensor_tensor(lt[:, 4:5], lt[:, 3:4], lt[:, 2:3], ALU.subtract), [lt], [lt])
    P.op("dve", lambda e: e.tensor_scalar(lt[:, 4:5], lt[:, 4:5], -float(lam_init), None, ALU.add), [lt], [lt])
    P.op("pe", lambda e: e.matmul(ps_x[:, 0:1], onesf[0:1, :], lt[0:1, 4:5], start=True, stop=True), [onesf, lt], [ps_x])
    P.op("dve", lambda e: e.tensor_copy(lam[:, 0:1], ps_x[:, 0:1]), [ps_x], [lam])
    P.op("dve", lambda e: e.tensor_scalar(lam[:, 1:2], sub[:, 0:1], 1.0 - float(lam_init), None, ALU.mult), [sub], [lam])

    for h in range(H):
        P.dma(Vb[:, :, :], v.t[h].rearrange("(c p) e -> p c e", p=128), [v], [Vb], eng="pool")
        P.op("dve", lambda e: e.memset(kmax2[:, :], 0.0), [], [kmax2])
        for m in range(2):
            P.op("dve", lambda e, m=m: e.memset(Ka[m][64:65, :], -1.0), [], [Ka[m]])

        def rope_chunk(src_ap, src_buf, dst, c0, n, use_rope, c_tab0, is_q, m):
            x = nxt(xin, "x")
            P.dma(x[:, :n], src_ap, [src_buf], [x])
            s = nxt(sq, "ss")
            P.op("act", lambda e: e.activation(s[:, :n], x[:, :n], AF.Square), [x], [s])
            P.op("pe", lambda e: e.matmul(ps_x[:65, :n], ones65[:, :], s[:, :n], start=True, stop=True), [ones65, s], [ps_x])
            if use_rope:
                ct = nxt(csb, "c")
                P.dma(ct[:, :, :n], cs.t[:, :, c_tab0:c_tab0 + n].rearrange("a d t -> d a t"), [cs], [ct])
                pr = nxt(ps_s, "s")
                P.op("pe", lambda e: e.matmul(pr[:64, :n], rtb[:, :], x[:, :n], start=True, stop=True), [rtb, x], [pr])
                a = nxt(t1, "t")
                b = t2[(cnt["t"] - 1) % 2]
                P.op("dve", lambda e: e.tensor_tensor(a[:, :n], x[:, :n], ct[:, 0, :n], ALU.mult), [x, ct], [a])
                P.op("dve", lambda e: e.tensor_tensor(b[:, :n], pr[:64, :n], ct[:, 1, :n], ALU.mult), [pr, ct], [b])
                P.op("dve", lambda e: e.tensor_tensor(dst[0:64, c0:c0 + n], a[:, :n], b[:, :n], ALU.add), [a, b], [dst])
            else:
                P.op("dve", lambda e: e.tensor_copy(dst[0:64, c0:c0 + n], x[:, :n]), [x], [dst])
            if is_q:
                P.op("act", lambda e: e.activation(dst[64:65, c0:c0 + n], ps_x[64:65, :n], AF.Sqrt, scale=kmax2[64:65, 0:1]), [ps_x, kmax2], [dst])
            else:
                P.op("dve", lambda e: e.reduce_max(kmt[:, :], ps_x[:65, :n], axis=AX.X), [ps_x], [kmt])
                P.op("dve", lambda e: e.tensor_tensor(kmax2[:, :], kmax2[:, :], kmt[:, :], ALU.max), [kmt, kmax2], [kmax2])

        for m in range(2):
            for c0 in range(0, LQ, 512):
                n = min(512, LQ - c0)
                rope_chunk(k.t[h, m, :, c0:c0 + n], k, Ka[m], c0, n, True, c0, False, m)
            for c0 in range(0, LC, 512):
                n = min(512, LC - c0)
                rope_chunk(kc.t[h, m, :, c0:c0 + n], kc, Ka[m], LQ + c0, n, False, 0, False, m)
        for m in range(2):
            for c0 in range(0, LQ, 512):
                n = min(512, LQ - c0)
                rope_chunk(q.t[h, m, :, c0:c0 + n], q, Qa[m], c0, n, True, c0, True, m)
            if with_ctx_q:
                for c0 in range(0, LC, 512):
                    n = min(512, LC - c0)
                    rope_chunk(qc.t[h, m, :, c0:c0 + n], qc, Qa[m], LQ + c0, n, False, 0, True, m)

        def attend(qc0, n, kchunks, out_ap):
            LA = 2
            nk = len(kchunks)
            for m in range(2):
                sl = {}

                def emit_s(j, m=m, sl=sl):
                    ss = nxt(ps_s, "s")
                    kci = kchunks[j]
                    P.op("pe", lambda e, ss=ss, kci=kci, m=m: e.matmul(ss[:, :n], Ka[m][0:65, kci * 128:(kci + 1) * 128], Qa[m][0:65, qc0:qc0 + n],
                                                                   start=True, stop=True), [Ka[m], Qa[m]], [ss])
                    sl[j] = ss
                for j in range(min(LA, nk)):
                    emit_s(j)
                for j, kci in enumerate(kchunks):
                    if j + LA < nk:
                        emit_s(j + LA)
                    ss = sl.pop(j)
                    p_ = nxt(pt, "p")
                    P.op("act", lambda e, ss=ss, p_=p_: e.activation(p_[:, :n], ss[:, :n], AF.Exp, scale=0.125), [ss], [p_])
                    P.op("pe", lambda e, p_=p_, kci=kci, j=j, m=m: e.matmul(ps_o[m][:, :n], Vb[:, kci, :], p_[:, :n], start=(j == 0), stop=(j == nk - 1)),
                         [Vb, p_], [ps_o[m]])
                    P.op("pe", lambda e, p_=p_, j=j, m=m: e.matmul(ps_l[m][:, :n], onesb[:, :], p_[:, :n], start=(j == 0), stop=(j == nk - 1)),
                         [onesb, p_], [ps_l[m]])
            rl = [nxt(wk, "w") for _ in range(2)]
            o = [nxt(wk, "w") for _ in range(2)]
            for m in range(2):
                P.op("dve", lambda e, m=m: e.reciprocal(rl[m][:, :n], ps_l[m][:, :n]), [ps_l[m]], [rl[m]])
                P.op("dve", lambda e, m=m: e.tensor_tensor(o[m][:, :n], ps_o[m][:, :n], rl[m][:, :n], ALU.mult), [ps_o[m], rl[m]], [o[m]])
            d = nxt(wk, "w")
            P.op("dve", lambda e: e.scalar_tensor_tensor(d[:, :n], o[1][:, :n], lam[:, 0:1], o[0][:, :n], ALU.mult, ALU.add), [o[0], o[1], lam], [d])
            s2 = nxt(wk, "w")
            P.op("act", lambda e: e.activation(s2[:, :n], d[:, :n], AF.Square), [d], [s2])
            P.op("pe", lambda e: e.matmul(ps_x[:, :n], onesf[:, :], s2[:, :n], start=True, stop=True), [onesf, s2], [ps_x])
            r = rl[0]
            P.op("dve", lambda e: e.tensor_scalar(r[:, :n], ps_x[:, :n], 1.0 / 128, EPS, ALU.mult, ALU.add), [ps_x], [r])
            P.op("act", lambda e: e.activation(r[:, :n], r[:, :n], AF.Sqrt), [r], [r])
            P.op("dve", lambda e: e.reciprocal(r[:, :n], r[:, :n]), [r], [r])
            y = nxt(yo, "y")
            P.op("dve", lambda e: e.scalar_tensor_tensor(y[:, :n], d[:, :n], lam[:, 1:2], r[:, :n], ALU.mult, ALU.mult), [d, lam, r], [y])
            ob = Buf("yout", out_ap)
            P.dma(out_ap, y[:, :n], [y], [ob], key=f"st{y.name}")

        allk = list(range(NKC))
        for c0 in range(0, LQ, 512):
            n = min(512, LQ - c0)
            attend(c0, n, allk, ya[h, :, c0:c0 + n])
        if with_ctx_q:
            ck = list(range(LQ // 128, NKC))
            for c0 in range(0, LC, 512):
                n = min(512, LC - c0)
                attend(LQ + c0, n, ck, yac[h, :, c0:c0 + n])
    return P.build()


def rope_tables(LQ, grid_w=64, theta=10000.0):
    t = np.arange(LQ)
    row = (t // grid_w).astype(np.float32)
    col = (t % grid_w).astype(np.float32)
    inv = (theta ** (-np.arange(16, dtype=np.float32) / 16)).astype(np.float32)
    ang = np.zeros((64, LQ), np.float32)
    for d in range(64):
        pos = row if d < 32 else col
        ang[d] = pos * inv[d % 16]
    cs = np.stack([np.cos(ang), np.sin(ang)]).astype(np.float32)
    rt = np.zeros((64, 64), np.float32)
    for d in range(64):
        if d % 32 < 16:
            rt[d + 16, d] = -1.0
        else:
            rt[d - 16, d] = 1.0
    return cs, rt

import numpy as np
import math

MAGIC = 12582912.0
TWO_PI = 2.0 * math.pi


def build_hy(nc, L, C=128, CHP=64):
    P = Prog(nc)
    NB = L // 128
    NI = 2 * L - 1
    NE = 2 * NB - 1
    TW = NE * 128
    hp = P.dram("hp", [L + 2, 3 * C], F32, kind="ExternalInput")
    swb_d = P.dram("swb", [128, 3, 3, C], F32, kind="ExternalInput")
    skipb_d = P.dram("skipb", [128, 2, C], F32, kind="ExternalInput")
    zT = P.dram("zT", [33, NI], F32, kind="ExternalInput")
    w1_d = P.dram("w1", [33, 64], F32, kind="ExternalInput")
    v1_d = P.dram("v1", [64, 4], F32, kind="ExternalInput")
    w2_d = P.dram("w2", [64, 64], F32, kind="ExternalInput")
    w3_d = P.dram("w3c", [64, 2, 2, C], F32, kind="ExternalInput")
    win_d = P.dram("win", [C, NI], F32, kind="ExternalInput")
    jm_d = P.dram("jm", [128, 128], F32, kind="ExternalInput")
    yh = nc.dram_tensor("yh", [128, C, NB], F32, kind="ExternalOutput").ap()
    Hd_t = [nc.dram_tensor(f"Hd{o}", [C, 2 * L], BF16, kind="Internal") for o in range(2)]
    Hd = [Buf(f"Hd{o}", Hd_t[o].ap()) for o in range(2)]

    ps = [P.ps([128, 512], F32, name=f"psb{i}") for i in range(8)]
    pc = {"i": 0}

    def nps():
        b = ps[pc["i"] % 8]
        pc["i"] += 1
        return b

    swb = P.sb([128, 3, 3, C], F32, name="swb_s")
    skipb = P.sb([128, 2, C], F32, name="skipb_s")
    w1 = P.sb([33, 64], F32, name="w1_s")
    v1 = P.sb([64, 8], F32, name="v1_s")
    w2 = P.sb([64, 64], F32, name="w2_s")
    w3 = P.sb([64, 2, 2, C], F32, name="w3_s")
    jf = P.sb([128, 128], F32, name="jf")
    jb = P.sb([128, 128], BF16, name="jb")
    P.dma(swb[:], swb_d.t, [swb_d], [swb])
    P.dma(skipb[:], skipb_d.t, [skipb_d], [skipb])
    P.dma(w1[:], w1_d.t, [w1_d], [w1])
    P.dma(v1[:, 0:4], v1_d.t, [v1_d], [v1])
    P.dma(w2[:], w2_d.t, [w2_d], [w2])
    P.dma(w3[:], w3_d.t, [w3_d], [w3])
    P.dma(jf[:], jm_d.t, [jm_d], [jf])
    P.op("dve", lambda e: e.tensor_copy(jb[:], jf[:]), [jf], [jb])

    zin = [P.sb([33, 512], F32, name=f"zin{i}") for i in range(2)]
    fa = [P.sb([64, 512], F32, name=f"fa{i}") for i in range(2)]
    fq = [P.sb([64, 512], F32, name=f"fq{i}") for i in range(2)]
    fh = [P.sb([64, 512], F32, name=f"fh{i}") for i in range(4)]
    wn = [P.sb([C, 512], F32, name=f"wn{i}") for i in range(2)]
    hb = [P.sb([C, 512], BF16, name=f"hb{i}") for i in range(4)]
    fc = {"a": 0, "h": 0, "b": 0}

    def sin_layer(psrc, n, bcol, fcol):
        a = fa[fc["a"] % 2]
        q_ = fq[fc["a"] % 2]
        fc["a"] += 1
        h = fh[fc["h"] % 4]
        fc["h"] += 1
        P.op("dve", lambda e: e.tensor_scalar(a[:, :n], psrc[:64, :n], v1[:, bcol:bcol + 1], v1[:, fcol:fcol + 1], ALU.add, ALU.mult), [psrc, v1], [a])
        P.op("dve", lambda e: e.tensor_scalar(q_[:, :n], a[:, :n], 1.0 / TWO_PI, MAGIC, ALU.mult, ALU.add), [a], [q_])
        P.op("dve", lambda e: e.tensor_scalar(q_[:, :n], q_[:, :n], MAGIC, -TWO_PI, ALU.subtract, ALU.mult), [q_], [q_])
        P.op("dve", lambda e: e.tensor_tensor(a[:, :n], a[:, :n], q_[:, :n], ALU.add), [a, q_], [a])
        P.op("dve", lambda e: e.tensor_scalar(a[:, :n], a[:, :n], 3.1415925, -3.1415925, ALU.min, ALU.max), [a], [a])
        P.op("act", lambda e: e.activation(h[:, :n], a[:, :n], AF.Sin), [a], [h])
        return h

    for ci, i0 in enumerate(range(0, NI, 512)):
        n = min(512, NI - i0)
        z = zin[ci % 2]
        P.dma(z[:, :n], zT.t[:, i0:i0 + n], [zT], [z])
        w_ = wn[ci % 2]
        P.dma(w_[:, :n], win_d.t[:, i0:i0 + n], [win_d], [w_])
        p1 = nps()
        P.op("pe", lambda e, p1=p1, z=z, n=n: e.matmul(p1[:64, :n], w1[:, :], z[:, :n], start=True, stop=True), [w1, z], [p1])
        h1 = sin_layer(p1, n, 0, 1)
        p2 = nps()
        P.op("pe", lambda e, p2=p2, h1=h1, n=n: e.matmul(p2[:64, :n], w2[:, :], h1[:, :n], start=True, stop=True), [w2, h1], [p2])
        h2 = sin_layer(p2, n, 2, 3)
        nbw = max(0, min(n, (L - 1) - i0))
        for o in range(2):
            p3 = nps()
            if nbw > 0:
                P.op("pe", lambda e, p3=p3, h2=h2, o=o, nbw=nbw: e.matmul(p3[:C, :nbw], w3[:, o, 1, :], h2[:, :nbw], start=True, stop=True), [w3, h2], [p3])
            if nbw < n:
                P.op("pe", lambda e, p3=p3, h2=h2, o=o, nbw=nbw, n=n: e.matmul(p3[:C, nbw:n], w3[:, o, 0, :], h2[:, nbw:n], start=True, stop=True), [w3, h2], [p3])
            hh = hb[fc["b"] % 4]
            fc["b"] += 1
            P.op("dve", lambda e, p3=p3, hh=hh, w_=w_, n=n: e.tensor_tensor(hh[:, :n], p3[:C, :n], w_[:, :n], ALU.mult), [p3, w_], [hh])
            P.dma(Hd[o].t[:, i0:i0 + n], hh[:, :n], [hh], [Hd[o]], key=f"st{hh.name}")

    FW = NB * CHP
    sh = [P.sb([128, NB, CHP], F32, name=f"sh{i}") for i in range(3)]
    vs = P.sb([128, NB, CHP], BF16, name="vs")
    vr = P.sb([128, NB, CHP], BF16, name="vr")
    zz = P.sb([128, NB, CHP], BF16, name="zz")
    xs = P.sb([128, NB, CHP], F32, name="xs")
    tt = [P.sb([128, TW], BF16, name=f"tt{i}") for i in range(3)]
    ep = [P.sb([128, 8, NB], F32, name=f"ep{i}") for i in range(3)]
    tc_ = {"i": 0, "e": 0}

    def short_conv(xi, c0, dst, dst_buf):
        for s in range(3):
            src = hp.t[s:s + L, xi * C + c0: xi * C + c0 + CHP].rearrange("(b j) c -> j b c", j=128)
            P.dma(sh[s][:, :, :], src, [hp], [sh[s]])
            wb = swb[:, xi, s, c0:c0 + CHP].unsqueeze(1).broadcast_to([128, NB, CHP])
            P.op("dve", lambda e, s=s, wb=wb: e.tensor_tensor(sh[s][:, :, :], sh[s][:, :, :], wb, ALU.mult), [sh[s], swb], [sh[s]])
        P.op("dve", lambda e: e.tensor_tensor(sh[0][:, :, :], sh[0][:, :, :], sh[1][:, :, :], ALU.add), [sh[0], sh[1]], [sh[0]])
        P.op("dve", lambda e: e.tensor_tensor(dst, sh[0][:, :, :], sh[2][:, :, :], ALU.add), [sh[0], sh[2]], [dst_buf])

    def reverse_j(src, dst):
        sv = src.t.rearrange("p b c -> p (b c)")
        dv = dst.t.rearrange("p b c -> p (b c)")
        for k, f0 in enumerate(range(0, FW, 512)):
            n = min(512, FW - f0)
            pr = nps()
            P.op("pe", lambda e, pr=pr, f0=f0, n=n: e.matmul(pr[:, :n], jb[:, :], sv[:, f0:f0 + n], start=True, stop=True), [jb, src], [pr])
            if k % 2 == 0:
                P.op("act", lambda e, pr=pr, f0=f0, n=n: e.copy(dv[:, f0:f0 + n], pr[:, :n]), [pr], [dst])
            else:
                P.op("dve", lambda e, pr=pr, f0=f0, n=n: e.tensor_copy(dv[:, f0:f0 + n], pr[:, :n]), [pr], [dst])

    def long_conv(o, c0, urev, epilogue):
        for g in range(CHP // 8):
            pg = nps()
            for c8 in range(8):
                c = g * 8 + c8
                T = tt[tc_["i"] % 3]
                tc_["i"] += 1
                src = bass.AP(Hd_t[o], (c0 + c) * 2 * L, [[1, 128], [1, TW]])
                P.dma(T[:, :], src, [Hd[o]], [T])
                order = [NB - 1] + [e_ for e_ in range(NE) if e_ != NB - 1]
                for k, e_ in enumerate(order):
                    d = e_ - (NB - 1)
                    bt0 = max(0, d)
                    n = NB - abs(d)
                    bs0 = bt0 - d
                    P.op("pe", lambda e, T=T, e_=e_, pg=pg, c8=c8, bt0=bt0, n=n, bs0=bs0, c=c, k=k: e.matmul(
                        pg[:, c8 * NB + bt0: c8 * NB + bt0 + n], T[:, e_ * 128:(e_ + 1) * 128], urev[:, bs0:bs0 + n, c],
                        start=(k == 0), stop=(k == NE - 1), skip_group_check=True), [T, urev], [pg])
            epilogue(g, pg)

    for c0 in range(0, C, CHP):
        short_conv(0, c0, vs[:, :, :], vs)
        short_conv(1, c0, xs[:, :, :], xs)
        reverse_j(vs, vr)

        def epi1(g, pg, c0=c0):
            t = ep[tc_["e"] % 3]
            tc_["e"] += 1
            vv = vs[:, :, g * 8:(g + 1) * 8].rearrange("p b c -> p c b")
            xv = xs[:, :, g * 8:(g + 1) * 8].rearrange("p b c -> p c b")
            zv = zz[:, :, g * 8:(g + 1) * 8].rearrange("p b c -> p c b")
            sk = skipb[:, 0, c0 + g * 8:c0 + (g + 1) * 8].unsqueeze(2).broadcast_to([128, 8, NB])
            pv = pg[:, :8 * NB].rearrange("p (c b) -> p c b", b=NB)
            P.op("dve", lambda e: e.tensor_tensor(t[:, :, :], vv, sk, ALU.mult), [vs, skipb], [t])
            P.op("dve", lambda e: e.tensor_tensor(t[:, :, :], t[:, :, :], pv, ALU.add), [t, pg], [t])
            P.op("dve", lambda e: e.tensor_tensor(zv, t[:, :, :], xv, ALU.mult), [t, xs], [zz])
        long_conv(0, c0, vr, epi1)
        short_conv(2, c0, xs[:, :, :], xs)
        reverse_j(zz, vr)

        def epi2(g, pg, c0=c0):
            t = ep[tc_["e"] % 3]
            tc_["e"] += 1
            zv = zz[:, :, g * 8:(g + 1) * 8].rearrange("p b c -> p c b")
            xv = xs[:, :, g * 8:(g + 1) * 8].rearrange("p b c -> p c b")
            sk = skipb[:, 1, c0 + g * 8:c0 + (g + 1) * 8].unsqueeze(2).broadcast_to([128, 8, NB])
            pv = pg[:, :8 * NB].rearrange("p (c b) -> p c b", b=NB)
            P.op("dve", lambda e: e.tensor_tensor(t[:, :, :], zv, sk, ALU.mult), [zz, skipb], [t])
            P.op("dve", lambda e: e.tensor_tensor(t[:, :, :], t[:, :, :], pv, ALU.add), [t, pg], [t])
            P.op("dve", lambda e: e.tensor_tensor(t[:, :, :], t[:, :, :], xv, ALU.mult), [t, xs], [t])
            ob = Buf("yhout", None)
            P.dma(yh[:, c0 + g * 8:c0 + (g + 1) * 8, :], t[:, :, :], [t], [ob], key=f"st{t.name}")
        long_conv(1, c0, vr, epi2)
    return P.build()


def hyena_consts(L, C, c_lo, hy_width=1024, bands_n=16):
    idx = np.arange(2 * L - 1)
    t = np.abs(idx - (L - 1)).astype(np.float32)
    t_norm = t / np.float32(L)
    bands = np.linspace(1e-4, bands_n - 1, bands_n, dtype=np.float32)
    ang = (np.float32(2.0 * math.pi / L) * t[:, None]) * bands[None, :]
    z = np.concatenate([t_norm[:, None], np.cos(ang), -np.sin(ang)], axis=-1).astype(np.float32)
    deltas = np.abs(np.linspace(math.log(1e-2) / 1.5, math.log(1e-2) / 0.3, hy_width, dtype=np.float32))
    win = np.exp(-t_norm[None, :] * deltas[c_lo:c_lo + C, None]).astype(np.float32)
    jm = np.eye(128, dtype=np.float32)[::-1].copy()
    return np.ascontiguousarray(z.T), win, jm

import numpy as np
import math

GN_EPS = 64e-5
NG = 7


def build_rw(nc, LQ, LC, want_ctx_out, CH=16):
    P = Prog(nc)
    LT = LC + LQ
    segs = [(0, LC), (LC, LQ)]
    rinA = P.dram("rinA", [2, 3, 128, LT + 4], F32, kind="ExternalInput")
    rinB = P.dram("rinB", [2, 4, 128, LT + 4], F32, kind="ExternalInput")
    GSRC = {0: (rinA, 0), 1: (rinA, 1), 4: (rinA, 2), 2: (rinB, 0), 3: (rinB, 1), 5: (rinB, 2), 6: (rinB, 3)}
    swt_d = P.dram("swt", [128, NG, 3], F32, kind="ExternalInput")
    pv_d = P.dram("pv", [128, 12], F32, kind="ExternalInput")
    w2_d = P.dram("w2t", [128, 128], F32, kind="ExternalInput")
    a2_d = P.dram("a2t", [128, 128], F32, kind="ExternalInput")
    g2a_d = P.dram("g2a", [128, 128], F32, kind="ExternalInput")
    g2b_d = P.dram("g2b", [32, 128], F32, kind="ExternalInput")
    cm_d = P.dram("cm", [3, 128, 128], F32, kind="ExternalInput")
    yr = nc.dram_tensor("yr", [128, LQ], F32, kind="ExternalOutput").ap()
    if want_ctx_out:
        yrc = nc.dram_tensor("yrc", [128, LC], F32, kind="ExternalOutput").ap()
    cols = [Buf(f"cols{h}", nc.dram_tensor(f"cols{h}", [4, 128, LT], F32, kind="Internal").ap()) for h in range(2)]
    kv_t = [nc.dram_tensor(f"kv{h}", [2, 2, LT, 64], F32, kind="Internal") for h in range(2)]
    kv = [Buf(f"kv{h}", kv_t[h].ap()) for h in range(2)]
    yT = [Buf(f"yT{h}", nc.dram_tensor(f"yT{h}", [128, LT], F32, kind="Internal").ap()) for h in range(2)]

    ps = [P.ps([128, 512], F32, name=f"psb{i}") for i in range(8)]
    pc = {"i": 0}

    def nps():
        b = ps[pc["i"] % 8]
        pc["i"] += 1
        return b

    swt = P.sb([128, NG, 3], F32, name="swt_s")
    pv = P.sb([128, 16], F32, name="pv_s")
    w2 = P.sb([128, 128], F32, name="w2_s")
    a2 = P.sb([128, 128], F32, name="a2_s")
    g2a = P.sb([128, 128], F32, name="g2a_s")
    g2b = P.sb([32, 128], F32, name="g2b_s")
    cm = P.sb([128, 3, 128], F32, name="cm_s")
    P.dma(swt[:], swt_d.t, [swt_d], [swt])
    P.dma(pv[:, 0:12], pv_d.t, [pv_d], [pv])
    P.dma(w2[:], w2_d.t, [w2_d], [w2])
    P.dma(a2[:], a2_d.t, [a2_d], [a2])
    P.dma(g2a[:], g2a_d.t, [g2a_d], [g2a])
    P.dma(g2b[:], g2b_d.t, [g2b_d], [g2b])
    P.dma(cm[:], cm_d.t.rearrange("a p n -> p a n"), [cm_d], [cm])
    ident, jm, bones = cm[:, 0, :], cm[:, 1, :], cm[:, 2, :]
    bonus = P.sb([128, LT], F32, name="bonus")
    gout = P.sb([128, LT], F32, name="gout")

    NW = 26
    xin = [P.sb([128, 516], F32, name=f"xin{i}") for i in range(NG + 2)]
    wkb = [P.sb([128, 512], F32, name=f"wk{i}") for i in range(NW)]
    stg = [P.sb([128, 2, 128], F32, name=f"stg{i}") for i in range(2)]
    cnt = {"x": 0, "w": 0, "s": 0}

    def nw():
        b = wkb[cnt["w"] % NW]
        cnt["w"] += 1
        return b

    def conv(d, g, col0, n):
        x = xin[cnt["x"] % len(xin)]
        cnt["x"] += 1
        rsrc, gi = GSRC[g]
        P.dma(x[:, :n + 2], rsrc.t[d, gi, :, col0:col0 + n + 2], [rsrc], [x])
        u = nw()
        taps = (0, 1, 2) if d == 0 else (2, 1, 0)
        P.op("act", lambda e: e.activation(u[:, :n], x[:, 0:n], AF.Identity, scale=swt[:, g, taps[0]:taps[0] + 1]), [x, swt], [u])
        P.op("dve", lambda e: e.scalar_tensor_tensor(u[:, :n], x[:, 1:n + 1], swt[:, g, taps[1]:taps[1] + 1], u[:, :n], ALU.mult, ALU.add), [x, swt, u], [u])
        P.op("dve", lambda e: e.scalar_tensor_tensor(u[:, :n], x[:, 2:n + 2], swt[:, g, taps[2]:taps[2] + 1], u[:, :n], ALU.mult, ALU.add), [x, swt, u], [u])
        return u

    def lowrank(u, wt, d, n, bias_col):
        p_ = nps()
        lo = d * 64
        P.op("pe", lambda e: e.matmul(p_[:, :n], wt[lo:lo + 64, :], u[lo:lo + 64, :n], start=True, stop=True), [wt, u], [p_])
        o = nw()
        P.op("act", lambda e: e.activation(o[:, :n], p_[:, :n], AF.Sigmoid, bias=pv[:, bias_col:bias_col + 1], scale=1.0), [p_, pv], [o])
        return o

    for d in range(2):
        for (s0, sl) in segs:
            for c0 in range(0, sl, 512):
                n = min(512, sl - c0)
                n0 = s0 + c0
                seg_i = 0 if s0 == 0 else 1
                pad0 = s0 + 2 * seg_i + c0
                uk = conv(d, 0, pad0, n)
                uv = conv(d, 1, pad0, n)
                uwd = conv(d, 2, pad0, n)
                uad = conv(d, 3, pad0, n)
                ur = conv(d, 4, pad0, n)
                P.op("act", lambda e, uwd=uwd, n=n: e.activation(uwd[:, :n], uwd[:, :n], AF.Tanh), [uwd], [uwd])
                dirs = (0, 1) if d == 0 else (1,)
                a_ = {}
                dec = None
                for dd in dirs:
                    a_[dd] = lowrank(uad, a2, dd, n, 2 + dd)
                sg = lowrank(uwd, w2, d, n, d)
                dec = nw()
                P.op("act", lambda e, dec=dec, sg=sg, n=n: e.activation(dec[:, :n], sg[:, :n], AF.Exp, scale=-math.exp(-0.5)), [sg], [dec])
                kkr = nw()
                P.op("dve", lambda e, kkr=kkr, uk=uk, n=n: e.tensor_scalar(kkr[:, :n], uk[:, :n], pv[:, 4:5], None, ALU.mult), [uk, pv], [kkr])
                sq = nw()
                P.op("act", lambda e, sq=sq, kkr=kkr, n=n: e.activation(sq[:, :n], kkr[:, :n], AF.Square), [kkr], [sq])
                pn = nps()
                P.op("pe", lambda e, pn=pn, sq=sq, n=n: e.matmul(pn[:, :n], bones, sq[:, :n], start=True, stop=True), [cm, sq], [pn])
                P.op("act", lambda e, pn=pn, sq=sq, n=n: e.activation(sq[:, :n], pn[:, :n], AF.Sqrt), [pn], [sq])
                P.op("dve", lambda e, sq=sq, n=n: e.tensor_scalar(sq[:, :n], sq[:, :n], 1e-12, None, ALU.max), [sq], [sq])
                P.op("dve", lambda e, sq=sq, n=n: e.reciprocal(sq[:, :n], sq[:, :n]), [sq], [sq])
                nkk = nw()
                P.op("dve", lambda e, nkk=nkk, kkr=kkr, sq=sq, n=n: e.scalar_tensor_tensor(nkk[:, :n], kkr[:, :n], -1.0, sq[:, :n], ALU.mult, ALU.mult), [kkr, sq], [nkk])
                bb = nw()
                P.op("dve", lambda e, bb=bb, nkk=nkk, ad=a_[d], n=n: e.scalar_tensor_tensor(bb[:, :n], nkk[:, :n], -1.0, ad[:, :n], ALU.mult, ALU.mult), [nkk, a_[d]], [bb])
                kd = {}
                for dd in dirs:
                    t_ = nw()
                    P.op("dve", lambda e, t_=t_, ad=a_[dd], n=n: e.tensor_scalar(t_[:, :n], ad[:, :n], -1.0, pv[:, 5:6], ALU.add, ALU.mult), [a_[dd], pv], [t_])
                    P.op("dve", lambda e, t_=t_, uk=uk, n=n: e.scalar_tensor_tensor(t_[:, :n], t_[:, :n], 1.0, uk[:, :n], ALU.add, ALU.mult), [t_, uk], [t_])
                    kd[dd] = t_
                if d == 0:
                    ks = nw()
                    P.op("dve", lambda e, ks=ks, kd=kd, n=n: e.tensor_tensor(ks[:, :n], kd[0][:, :n], kd[1][:, :n], ALU.add), [kd[0], kd[1]], [ks])
                    P.op("dve", lambda e, ks=ks, ur=ur, n=n: e.scalar_tensor_tensor(ks[:, :n], ks[:, :n], pv[:, 6:7], ur[:, :n], ALU.mult, ALU.mult), [ks, pv, ur], [ks])
                    pb = nps()
                    P.op("pe", lambda e, pb=pb, ks=ks, n=n: e.matmul(pb[:, :n], bones, ks[:, :n], start=True, stop=True), [cm, ks], [pb])
                    P.op("dve", lambda e, pb=pb, uv=uv, n=n, n0=n0: e.tensor_tensor(bonus[:, n0:n0 + n], pb[:, :n], uv[:, :n], ALU.mult), [pb, uv], [bonus])
                    ug = conv(d, 5, pad0, n)
                    ug2 = conv(d, 6, pad0, n)
                    P.op("act", lambda e, ug=ug, n=n: e.activation(ug[:, :n], ug[:, :n], AF.Sigmoid), [ug], [ug])
                    P.op("act", lambda e, ug2=ug2, n=n: e.activation(ug2[:32, :n], ug2[:32, :n], AF.Sigmoid), [ug2], [ug2])
                    pg_ = nps()
                    P.op("pe", lambda e, pg_=pg_, ug=ug, n=n: e.matmul(pg_[:, :n], g2a[:, :], ug[:, :n], start=True, stop=False), [g2a, ug], [pg_])
                    P.op("pe", lambda e, pg_=pg_, ug2=ug2, n=n: e.matmul(pg_[:, :n], g2b[:, :], ug2[:32, :n], start=False, stop=True), [g2b, ug2], [pg_])
                    P.op("act", lambda e, pg_=pg_, n=n, n0=n0: e.copy(gout[:, n0:n0 + n], pg_[:, :n]), [pg_], [gout])
                for h in range(2):
                    for qi, sb_ in enumerate([dec, nkk, bb, ur]):
                        P.dma(cols[h].t[qi, d * 64:(d + 1) * 64, n0:n0 + n], sb_[h * 64:(h + 1) * 64, :n], [sb_], [cols[h]], key=f"cst{h}")
                for b0 in range(0, n, 128):
                    st = stg[cnt["s"] % 2]
                    cnt["s"] += 1
                    pt_ = nps()
                    for qi, sb_ in enumerate([kd[d], uv]):
                        P.op("pe", lambda e, pt_=pt_, sb_=sb_, b0=b0, qi=qi: e.matmul(pt_[:, qi * 128:(qi + 1) * 128], sb_[:, b0:b0 + 128], ident, start=True, stop=True),
                             [sb_, cm], [pt_])
                    P.op("act", lambda e, pt_=pt_, st=st: e.copy(st[:, 0:2, :], pt_[:, 0:256].rearrange("p (q c) -> p q c", c=128)), [pt_], [st])
                    for h in range(2):
                        dst = kv_t[h].ap()[:, d, n0 + b0:n0 + b0 + 128, :].rearrange("q t k -> t q k")
                        P.dma(dst, st[:, 0:2, h * 64:(h + 1) * 64], [st], [kv[h]], key=f"kst{h}")

    ST = [[P.sb([128, 64], F32, name=f"ST{h}{i}") for i in range(2)] for h in range(2)]
    Tm = [P.sb([128, 64], F32, name=f"Tm{h}") for h in range(2)]
    CB = [[P.sb([128, 4, CH], F32, name=f"CB{h}{i}") for i in range(2)] for h in range(2)]
    AR = [[P.sb([128, CH, 64], F32, name=f"AR{h}{i}") for i in range(2)] for h in range(2)]
    KV = [[P.sb([128, 2, CH, 64], F32, name=f"KV{h}{i}") for i in range(2)] for h in range(2)]
    YC = [[P.sb([64, 2, CH], F32, name=f"YC{h}{i}") for i in range(2)] for h in range(2)]
    psab = [[Buf(f"psab{h}{i}", ps[h].t[:, i * 64:(i + 1) * 64]) for i in range(2)] for h in range(2)]
    pvk = [[Buf(f"pvk{h}{i}", ps[2 + h].t[:, i * 64:(i + 1) * 64]) for i in range(2)] for h in range(2)]
    py = [[Buf(f"py{h}{i}", ps[4 + h].t[0:64, i * 2 * CH:(i + 1) * 2 * CH].rearrange("p (d c) -> p d c", d=2)) for i in range(2)] for h in range(2)]
    for h in range(2):
        P.op("dve", lambda e, h=h: e.memset(ST[h][0][:, :], 0.0), [ps[h], ps[2 + h], ps[4 + h]], [ST[h][0]])

    def flush(h, pci):
        yc = YC[h][pci % 2]
        P.op("act", lambda e, yc=yc, yp=py[h][pci % 2]: e.copy(yc[:, :, :], yp[:, :, :]), [py[h][pci % 2]], [yc])
        for d in range(2):
            P.dma(yT[h].t[d * 64:(d + 1) * 64, pci * CH:(pci + 1) * CH], yc[:, d, :], [yc], [yT[h]], key=f"yst{h}{pci % 2}")

    nchunk = LT // CH
    step = 0
    for ci in range(nchunk):
        n0 = ci * CH
        for h in range(2):
            cb = CB[h][ci % 2]; ar = AR[h][ci % 2]; kvb = KV[h][ci % 2]
            P.dma(cb[:, :, :], cols[h].t[:, :, n0:n0 + CH].rearrange("q p t -> p q t"), [cols[h]], [cb], key=f"cb{h}{ci % 2}")
            for d in range(2):
                P.dma(kvb[d * 64:d * 64 + 1, :, :, :].rearrange("p q t k -> p q (t k)"),
                      kv_t[h].ap()[:, d:d + 1, n0:n0 + CH, :].rearrange("q o t k -> o q (t k)"), [kv[h]], [kvb], key=f"kv{h}{ci % 2}")
            P.op("act", lambda e, ar=ar, cb=cb: e.activation(ar[:, :, :], cb[:, 1, :].unsqueeze(2).broadcast_to([128, CH, 64]), AF.Identity), [cb], [ar])
        for i in range(CH):
            for h in range(2):
                cb = CB[h][ci % 2]; ar = AR[h][ci % 2]; kvb = KV[h][ci % 2]
                so = ST[h][step % 2]; sn = ST[h][(step + 1) % 2]; tm = Tm[h]
                sab = psab[h][step % 2]; vk = pvk[h][step % 2]
                for d in range(2):
                    lo = d * 64
                    P.op("pe", lambda e, ar=ar, so=so, sab=sab, lo=lo, i=i: e.matmul(sab[lo:lo + 64, :], ar[lo:lo + 64, i, :], so[lo:lo + 64, :], start=True, stop=True), [ar, so], [sab])
                for d in range(2):
                    lo = d * 64
                    P.op("pe", lambda e, kvb=kvb, vk=vk, lo=lo, i=i: e.matmul(vk[lo:lo + 64, :], kvb[lo:lo + 1, 0, i, :], kvb[lo:lo + 1, 1, i, :], start=True, stop=True), [kvb], [vk])
                if step > 0:
                    pi = (i - 1) % CH
                    pci = ci if i > 0 else ci - 1
                    ypp = py[h][pci % 2]; cbp = CB[h][pci % 2]
                    for d in range(2):
                        lo = d * 64
                        P.op("pe", lambda e, so=so, cbp=cbp, ypp=ypp, lo=lo, d=d, pi=pi: e.matmul(ypp[:, d, pi:pi + 1], so[lo:lo + 64, :], cbp[lo:lo + 64, 3, pi:pi + 1], start=True, stop=True), [so, cbp], [ypp])
                    if i == 0:
                        flush(h, pci)
                P.op("dve", lambda e, tm=tm, so=so, cb=cb, vk=vk, i=i: e.scalar_tensor_tensor(tm[:, :], so[:, :], cb[:, 0, i:i + 1], vk[:, :], ALU.mult, ALU.add), [so, cb, vk], [tm])
                P.op("dve", lambda e, tm=tm, sn=sn, cb=cb, sab=sab, i=i: e.scalar_tensor_tensor(sn[:, :], sab[:, :], cb[:, 2, i:i + 1], tm[:, :], ALU.mult, ALU.add), [sab, cb, tm], [sn])
            step += 1
    lastc = nchunk - 1
    for h in range(2):
        so = ST[h][step % 2]; cbp = CB[h][lastc % 2]; ypp = py[h][lastc % 2]
        for d in range(2):
            lo = d * 64
            P.op("pe", lambda e, so=so, cbp=cbp, ypp=ypp, lo=lo, d=d: e.matmul(ypp[:, d, CH - 1:CH], so[lo:lo + 64, :], cbp[lo:lo + 64, 3, CH - 1:CH], start=True, stop=True), [so, cbp], [ypp])
        flush(h, lastc)

    yf = [P.sb([128, 128], F32, name=f"yf{i}") for i in range(2)]
    yb = [P.sb([128, 128], F32, name=f"yb{i}") for i in range(2)]
    yt = [P.sb([128, 128], F32, name=f"yt{i}") for i in range(2)]
    w3 = [P.sb([128, 128], F32, name=f"w3_{i}") for i in range(6)]
    k3 = {"i": 0, "w": 0}

    def n3():
        b = w3[k3["w"] % 6]
        k3["w"] += 1
        return b

    for (s0, sl) in segs:
        if s0 == 0 and not want_ctx_out:
            continue
        nb = sl // 128
        for b in range(nb):
            t0 = s0 + b * 128
            tb = s0 + (nb - 1 - b) * 128
            f_, b_, t_ = yf[k3["i"] % 2], yb[k3["i"] % 2], yt[k3["i"] % 2]
            k3["i"] += 1
            for h in range(2):
                P.dma(f_[h * 64:(h + 1) * 64, :], yT[h].t[0:64, t0:t0 + 128], [yT[h]], [f_], key=f"yf{k3['i'] % 2}")
                P.dma(b_[h * 64:(h + 1) * 64, :], yT[h].t[64:128, tb:tb + 128], [yT[h]], [b_], key=f"yb{k3['i'] % 2}")
            p1 = nps()
            P.op("pe", lambda e, p1=p1, b_=b_: e.matmul(p1[:, :128], b_[:, :], ident, start=True, stop=True), [b_, cm], [p1])
            P.op("act", lambda e, p1=p1, t_=t_: e.copy(t_[:, :], p1[:, :128]), [p1], [t_])
            p2 = nps()
            P.op("pe", lambda e, p2=p2, t_=t_: e.matmul(p2[:, :128], t_[:, :], jm, start=True, stop=True), [t_, cm], [p2])
            y = n3()
            P.op("dve", lambda e, y=y, p2=p2, f_=f_: e.tensor_tensor(y[:, :], p2[:, :128], f_[:, :], ALU.add), [p2, f_], [y])
            pm = nps()
            P.op("pe", lambda e, pm=pm, y=y: e.matmul(pm[:, :128], bones, y[:, :], start=True, stop=True), [cm, y], [pm])
            yc_ = n3()
            P.op("dve", lambda e, yc_=yc_, pm=pm, y=y: e.scalar_tensor_tensor(yc_[:, :], pm[:, :128], -1.0 / 64, y[:, :], ALU.mult, ALU.add), [pm, y], [yc_])
            sq = n3()
            P.op("act", lambda e, sq=sq, yc_=yc_: e.activation(sq[:, :], yc_[:, :], AF.Square), [yc_], [sq])
            pv_ = nps()
            P.op("pe", lambda e, pv_=pv_, sq=sq: e.matmul(pv_[:, :128], bones, sq[:, :], start=True, stop=True), [cm, sq], [pv_])
            P.op("dve", lambda e, pv_=pv_, sq=sq: e.tensor_scalar(sq[:, :], pv_[:, :128], 1.0 / 64, GN_EPS, ALU.mult, ALU.add), [pv_], [sq])
            P.op("act", lambda e, sq=sq: e.activation(sq[:, :], sq[:, :], AF.Sqrt), [sq], [sq])
            P.op("dve", lambda e, sq=sq: e.reciprocal(sq[:, :], sq[:, :]), [sq], [sq])
            P.op("dve", lambda e, yc_=yc_, sq=sq: e.scalar_tensor_tensor(yc_[:, :], yc_[:, :], pv[:, 7:8], sq[:, :], ALU.mult, ALU.mult), [yc_, pv, sq], [yc_])
            P.op("dve", lambda e, yc_=yc_, t0=t0: e.scalar_tensor_tensor(yc_[:, :], yc_[:, :], pv[:, 8:9], bonus[:, t0:t0 + 128], ALU.add, ALU.add), [yc_, pv, bonus], [yc_])
            o_ = n3()
            P.op("dve", lambda e, o_=o_, yc_=yc_, t0=t0: e.tensor_tensor(o_[:, :], yc_[:, :], gout[:, t0:t0 + 128], ALU.mult), [yc_, gout], [o_])
            dst = yr[:, t0 - LC:t0 - LC + 128] if s0 > 0 else yrc[:, t0:t0 + 128]
            P.dma(dst, o_[:, :], [o_], [Buf("yrout", None)], key=f"st{o_.name}")
    return P.build()


def build_M(nc, D, NCOL, NLAY):
    P = Prog(nc)
    DC = D // 128
    cc = P.dram("cc", [128, DC, 2], F32, kind="ExternalInput")
    wm = P.dram("wm", [NLAY, D, NCOL], F32, kind="ExternalInput")
    bm = P.dram("bm", [NLAY, 2, NCOL], F32, kind="ExternalInput")
    mo = nc.dram_tensor("mo", [NLAY, 2, NCOL], F32, kind="ExternalOutput").ap()
    cs = P.sb([128, DC, 2], F32, name="cs")
    P.dma(cs[:], cc.t, [cc], [cs])
    P.op("act", lambda e: e.activation(cs[:], cs[:], AF.Silu), [cs], [cs])
    wt = [P.sb([128, DC, 256], F32, name=f"wt{i}") for i in range(4)]
    ps = [P.ps([128, 512], F32, name=f"ps{i}") for i in range(4)]
    ob = [P.sb([2, 256], F32, name=f"ob{i}") for i in range(3)]
    bb = [P.sb([2, 256], F32, name=f"bb{i}") for i in range(3)]
    i = 0
    for l in range(NLAY):
        for c0 in range(0, NCOL, 256):
            n = min(256, NCOL - c0)
            w = wt[i % 4]; p_ = ps[i % 4]; o = ob[i % 3]; b = bb[i % 3]
            i += 1
            P.dma(w[:, :, :n], wm.t[l, :, c0:c0 + n].rearrange("(c p) n -> p c n", p=128), [wm], [w])
            P.dma(b[:, :n], bm.t[l, :, c0:c0 + n], [bm], [b])
            for k in range(DC):
                P.op("pe", lambda e, w=w, p_=p_, k=k, n=n: e.matmul(p_[:2, :n], cs[:, k, :], w[:, k, :n], start=(k == 0), stop=(k == DC - 1)), [cs, w], [p_])
            P.op("dve", lambda e, o=o, p_=p_, b=b, n=n: e.tensor_tensor(o[:, :n], p_[:2, :n], b[:, :n], ALU.add), [p_, b], [o])
            P.dma(mo[l, :, c0:c0 + n], o[:, :n], [o], [Buf("moout", None)], key=f"st{o.name}")
    return P.build()

from concourse.bass_utils import run_bass_kernel_spmd

NCORES = 8
D_MODEL = 4096
SEQ = 8192
CTX_LEN = 256
D_FF = 5632
DEPTH = 2
HY_W = 1024
DA_W = 2048
RW_W = 1024
O_HY, O_Q, O_K, O_V, O_RW = 0, 3072, 5120, 7168, 9216
RW_STATE = 2304
O_RW_RG = O_RW + RW_STATE
O_GATE = 12704
TLAT = SEQ // NCORES
TCTX = CTX_LEN // NCORES
TT = TLAT + TCTX
DC = D_MODEL // 128


def _launch(build, in_maps):
    nc = bass.Bass("TRN2", target_bir_lowering=False)
    build(nc)
    res = run_bass_kernel_spmd(nc, in_maps, core_ids=list(range(NCORES)))
    return res.results


def _pc(vec):
    return np.asarray(vec, np.float32).reshape(DC, 128).T


def _vecs(rows):
    return np.ascontiguousarray(np.stack([_pc(r) for r in rows], axis=1))


def kernel(x, c, ctx, c_ctx, w_mod, b_mod, norm_gain, w_ff_in, w_ff_out, w_in,
           hy_short, hy_w1, hy_b1, hy_f1, hy_w2, hy_b2, hy_f2, hy_w3, hy_skip,
           da_lambda, da_subln,
           rw_shift, rw_w0, rw_w2, rw_a0, rw_a2, rw_g2, rw_k_k, rw_k_a, rw_r_k, rw_gn_w, rw_gn_b,
           w_branch, w_out, final_gain):
    f32 = np.float32
    A = lambda a: np.asarray(a, f32)
    x, c, ctx, c_ctx = A(x), A(c), A(ctx), A(c_ctx)
    w_in = A(w_in)
    NCOL = 9 * D_MODEL // NCORES
    cc = np.ascontiguousarray(np.stack([c[0], c_ctx], 0).reshape(2, DC, 128).transpose(2, 1, 0))
    w_mod = A(w_mod); b_mod = A(b_mod)
    ims = []
    for i in range(NCORES):
        sl = slice(i * NCOL, (i + 1) * NCOL)
        ims.append({"cc": cc, "wm": np.ascontiguousarray(w_mod[:, :, sl]),
                    "bm": np.ascontiguousarray(np.broadcast_to(b_mod[:, None, sl], (DEPTH, 2, NCOL)))})
    res = _launch(lambda nc: build_M(nc, D_MODEL, NCOL, DEPTH), ims)
    mo = np.concatenate([r["mo"] for r in res], axis=2)
    del ims
    mod = mo[:, 0].reshape(DEPTH, 9, D_MODEL)
    modc = mo[:, 1].reshape(DEPTH, 9, D_MODEL)

    xT = [np.ascontiguousarray(np.concatenate([x[0, i * TLAT:(i + 1) * TLAT], ctx[0, i * TCTX:(i + 1) * TCTX]], 0).T) for i in range(NCORES)]
    cs_tab, rt = rope_tables(SEQ)
    bones = np.zeros((128, 128), f32); bones[:64, :64] = 1; bones[64:, 64:] = 1
    cm = np.stack([np.eye(128, dtype=f32), np.eye(128, dtype=f32)[::-1].copy(), bones])
    LT = CTX_LEN + SEQ

    for l in range(DEPTH):
        last = l == DEPTH - 1
        lam_init = 0.8 - 0.6 * math.exp(-0.3 * l)
        ng = A(norm_gain[l])
        vA = _vecs([ng[0], ng[1], mod[l, 0], mod[l, 1], mod[l, 2], mod[l, 3], mod[l, 4],
                    modc[l, 0], modc[l, 1], modc[l, 2], modc[l, 3], modc[l, 4]])
        wfi, wfo = A(w_ff_in[l, 0]), A(w_ff_out[l, 0])
        win = np.ascontiguousarray(w_in[l][:, :O_GATE])
        ims = [{"xT": xT[i], "vecs": vA, "wfi": wfi, "wfo": wfo, "win": win} for i in range(NCORES)]
        res = _launch(lambda nc: build_A(nc, D_MODEL, D_FF, O_GATE, TT, TLAT), ims)
        del ims, win
        x1T = [r["x1T"] for r in res]
        p_lat = np.concatenate([r["pT"][:, :TLAT].T for r in res], 0)
        p_ctx = np.concatenate([r["pT"][:, TLAT:].T for r in res], 0)
        del res

        hs = A(hy_short[l]); w3 = A(hy_w3[l]).reshape(64, 2, 2, HY_W); sk = A(hy_skip[l])
        v1 = np.ascontiguousarray(np.stack([A(hy_b1[l]), A(hy_f1[l]), A(hy_b2[l]), A(hy_f2[l])], 1))

        def hy_run(pp, L):
            ims = []
            for i in range(NCORES):
                c_lo = i * 128
                cols = np.concatenate([np.arange(c_lo, c_lo + 128) + k * HY_W for k in range(3)])
                hp = np.zeros((L + 2, 384), f32); hp[1:L + 1] = pp[:, O_HY + cols]
                zT, win_c, jm = hyena_consts(L, 128, c_lo)
                swb = np.ascontiguousarray(np.broadcast_to(hs[:, cols].reshape(3, 3, 128).transpose(1, 0, 2)[None], (128, 3, 3, 128)))
                skipb = np.ascontiguousarray(np.broadcast_to(sk[:, c_lo:c_lo + 128][None], (128, 2, 128)))
                ims.append({"hp": hp, "swb": swb, "skipb": skipb, "zT": zT, "w1": A(hy_w1[l]), "v1": v1, "w2": A(hy_w2[l]),
                            "w3c": np.ascontiguousarray(w3[:, :, :, c_lo:c_lo + 128]), "win": win_c, "jm": jm})
            res = _launch(lambda nc: build_hy(nc, L, 128, 32 if L > 1024 else 64), ims)
            return np.concatenate([r["yh"].transpose(2, 0, 1).reshape(L, 128) for r in res], 1)
        y_h = hy_run(p_lat, SEQ)
        yc_h = hy_run(p_ctx, CTX_LEN) if not last else None

        ims = []
        for i in range(NCORES):
            hc = slice(i * 256, (i + 1) * 256)

            def fm(pp, off):
                return np.ascontiguousarray(pp[:, off:off + DA_W][:, hc].reshape(-1, 2, 2, 64).transpose(1, 2, 3, 0))
            vv = np.concatenate([p_lat[:, O_V:O_V + DA_W][:, hc], p_ctx[:, O_V:O_V + DA_W][:, hc]], 0)
            im = {"q": fm(p_lat, O_Q), "k": fm(p_lat, O_K), "kc": fm(p_ctx, O_K),
                  "v": np.ascontiguousarray(vv.reshape(LT, 2, 128).transpose(1, 0, 2)), "cs": cs_tab, "rt": rt,
                  "lamv": np.ascontiguousarray(A(da_lambda[l]).reshape(1, 256)), "subln": np.ascontiguousarray(A(da_subln[l]).reshape(128, 1))}
            if not last:
                im["qc"] = fm(p_ctx, O_Q)
            ims.append(im)
        res = _launch(lambda nc: build_at(nc, 2, SEQ, CTX_LEN, lam_init, not last), ims)
        del ims
        y_a = np.concatenate([r["ya"].transpose(2, 0, 1).reshape(SEQ, 256) for r in res], 1)
        yc_a = np.concatenate([r["yac"].transpose(2, 0, 1).reshape(CTX_LEN, 256) for r in res], 1) if not last else None
        del res

        rs = A(rw_shift[l]); w0 = A(rw_w0[l]); w2 = A(rw_w2[l]); a0 = A(rw_a0[l]); a2 = A(rw_a2[l]); g2 = A(rw_g2[l])

        def padded(rows_c, rows_l):
            n = rows_c.shape[0]
            o = np.zeros((2, n, 128, LT + 4), f32)
            o[0, :, :, 1:1 + CTX_LEN] = rows_c; o[0, :, :, CTX_LEN + 3:CTX_LEN + 3 + SEQ] = rows_l
            o[1, :, :, 1:1 + CTX_LEN] = rows_c[:, :, ::-1]; o[1, :, :, CTX_LEN + 3:CTX_LEN + 3 + SEQ] = rows_l[:, :, ::-1]
            return o

        def grpB(pp):
            r = pp[:, O_RW:O_GATE]
            gd2 = np.zeros((pp.shape[0], 128), f32); gd2[:, :32] = r[:, RW_STATE + RW_W + 128:]
            return np.stack([r[:, 2 * RW_W:2 * RW_W + 128].T, r[:, 2 * RW_W + 128:2 * RW_W + 256].T, r[:, RW_STATE + RW_W:RW_STATE + RW_W + 128].T, gd2.T])
        rinB = padded(grpB(p_ctx), grpB(p_lat))
        ims = []
        for i in range(NCORES):
            cs_ = slice(i * 128, (i + 1) * 128)

            def grpA(pp):
                r = pp[:, O_RW:O_GATE]
                return np.stack([r[:, 0:RW_W][:, cs_].T, r[:, RW_W:2 * RW_W][:, cs_].T, r[:, RW_STATE:RW_STATE + RW_W][:, cs_].T])
            sw = lambda cols: rs[:, cols].T
            swt = np.zeros((128, 7, 3), f32)
            swt[:, 0] = sw(np.arange(0, RW_W)[cs_]); swt[:, 1] = sw(np.arange(RW_W, 2 * RW_W)[cs_])
            swt[:, 2] = sw(np.arange(2 * RW_W, 2 * RW_W + 128)); swt[:, 3] = sw(np.arange(2 * RW_W + 128, 2 * RW_W + 256))
            swt[:, 4] = sw(np.arange(RW_STATE, RW_STATE + RW_W)[cs_]); swt[:, 5] = sw(np.arange(RW_STATE + RW_W, RW_STATE + RW_W + 128))
            swt[:32, 6] = sw(np.arange(RW_STATE + RW_W + 128, RW_STATE + RW_W + 160))
            pvv = np.zeros((128, 12), f32)
            pvv[:, 0] = w0[0, cs_]; pvv[:, 1] = w0[1, cs_]; pvv[:, 2] = a0[0, cs_]; pvv[:, 3] = a0[1, cs_]
            pvv[:, 4] = A(rw_k_k[l])[cs_]; pvv[:, 5] = A(rw_k_a[l])[cs_]; pvv[:, 6] = A(rw_r_k[l]).reshape(-1)[cs_]
            pvv[:, 7] = A(rw_gn_w[l])[cs_]; pvv[:, 8] = A(rw_gn_b[l])[cs_]
            ims.append({"rinA": padded(grpA(p_ctx), grpA(p_lat)), "rinB": rinB, "swt": swt, "pv": pvv,
                        "w2t": np.ascontiguousarray(np.concatenate([w2[0][:, cs_], w2[1][:, cs_]], 0)),
                        "a2t": np.ascontiguousarray(np.concatenate([a2[0][:, cs_], a2[1][:, cs_]], 0)),
                        "g2a": np.ascontiguousarray(g2[:128, cs_]), "g2b": np.ascontiguousarray(g2[128:, cs_]), "cm": cm})
        res = _launch(lambda nc: build_rw(nc, SEQ, CTX_LEN, not last), ims)
        del ims, rinB
        y_r = np.concatenate([r["yr"].T for r in res], 1)
        yc_r = np.concatenate([r["yrc"].T for r in res], 1) if not last else None
        del res, p_lat, p_ctx

        y_lat = np.concatenate([y_h, y_a, y_r], 1)
        y_ctx = np.concatenate([yc_h, yc_a, yc_r], 1) if not last else np.zeros((CTX_LEN, 4096), f32)
        vC = _vecs([ng[1], ng[2], mod[l, 3], mod[l, 4], mod[l, 5], mod[l, 6], mod[l, 7], mod[l, 8],
                    modc[l, 3], modc[l, 4], modc[l, 5], modc[l, 6], modc[l, 7], modc[l, 8], A(final_gain)])
        wg = np.ascontiguousarray(w_in[l][:, O_GATE:])
        ims = []
        for i in range(NCORES):
            yT = np.ascontiguousarray(np.concatenate([y_lat[i * TLAT:(i + 1) * TLAT], y_ctx[i * TCTX:(i + 1) * TCTX]], 0).T)
            ims.append({"x1T": x1T[i], "yT": yT, "vecs": vC, "wg": wg, "wbr": A(w_branch[l]), "wo": A(w_out[l]),
                        "wfi": A(w_ff_in[l, 1]), "wfo": A(w_ff_out[l, 1])})
        res = _launch(lambda nc: build_C(nc, D_MODEL, D_FF, TT, TLAT, HY_W, DA_W, RW_W, last), ims)
        del ims, wg
        xT = [r["x3T"] for r in res]
        del res
    out = np.concatenate([t[:, :TLAT].T for t in xT], 0)[None]
    return np.ascontiguousarray(out.astype(np.float32))
```

```python
import math
import numpy as np

from contextlib import ExitStack
import numpy as np
import concourse.bass as bass
import concourse.mybir as mybir

F32 = mybir.dt.float32
BF16 = mybir.dt.bfloat16
AF = mybir.ActivationFunctionType
ALU = mybir.AluOpType
AX = mybir.AxisListType


class Buf:
    __slots__ = ("name", "t", "lw", "rd")

    def __init__(self, name, t=None):
        self.name = name
        self.t = t
        self.lw = None
        self.rd = []

    def __getitem__(self, k):
        return self.t[k]


class Op:
    __slots__ = ("eng", "fn", "dma", "deps", "sig", "sem", "val", "waits", "idx")

    def __init__(self, eng, fn, dma):
        self.eng, self.fn, self.dma = eng, fn, dma
        self.deps = []
        self.sig = False
        self.sem = None
        self.val = 0
        self.waits = []


class Prog:
    ENGS = ("pe", "dve", "act", "pool", "sp")

    def __init__(self, nc, pfx=""):
        self.nc = nc
        self.pfx = pfx
        self.es = ExitStack()
        self.ops = []
        self.dma_keys = {}
        self.nbuf = 0
        self.bound = {}

    def sb(self, shape, dt=F32, name=None):
        self.nbuf += 1
        name = self.pfx + (name or f"sb{self.nbuf}")
        t = self.es.enter_context(self.nc.sbuf_tensor(name, list(shape), dt))
        return Buf(name, t)

    def ps(self, shape, dt=F32, name=None):
        self.nbuf += 1
        name = self.pfx + (name or f"ps{self.nbuf}")
        t = self.es.enter_context(self.nc.psum_tensor(name, list(shape), dt))
        return Buf(name, t)

    def dram(self, name, shape, dt=F32, kind="Internal"):
        if name in self.bound:
            return self.bound[name]
        t = self.nc.dram_tensor(self.pfx + name, list(shape), dt, kind=kind)
        return Buf(self.pfx + name, t.ap())

    def dt(self, name, shape, dt=F32, kind="Internal"):
        if name in self.bound:
            return self.bound[name]
        return self.nc.dram_tensor(self.pfx + name, list(shape), dt, kind=kind)

    def _rec(self, eng, fn, reads, writes, dma=False, key=None):
        op = Op(eng, fn, dma)
        op.idx = len(self.ops)
        if dma:
            op.sem = key if key is not None else writes[0].name
        for b in reads:
            if b.lw is not None:
                op.deps.append((b.lw, "raw"))
        for b in writes:
            if b.lw is not None:
                op.deps.append((b.lw, "waw"))
            for r in b.rd:
                op.deps.append((r, "war"))
        for b in reads:
            b.rd.append(op)
        for b in writes:
            b.lw = op
            b.rd = []
        self.ops.append(op)
        return op

    def op(self, eng, fn, reads=(), writes=()):
        return self._rec(eng, fn, list(reads), list(writes))

    def dma(self, out_ap, in_ap, reads, writes, eng="sp", key=None, **kw):
        def fn(e):
            return e.dma_start(out=out_ap, in_=in_ap, **kw)
        return self._rec(eng, fn, list(reads), list(writes), dma=True, key=key)

    def build(self):
        nc = self.nc
        need = []
        for x in self.ops:
            for (p, kind) in x.deps:
                if p is x:
                    continue
                if not p.dma and not x.dma and p.eng == x.eng:
                    if p.eng == "pe" or kind != "raw":
                        continue
                need.append((x, p))
                p.sig = True
        cnt = {e: 0 for e in self.ENGS}
        dcnt = {}
        for o in self.ops:
            if o.dma:
                dcnt[o.sem] = dcnt.get(o.sem, 0) + 16
                o.val = dcnt[o.sem]
                o.sig = True
            elif o.sig:
                cnt[o.eng] += 1
                o.val = cnt[o.eng]
                o.sem = "eng_" + o.eng
        semnames = ["eng_" + e for e in self.ENGS if cnt[e] > 0] + list(dcnt.keys())
        sems = {}
        for i, n in enumerate(semnames):
            sems[n] = self.es.enter_context(nc.semaphore(f"{self.pfx}s{i}"))
        self.nsems = len(sems)
        waited = {e: {} for e in self.ENGS}
        per_eng = {e: [] for e in self.ENGS}
        needmap = {}
        for (x, p) in need:
            needmap.setdefault(id(x), []).append(p)
        for x in self.ops:
            w = waited[x.eng]
            best = {}
            for p in needmap.get(id(x), []):
                if w.get(p.sem, 0) >= p.val:
                    continue
                if best.get(p.sem, 0) < p.val:
                    best[p.sem] = p.val
            for s, v in best.items():
                w[s] = v
                x.waits.append((s, v))
            per_eng[x.eng].append(x)
        finals = [(s, v) for s, v in dcnt.items()]
        engobj = {"pe": "tensor", "dve": "vector", "act": "scalar", "pool": "gpsimd", "sp": "sync"}
        with nc.Block() as block:
            for e in self.ENGS:
                lst = per_eng[e]
                if not lst and e != "sp":
                    continue

                def body(eng, lst=lst, e=e):
                    for x in lst:
                        for (s, v) in x.waits:
                            eng.wait_ge(sems[s], v)
                        ins = x.fn(eng)
                        if x.sig:
                            ins.then_inc(sems[x.sem], 16 if x.dma else 1)
                    if e == "sp":
                        for (s, v) in finals:
                            eng.wait_ge(sems[s], v)
                getattr(block, engobj[e])(body)
        self.es.close()
        return cnt, dcnt


EPS = 1e-6


def split_tokens(TT):
    n = (TT + 511) // 512
    sz = TT // n
    assert sz * n == TT
    return [(i * sz, sz) for i in range(n)]


class Env:
    pass


def setup_env(P, D, TT, KCmax, nvec):
    E = Env()
    E.D, E.TT = D, TT
    E.DC = D // 128
    E.tts = split_tokens(TT)
    E.ps = [P.ps([128, 512], F32, name=f"psb{i}") for i in range(8)]
    E.psi = 0
    E.wt = [P.sb([128, KCmax, 256], BF16, name=f"wt{i}") for i in range(3)]
    E.wi = 0
    E.actA = P.sb([128, KCmax * TT], BF16, name="actA")
    E.xin = [P.sb([128, TT], F32, name=f"xin{i}") for i in range(2)]
    E.xi = 0
    E.tmp = [P.sb([128, TT], F32, name=f"tmp{i}") for i in range(2)]
    E.ti = 0
    E.ost = [P.sb([128, TT], F32, name=f"ost{i}") for i in range(3)]
    E.oi = 0
    E.obf = [P.sb([128, TT], BF16, name=f"obf{i}") for i in range(3)]
    E.bi = 0
    E.acc = P.sb([128, TT], F32, name="acc")
    E.rstd = P.sb([128, TT], F32, name="rstd")
    E.ones = P.sb([128, 128], F32, name="ones")
    P.op("dve", lambda e: e.memset(E.ones[:], 1.0), [], [E.ones])
    E.vec = P.sb([128, nvec, E.DC], F32, name="vec")
    return E


def rot(E, name, idx):
    lst = getattr(E, name)
    i = getattr(E, idx)
    setattr(E, idx, i + 1)
    return lst[i % len(lst)]


def psum_group(E):
    g = []
    for _ in E.tts:
        g.append(E.ps[E.psi % 8])
        E.psi += 1
    return g


def load_w(P, E, Wd, r0, KC, c0, width):
    wt = rot(E, "wt", "wi")
    src = Wd.t[r0:r0 + KC * 128, c0:c0 + width].rearrange("(c p) n -> p c n", p=128)
    P.dma(wt[:, :KC, :width], src, [Wd], [wt], eng="pool")
    return wt


def mm_chunk(P, E, wt, wo, width, KC, Xb, xview, psg, first=True, last=True, kbase=0):
    for k in range(KC):
        for ti, (t0, sz) in enumerate(E.tts):
            ps = psg[ti]
            P.op("pe", lambda e, ps=ps, k=k, t0=t0, sz=sz: e.matmul(
                ps[:width, :sz], wt[:, k, wo:wo + width], xview[:, kbase + k, t0:t0 + sz],
                start=(first and k == 0), stop=(last and k == KC - 1)), [wt, Xb], [ps])


def rms_stats(P, E, acc_ready_buf):
    D = E.D
    psg = psum_group(E)
    for ti, (t0, sz) in enumerate(E.tts):
        ps = psg[ti]
        P.op("pe", lambda e, ps=ps, t0=t0, sz=sz: e.matmul(ps[:, :sz], E.ones[:, :], E.acc[:, t0:t0 + sz], start=True, stop=True),
             [E.ones, E.acc], [ps])
        P.op("dve", lambda e, ps=ps, t0=t0, sz=sz: e.tensor_scalar(E.rstd[:, t0:t0 + sz], ps[:, :sz], 1.0 / D, EPS, ALU.mult, ALU.add),
             [ps], [E.rstd])
    P.op("act", lambda e: e.activation(E.rstd[:, :], E.rstd[:, :], AF.Sqrt), [E.rstd], [E.rstd])
    P.op("dve", lambda e: e.reciprocal(E.rstd[:, :], E.rstd[:, :]), [E.rstd], [E.rstd])


def norm_pass(P, E, src_chunks, NL, gs, sh, gsc, shc):
    TT, DC = E.TT, E.DC
    for c in range(DC):
        xc = rot(E, "xin", "xi")
        P.dma(xc[:, :], src_chunks[c].t, [src_chunks[c]], [xc])
        if c == 0:
            P.op("act", lambda e, xc=xc: e.activation(E.acc[:, :], xc[:, :], AF.Square), [xc], [E.acc])
        else:
            sq = rot(E, "tmp", "ti")
            P.op("act", lambda e, xc=xc, sq=sq: e.activation(sq[:, :], xc[:, :], AF.Square), [xc], [sq])
            P.op("dve", lambda e, sq=sq: e.tensor_tensor(E.acc[:, :], E.acc[:, :], sq[:, :], ALU.add), [sq, E.acc], [E.acc])
    rms_stats(P, E, E.acc)
    xv = E.actA.t[:, :DC * TT].rearrange("p (c t) -> p c t", t=TT)
    for c in range(DC):
        xc = rot(E, "xin", "xi")
        P.dma(xc[:, :], src_chunks[c].t, [src_chunks[c]], [xc])
        tm = rot(E, "tmp", "ti")
        for (a, b, g_, s_) in ((0, NL, gs, sh), (NL, TT, gsc, shc)):
            if b <= a:
                continue
            P.op("dve", lambda e, xc=xc, tm=tm, a=a, b=b, g_=g_, c=c: e.scalar_tensor_tensor(
                tm[:, a:b], xc[:, a:b], E.vec[:, g_, c:c + 1], E.rstd[:, a:b], ALU.mult, ALU.mult), [xc, E.vec, E.rstd], [tm])
            P.op("act", lambda e, tm=tm, a=a, b=b, s_=s_, c=c: e.activation(
                xv[:, c, a:b], tm[:, a:b], AF.Identity, bias=E.vec[:, s_, c:c + 1], scale=1.0), [tm, E.vec], [E.actA])
    return xv


def derive_gs(P, E, dst, g, sc):
    P.op("dve", lambda e: e.scalar_tensor_tensor(E.vec[:, dst, :], E.vec[:, sc, :], 1.0, E.vec[:, g, :], ALU.add, ALU.mult), [E.vec], [E.vec])


def derive_half(P, E, dst, src):
    P.op("dve", lambda e: e.tensor_scalar(E.vec[:, dst, :], E.vec[:, src, :], 0.5, None, ALU.mult), [E.vec], [E.vec])


def ffn_block(P, E, NL, src_ch, dst_ch, wfi, wfo, FF, aT_ch, gs, sh, gsc, shc, hg, hgc):
    DC, TT = E.DC, E.TT
    FC = FF // 128
    hv = norm_pass(P, E, src_ch, NL, gs, sh, gsc, shc)
    for f in range(FC):
        if f % 2 == 0:
            nb = min(2, FC - f) * 128
            wg = load_w(P, E, wfi, 0, DC, f * 128, nb)
            wu = load_w(P, E, wfi, 0, DC, FF + f * 128, nb)
        wo = (f % 2) * 128
        pg = psum_group(E)
        mm_chunk(P, E, wg, wo, 128, DC, E.actA, hv, pg)
        pu = psum_group(E)
        mm_chunk(P, E, wu, wo, 128, DC, E.actA, hv, pu)
        sg = rot(E, "tmp", "ti")
        ao = rot(E, "obf", "bi")
        for ti, (t0, sz) in enumerate(E.tts):
            P.op("act", lambda e, ti=ti, t0=t0, sz=sz, pg=pg, sg=sg: e.activation(sg[:, t0:t0 + sz], pg[ti][:, :sz], AF.Silu), [pg[ti]], [sg])
            P.op("dve", lambda e, ti=ti, t0=t0, sz=sz, pu=pu, sg=sg, ao=ao: e.tensor_tensor(ao[:, t0:t0 + sz], sg[:, t0:t0 + sz], pu[ti][:, :sz], ALU.mult), [pu[ti], sg], [ao])
        P.dma(aT_ch[f].t, ao[:, :], [ao], [aT_ch[f]], eng="sp", key=f"st{ao.name}")
    av = E.actA.t[:, :FC * TT].rearrange("p (c t) -> p c t", t=TT)
    for f in range(FC):
        P.dma(av[:, f, :], aT_ch[f].t, [aT_ch[f]], [E.actA], key="actA_ld")
    for c in range(DC):
        if c % 2 == 0:
            wt = load_w(P, E, wfo, 0, FC, c * 128, 256)
        wo = (c % 2) * 128
        pg = psum_group(E)
        mm_chunk(P, E, wt, wo, 128, FC, E.actA, av, pg)
        xc = rot(E, "xin", "xi")
        P.dma(xc[:, :], src_ch[c].t, [src_ch[c]], [xc])
        xo = rot(E, "ost", "oi")
        for ti, (t0, sz) in enumerate(E.tts):
            for (a, b, hgi) in ((t0, min(t0 + sz, NL), hg), (max(t0, NL), t0 + sz, hgc)):
                if b <= a:
                    continue
                P.op("dve", lambda e, ti=ti, t0=t0, a=a, b=b, hgi=hgi, pg=pg, xc=xc, xo=xo, c=c: e.scalar_tensor_tensor(
                    xo[:, a:b], pg[ti][:, a - t0:b - t0], E.vec[:, hgi, c:c + 1], xc[:, a:b], ALU.mult, ALU.add),
                    [pg[ti], E.vec, xc], [xo])
        P.dma(dst_ch[c].t, xo[:, :], [xo], [dst_ch[c]], key=f"st{xo.name}")


def chunks_of(ap, n, pfx):
    return [Buf(f"{pfx}{c}", ap[c * 128:(c + 1) * 128, :]) for c in range(n)]


def build_A(nc, D, FF, NP, TT, NL, pfx=""):
    P = Prog(nc, pfx)
    DC, FC = D // 128, FF // 128
    E = setup_env(P, D, TT, max(DC, FC), 18)
    xT = P.dram("xT", [D, TT], F32, kind="ExternalInput")
    vecs = P.dram("vecs", [128, 12, DC], F32, kind="ExternalInput")
    wfi = P.dram("wfi", [D, 2 * FF], F32, kind="ExternalInput")
    wfo = P.dram("wfo", [FF, D], F32, kind="ExternalInput")
    win = P.dram("win", [D, NP], F32, kind="ExternalInput")
    x1T = P.dt("x1T", [D, TT], F32, kind="ExternalOutput").ap()
    pT = P.dt("pT", [NP, TT], F32, kind="ExternalOutput").ap()
    aT = P.dt("aT", [FF, TT], BF16, kind="Internal").ap()
    xT_ch = chunks_of(xT.t, DC, "xT")
    x1_ch = chunks_of(x1T, DC, "x1T")
    aT_ch = chunks_of(aT, FC, "aT")
    P.dma(E.vec[:, 0:12, :], vecs.t, [vecs], [E.vec])
    derive_gs(P, E, 12, 0, 3); derive_gs(P, E, 13, 0, 8); derive_gs(P, E, 16, 1, 6); derive_gs(P, E, 17, 1, 11)
    derive_half(P, E, 14, 4); derive_half(P, E, 15, 9)
    ffn_block(P, E, NL, xT_ch, x1_ch, wfi, wfo, FF, aT_ch, 12, 2, 13, 7, 14, 15)
    uv = norm_pass(P, E, x1_ch, NL, 16, 5, 17, 10)
    nchunks = (NP + 127) // 128
    for n in range(nchunks):
        if n % 2 == 0:
            nb = min(256, NP - n * 128)
            wt = load_w(P, E, win, 0, DC, n * 128, nb)
        wo = (n % 2) * 128
        width = min(128, NP - n * 128)
        pg = psum_group(E)
        mm_chunk(P, E, wt, wo, width, DC, E.actA, uv, pg)
        po = rot(E, "ost", "oi")
        for ti, (t0, sz) in enumerate(E.tts):
            if (ti + n) % 2 == 0:
                P.op("act", lambda e, ti=ti, t0=t0, sz=sz, pg=pg, po=po, width=width: e.copy(po[:width, t0:t0 + sz], pg[ti][:width, :sz]), [pg[ti]], [po])
            else:
                P.op("dve", lambda e, ti=ti, t0=t0, sz=sz, pg=pg, po=po, width=width: e.tensor_copy(po[:width, t0:t0 + sz], pg[ti][:width, :sz]), [pg[ti]], [po])
        pch = Buf(f"pT{n}", pT[n * 128:n * 128 + width, :])
        P.dma(pch.t, po[:width, :], [po], [pch], key=f"st{po.name}")
    return P.build()


def build_C(nc, D, FF, TT, NL, WH, WA, WR, last, pfx=""):
    P = Prog(nc, pfx)
    DC, FC = D // 128, FF // 128
    MIX = WH + WA + WR
    MC = MIX // 128
    E = setup_env(P, D, TT, max(DC, FC, MC), 24)
    x1T = P.dram("x1T", [D, TT], F32, kind="ExternalInput")
    yT = P.dram("yT", [MIX, TT], F32, kind="ExternalInput")
    vecs = P.dram("vecs", [128, 15, DC], F32, kind="ExternalInput")
    wg = P.dram("wg", [D, 3 * D], F32, kind="ExternalInput")
    wbr = P.dram("wbr", [MIX, D], F32, kind="ExternalInput")
    wo_ = P.dram("wo", [D, D], F32, kind="ExternalInput")
    wfi = P.dram("wfi", [D, 2 * FF], F32, kind="ExternalInput")
    wfo = P.dram("wfo", [FF, D], F32, kind="ExternalInput")
    x3T = P.dt("x3T", [D, TT], F32, kind="ExternalOutput").ap()
    x2T = P.dt("x2T", [D, TT], F32, kind="Internal").ap()
    gT = P.dt("gT", [3 * D, TT], BF16, kind="Internal").ap()
    mT = P.dt("mT", [D, TT], BF16, kind="Internal").ap()
    aT = P.dt("aT", [FF, TT], BF16, kind="Internal").ap()
    x1_ch = chunks_of(x1T.t, DC, "x1T")
    y_ch = chunks_of(yT.t, MC, "yT")
    x2_ch = chunks_of(x2T, DC, "x2T")
    x3_ch = chunks_of(x3T, DC, "x3T")
    g_ch = chunks_of(gT, 3 * DC, "gT")
    m_ch = chunks_of(mT, DC, "mT")
    aT_ch = chunks_of(aT, FC, "aT")
    P.dma(E.vec[:, 0:15, :], vecs.t, [vecs], [E.vec])
    derive_gs(P, E, 16, 0, 3); derive_gs(P, E, 17, 0, 9); derive_gs(P, E, 18, 1, 6); derive_gs(P, E, 19, 1, 12)
    derive_half(P, E, 20, 7); derive_half(P, E, 21, 13)
    uv = norm_pass(P, E, x1_ch, NL, 16, 2, 17, 8)
    for n in range(3 * DC):
        if n % 2 == 0:
            wt = load_w(P, E, wg, 0, DC, n * 128, 256)
        wo = (n % 2) * 128
        pg = psum_group(E)
        mm_chunk(P, E, wt, wo, 128, DC, E.actA, uv, pg)
        go = rot(E, "obf", "bi")
        for ti, (t0, sz) in enumerate(E.tts):
            P.op("act", lambda e, ti=ti, t0=t0, sz=sz, pg=pg, go=go: e.activation(go[:, t0:t0 + sz], pg[ti][:, :sz], AF.Sigmoid), [pg[ti]], [go])
        P.dma(g_ch[n].t, go[:, :], [go], [g_ch[n]], eng="sp", key=f"st{go.name}")
    yv = E.actA.t[:, :MC * TT].rearrange("p (c t) -> p c t", t=TT)
    for c in range(MC):
        P.dma(yv[:, c, :], y_ch[c].t, [y_ch[c]], [E.actA], eng="pool", key="actA_ld")
    kb = [(0, WH // 128), (WH // 128, WA // 128), ((WH + WA) // 128, WR // 128)]
    for c in range(DC):
        mo = rot(E, "ost", "oi")
        mb = rot(E, "obf", "bi")
        for b, (k0, kc) in enumerate(kb):
            wt = load_w(P, E, wbr, k0 * 128, kc, c * 128, 128)
            pg = psum_group(E)
            mm_chunk(P, E, wt, 0, 128, kc, E.actA, yv, pg, kbase=k0)
            gl = rot(E, "tmp", "ti")
            glb = gl.t.bitcast(BF16)
            P.dma(glb[:, :TT], g_ch[b * DC + c].t, [g_ch[b * DC + c]], [gl])
            for ti, (t0, sz) in enumerate(E.tts):
                if b == 0:
                    P.op("dve", lambda e, ti=ti, t0=t0, sz=sz, pg=pg, glb=glb, gl=gl, mo=mo: e.tensor_tensor(mo[:, t0:t0 + sz], pg[ti][:, :sz], glb[:, t0:t0 + sz], ALU.mult), [pg[ti], gl], [mo])
                else:
                    t2 = rot(E, "xin", "xi")
                    dst = mo if b == 1 else mb
                    P.op("dve", lambda e, ti=ti, t0=t0, sz=sz, pg=pg, glb=glb, gl=gl, t2=t2: e.tensor_tensor(t2[:, t0:t0 + sz], pg[ti][:, :sz], glb[:, t0:t0 + sz], ALU.mult), [pg[ti], gl], [t2])
                    P.op("dve", lambda e, t0=t0, sz=sz, t2=t2, mo=mo, dst=dst: e.tensor_tensor(dst[:, t0:t0 + sz], mo[:, t0:t0 + sz], t2[:, t0:t0 + sz], ALU.add), [t2, mo], [dst])
        P.dma(m_ch[c].t, mb[:, :], [mb], [m_ch[c]], eng="sp", key=f"st{mb.name}")
    mv = E.actA.t[:, :DC * TT].rearrange("p (c t) -> p c t", t=TT)
    for c in range(DC):
        P.dma(mv[:, c, :], m_ch[c].t, [m_ch[c]], [E.actA], key="actA_ld")
    for c in range(DC):
        if c % 2 == 0:
            wt = load_w(P, E, wo_, 0, DC, c * 128, 256)
        wo = (c % 2) * 128
        pg = psum_group(E)
        mm_chunk(P, E, wt, wo, 128, DC, E.actA, mv, pg)
        xc = rot(E, "xin", "xi")
        P.dma(xc[:, :], x1_ch[c].t, [x1_ch[c]], [xc])
        xo = rot(E, "ost", "oi")
        for ti, (t0, sz) in enumerate(E.tts):
            for (a, b, gi) in ((t0, min(t0 + sz, NL), 4), (max(t0, NL), t0 + sz, 10)):
                if b <= a:
                    continue
                P.op("dve", lambda e, ti=ti, t0=t0, a=a, b=b, gi=gi, pg=pg, xc=xc, xo=xo, c=c: e.scalar_tensor_tensor(
                    xo[:, a:b], pg[ti][:, a - t0:b - t0], E.vec[:, gi, c:c + 1], xc[:, a:b], ALU.mult, ALU.add),
                    [pg[ti], E.vec, xc], [xo])
        P.dma(x2_ch[c].t, xo[:, :], [xo], [x2_ch[c]], key=f"st{xo.name}")
    if not last:
        ffn_block(P, E, NL, x2_ch, x3_ch, wfi, wfo, FF, aT_ch, 18, 5, 19, 11, 20, 21)
    else:
        x3i = P.dt("x3i", [D, TT], F32, kind="Internal").ap()
        x3i_ch = chunks_of(x3i, DC, "x3i")
        ffn_block(P, E, NL, x2_ch, x3i_ch, wfi, wfo, FF, aT_ch, 18, 5, 19, 11, 20, 21)
        for c in range(DC):
            xc = rot(E, "xin", "xi")
            P.dma(xc[:, :], x3i_ch[c].t, [x3i_ch[c]], [xc])
            if c == 0:
                P.op("act", lambda e, xc=xc: e.activation(E.acc[:, :], xc[:, :], AF.Square), [xc], [E.acc])
            else:
                sq = rot(E, "tmp", "ti")
                P.op("act", lambda e, xc=xc, sq=sq: e.activation(sq[:, :], xc[:, :], AF.Square), [xc], [sq])
                P.op("dve", lambda e, sq=sq: e.tensor_tensor(E.acc[:, :], E.acc[:, :], sq[:, :], ALU.add), [sq, E.acc], [E.acc])
        rms_stats(P, E, E.acc)
        for c in range(DC):
            xc = rot(E, "xin", "xi")
            P.dma(xc[:, :], x3i_ch[c].t, [x3i_ch[c]], [xc])
            xo = rot(E, "ost", "oi")
            P.op("dve", lambda e, xc=xc, xo=xo, c=c: e.scalar_tensor_tensor(
                xo[:, :], xc[:, :], E.vec[:, 14, c:c + 1], E.rstd[:, :], ALU.mult, ALU.mult), [xc, E.vec, E.rstd], [xo])
            P.dma(x3_ch[c].t, xo[:, :], [xo], [x3_ch[c]], key=f"st{xo.name}")
    return P.build()

import numpy as np

EPS = 1e-6


def build_at(nc, H, LQ, LC, lam_init, with_ctx_q, pfx=""):
    P = Prog(nc, pfx)
    LK = LQ + LC
    NKC = LK // 128
    q = P.dram("q", [H, 2, 64, LQ], F32, kind="ExternalInput")
    k = P.dram("k", [H, 2, 64, LQ], F32, kind="ExternalInput")
    kc = P.dram("kc", [H, 2, 64, LC], F32, kind="ExternalInput")
    v = P.dram("v", [H, LK, 128], F32, kind="ExternalInput")
    cs = P.dram("cs", [2, 64, LQ], F32, kind="ExternalInput")
    rt = P.dram("rt", [64, 64], F32, kind="ExternalInput")
    lamv = P.dram("lamv", [1, 256], F32, kind="ExternalInput")
    subln = P.dram("subln", [128, 1], F32, kind="ExternalInput")
    ya = P.dt("ya", [H, 128, LQ], F32, kind="ExternalOutput").ap()
    if with_ctx_q:
        qc = P.dram("qc", [H, 2, 64, LC], F32, kind="ExternalInput")
        yac = P.dt("yac", [H, 128, LC], F32, kind="ExternalOutput").ap()
    Qa = [P.sb([65, LK], BF16, name=f"Qa{m}") for m in range(2)]
    Ka = [P.sb([65, LK], BF16, name=f"Ka{m}") for m in range(2)]
    Vb = P.sb([128, NKC, 128], BF16, name="Vb")
    ps_s = [P.ps([128, 512], F32, name=f"pss{i}") for i in range(3)]
    ps_o = [P.ps([128, 512], F32, name=f"pso{i}") for i in range(2)]
    ps_l = [P.ps([128, 512], F32, name=f"psl{i}") for i in range(2)]
    ps_x = P.ps([128, 512], F32, name="psx")
    xin = [P.sb([64, 512], F32, name=f"xin{i}") for i in range(3)]
    csb = [P.sb([64, 2, 512], F32, name=f"csb{i}") for i in range(2)]
    t1 = [P.sb([64, 512], F32, name=f"t1_{i}") for i in range(2)]
    t2 = [P.sb([64, 512], F32, name=f"t2_{i}") for i in range(2)]
    sq = [P.sb([64, 512], F32, name=f"sq{i}") for i in range(2)]
    pt = [P.sb([128, 512], BF16, name=f"pt{i}") for i in range(3)]
    wk = [P.sb([128, 512], F32, name=f"wk{i}") for i in range(6)]
    yo = [P.sb([128, 512], F32, name=f"yo{i}") for i in range(2)]
    rtb = P.sb([64, 64], F32, name="rtb")
    ones65 = P.sb([64, 65], F32, name="ones65")
    onesf = P.sb([128, 128], F32, name="onesf")
    onesb = P.sb([128, 128], BF16, name="onesb")
    kmax2 = P.sb([65, 1], F32, name="kmax2")
    kmt = P.sb([65, 1], F32, name="kmt")
    lam = P.sb([128, 4], F32, name="lam")
    lrow = P.sb([1, 256], F32, name="lrow")
    lt = P.sb([1, 8], F32, name="lt")
    sub = P.sb([128, 1], F32, name="sub")
    cnt = {"x": 0, "c": 0, "t": 0, "s": 0, "p": 0, "w": 0, "y": 0, "ss": 0}

    def nxt(lst, key):
        b = lst[cnt[key] % len(lst)]
        cnt[key] += 1
        return b

    P.dma(rtb[:, :], rt.t, [rt], [rtb])
    P.dma(lrow[:, :], lamv.t, [lamv], [lrow])
    P.dma(sub[:, :], subln.t, [subln], [sub])
    P.op("dve", lambda e: e.memset(ones65[:, :], 1.0), [], [ones65])
    P.op("dve", lambda e: e.memset(onesf[:, :], 1.0), [], [onesf])
    P.op("dve", lambda e: e.memset(onesb[:, :], 1.0), [], [onesb])
    P.op("dve", lambda e: e.memset(lt[:, :], 0.0), [], [lt])
    lj = P.sb([1, 64], F32, name="lj")
    for i in range(2):
        P.op("dve", lambda e, i=i: e.scalar_tensor_tensor(lj[:, :], lrow[:, 128 * i:128 * i + 64], 1.0, lrow[:, 128 * i + 64:128 * i + 128],
                                                          ALU.mult, ALU.mult, accum_out=lt[:, i:i + 1]), [lrow, lt], [lj, lt])
    P.op("act", lambda e: e.activation(lt[:, 2:4], lt[:, 0:2], AF.Exp), [lt], [lt])
    P.op("dve", lambda e: e.tensor_tensor(lt[:, 4:5], lt[:, 3:4], lt[:, 2:3], ALU.subtract), [lt], [lt])
    P.op("dve", lambda e: e.tensor_scalar(lt[:, 4:5], lt[:, 4:5], -float(lam_init), None, ALU.add), [lt], [lt])
    P.op("pe", lambda e: e.matmul(ps_x[:, 0:1], onesf[0:1, :], lt[0:1, 4:5], start=True, stop=True), [onesf, lt], [ps_x])
    P.op("dve", lambda e: e.tensor_copy(lam[:, 0:1], ps_x[:, 0:1]), [ps_x], [lam])
    P.op("dve", lambda e: e.tensor_scalar(lam[:, 1:2], sub[:, 0:1], 1.0 - float(lam_init), None, ALU.mult), [sub], [lam])

    for h in range(H):
        P.dma(Vb[:, :, :], v.t[h].rearrange("(c p) e -> p c e", p=128), [v], [Vb], eng="pool")
        P.op("dve", lambda e: e.memset(kmax2[:, :], 0.0), [], [kmax2])
        for m in range(2):
            P.op("dve", lambda e, m=m: e.memset(Ka[m][64:65, :], -1.0), [], [Ka[m]])

        def rope_chunk(src_ap, src_buf, dst, c0, n, use_rope, c_tab0, is_q, m):
            x = nxt(xin, "x")
            P.dma(x[:, :n], src_ap, [src_buf], [x])
            s = nxt(sq, "ss")
            P.op("act", lambda e: e.activation(s[:, :n], x[:, :n], AF.Square), [x], [s])
            P.op("pe", lambda e: e.matmul(ps_x[:65, :n], ones65[:, :], s[:, :n], start=True, stop=True), [ones65, s], [ps_x])
            if use_rope:
                ct = nxt(csb, "c")
                P.dma(ct[:, :, :n], cs.t[:, :, c_tab0:c_tab0 + n].rearrange("a d t -> d a t"), [cs], [ct])
                pr = nxt(ps_s, "s")
                P.op("pe", lambda e: e.matmul(pr[:64, :n], rtb[:, :], x[:, :n], start=True, stop=True), [rtb, x], [pr])
                a = nxt(t1, "t")
                b = t2[(cnt["t"] - 1) % 2]
                P.op("dve", lambda e: e.tensor_tensor(a[:, :n], x[:, :n], ct[:, 0, :n], ALU.mult), [x, ct], [a])
                P.op("dve", lambda e: e.tensor_tensor(b[:, :n], pr[:64, :n], ct[:, 1, :n], ALU.mult), [pr, ct], [b])
                P.op("dve", lambda e: e.tensor_tensor(dst[0:64, c0:c0 + n], a[:, :n], b[:, :n], ALU.add), [a, b], [dst])
            else:
                P.op("dve", lambda e: e.tensor_copy(dst[0:64, c0:c0 + n], x[:, :n]), [x], [dst])
            if is_q:
                P.op("act", lambda e: e.activation(dst[64:65, c0:c0 + n], ps_x[64:65, :n], AF.Sqrt, scale=kmax2[64:65, 0:1]), [ps_x, kmax2], [dst])
            else:
                P.op("dve", lambda e: e.reduce_max(kmt[:, :], ps_x[:65, :n], axis=AX.X), [ps_x], [kmt])
                P.op("dve", lambda e: e.tensor_tensor(kmax2[:, :], kmax2[:, :], kmt[:, :], ALU.max), [kmt, kmax2], [kmax2])

        for m in range(2):
            for c0 in range(0, LQ, 512):
                n = min(512, LQ - c0)
                rope_chunk(k.t[h, m, :, c0:c0 + n], k, Ka[m], c0, n, True, c0, False, m)
            for c0 in range(0, LC, 512):
                n = min(512, LC - c0)
                rope_chunk(kc.t[h, m, :, c0:c0 + n], kc, Ka[m], LQ + c0, n, False, 0, False, m)
        for m in range(2):
            for c0 in range(0, LQ, 512):
                n = min(512, LQ - c0)
                rope_chunk(q.t[h, m, :, c0:c0 + n], q, Qa[m], c0, n, True, c0, True, m)
            if with_ctx_q:
                for c0 in range(0, LC, 512):
                    n = min(512, LC - c0)
                    rope_chunk(qc.t[h, m, :, c0:c0 + n], qc, Qa[m], LQ + c0, n, False, 0, True, m)

        def attend(qc0, n, kchunks, out_ap):
            LA = 2
            nk = len(kchunks)
            for m in range(2):
                sl = {}

                def emit_s(j, m=m, sl=sl):
                    ss = nxt(ps_s, "s")
                    kci = kchunks[j]
                    P.op("pe", lambda e, ss=ss, kci=kci, m=m: e.matmul(ss[:, :n], Ka[m][0:65, kci * 128:(kci + 1) * 128], Qa[m][0:65, qc0:qc0 + n],
                                                                   start=True, stop=True), [Ka[m], Qa[m]], [ss])
                    sl[j] = ss
                for j in range(min(LA, nk)):
                    emit_s(j)
                for j, kci in enumerate(kchunks):
                    if j + LA < nk:
                        emit_s(j + LA)
                    ss = sl.pop(j)
                    p_ = nxt(pt, "p")
                    P.op("act", lambda e, ss=ss, p_=p_: e.activation(p_[:, :n], ss[:, :n], AF.Exp, scale=0.125), [ss], [p_])
                    P.op("pe", lambda e, p_=p_, kci=kci, j=j, m=m: e.matmul(ps_o[m][:, :n], Vb[:, kci, :], p_[:, :n], start=(j == 0), stop=(j == nk - 1)),
                         [Vb, p_], [ps_o[m]])
                    P.op("pe", lambda e, p_=p_, j=j, m=m: e.matmul(ps_l[m][:, :n], onesb[:, :], p_[:, :n], start=(j == 0), stop=(j == nk - 1)),
                         [onesb, p_], [ps_l[m]])
            rl = [nxt(wk, "w") for _ in range(2)]
            o = [nxt(wk, "w") for _ in range(2)]
            for m in range(2):
                P.op("dve", lambda e, m=m: e.reciprocal(rl[m][:, :n], ps_l[m][:, :n]), [ps_l[m]], [rl[m]])
                P.op("dve", lambda e, m=m: e.tensor_tensor(o[m][:, :n], ps_o[m][:, :n], rl[m][:, :n], ALU.mult), [ps_o[m], rl[m]], [o[m]])
            d = nxt(wk, "w")
            P.op("dve", lambda e: e.scalar_tensor_tensor(d[:, :n], o[1][:, :n], lam[:, 0:1], o[0][:, :n], ALU.mult, ALU.add), [o[0], o[1], lam], [d])
            s2 = nxt(wk, "w")
            P.op("act", lambda e: e.activation(s2[:, :n], d[:, :n], AF.Square), [d], [s2])
            P.op("pe", lambda e: e.matmul(ps_x[:, :n], onesf[:, :], s2[:, :n], start=True, stop=True), [onesf, s2], [ps_x])
            r = rl[0]
            P.op("dve", lambda e: e.tensor_scalar(r[:, :n], ps_x[:, :n], 1.0 / 128, EPS, ALU.mult, ALU.add), [ps_x], [r])
            P.op("act", lambda e: e.activation(r[:, :n], r[:, :n], AF.Sqrt), [r], [r])
            P.op("dve", lambda e: e.reciprocal(r[:, :n], r[:, :n]), [r], [r])
            y = nxt(yo, "y")
            P.op("dve", lambda e: e.scalar_tensor_tensor(y[:, :n], d[:, :n], lam[:, 1:2], r[:, :n], ALU.mult, ALU.mult), [d, lam, r], [y])
            ob = Buf("yout", out_ap)
            P.dma(out_ap, y[:, :n], [y], [ob], key=f"st{y.name}")

        allk = list(range(NKC))
        for c0 in range(0, LQ, 512):
            n = min(512, LQ - c0)
            attend(c0, n, allk, ya[h, :, c0:c0 + n])
        if with_ctx_q:
            ck = list(range(LQ // 128, NKC))
            for c0 in range(0, LC, 512):
                n = min(512, LC - c0)
                attend(LQ + c0, n, ck, yac[h, :, c0:c0 + n])
    return P.build()


def rope_tables(LQ, grid_w=64, theta=10000.0):
    t = np.arange(LQ)
    row = (t // grid_w).astype(np.float32)
    col = (t % grid_w).astype(np.float32)
    inv = (theta ** (-np.arange(16, dtype=np.float32) / 16)).astype(np.float32)
    ang = np.zeros((64, LQ), np.float32)
    for d in range(64):
        pos = row if d < 32 else col
        ang[d] = pos * inv[d % 16]
    cs = np.stack([np.cos(ang), np.sin(ang)]).astype(np.float32)
    rt = np.zeros((64, 64), np.float32)
    for d in range(64):
        if d % 32 < 16:
            rt[d + 16, d] = -1.0
        else:
            rt[d - 16, d] = 1.0
    return cs, rt

import numpy as np
import math

MAGIC = 12582912.0
TWO_PI = 2.0 * math.pi


def build_hy(nc, L, C=128, CHP=64, pfx=""):
    P = Prog(nc, pfx)
    NB = L // 128
    NI = 2 * L - 1
    NE = 2 * NB - 1
    TW = NE * 128
    hp = P.dram("hp", [L + 2, 3 * C], F32, kind="ExternalInput")
    swb_d = P.dram("swb", [128, 3, 3, C], F32, kind="ExternalInput")
    skipb_d = P.dram("skipb", [128, 2, C], F32, kind="ExternalInput")
    zT = P.dram("zT", [33, NI], F32, kind="ExternalInput")
    w1_d = P.dram("w1", [33, 64], F32, kind="ExternalInput")
    v1_d = P.dram("v1", [64, 4], F32, kind="ExternalInput")
    w2_d = P.dram("w2", [64, 64], F32, kind="ExternalInput")
    w3_d = P.dram("w3c", [64, 2, 2, C], F32, kind="ExternalInput")
    win_d = P.dram("win", [C, NI], F32, kind="ExternalInput")
    jm_d = P.dram("jm", [128, 128], F32, kind="ExternalInput")
    yh = P.dt("yh", [128, C, NB], F32, kind="ExternalOutput").ap()
    Hd_t = [P.dt(f"Hd{o}", [C, 2 * L], BF16, kind="Internal") for o in range(2)]
    Hd = [Buf(f"Hd{o}", Hd_t[o].ap()) for o in range(2)]

    ps = [P.ps([128, 512], F32, name=f"psb{i}") for i in range(8)]
    pc = {"i": 0}

    def nps():
        b = ps[pc["i"] % 8]
        pc["i"] += 1
        return b

    swb = P.sb([128, 3, 3, C], F32, name="swb_s")
    skipb = P.sb([128, 2, C], F32, name="skipb_s")
    w1 = P.sb([33, 64], F32, name="w1_s")
    v1 = P.sb([64, 8], F32, name="v1_s")
    w2 = P.sb([64, 64], F32, name="w2_s")
    w3 = P.sb([64, 2, 2, C], F32, name="w3_s")
    jf = P.sb([128, 128], F32, name="jf")
    jb = P.sb([128, 128], BF16, name="jb")
    P.dma(swb[:], swb_d.t, [swb_d], [swb])
    P.dma(skipb[:], skipb_d.t, [skipb_d], [skipb])
    P.dma(w1[:], w1_d.t, [w1_d], [w1])
    P.dma(v1[:, 0:4], v1_d.t, [v1_d], [v1])
    P.dma(w2[:], w2_d.t, [w2_d], [w2])
    P.dma(w3[:], w3_d.t, [w3_d], [w3])
    P.dma(jf[:], jm_d.t, [jm_d], [jf])
    P.op("dve", lambda e: e.tensor_copy(jb[:], jf[:]), [jf], [jb])

    zin = [P.sb([33, 512], F32, name=f"zin{i}") for i in range(2)]
    fa = [P.sb([64, 512], F32, name=f"fa{i}") for i in range(2)]
    fq = [P.sb([64, 512], F32, name=f"fq{i}") for i in range(2)]
    fh = [P.sb([64, 512], F32, name=f"fh{i}") for i in range(4)]
    wn = [P.sb([C, 512], F32, name=f"wn{i}") for i in range(2)]
    hb = [P.sb([C, 512], BF16, name=f"hb{i}") for i in range(4)]
    fc = {"a": 0, "h": 0, "b": 0}

    def sin_layer(psrc, n, bcol, fcol):
        a = fa[fc["a"] % 2]
        q_ = fq[fc["a"] % 2]
        fc["a"] += 1
        h = fh[fc["h"] % 4]
        fc["h"] += 1
        P.op("dve", lambda e: e.tensor_scalar(a[:, :n], psrc[:64, :n], v1[:, bcol:bcol + 1], v1[:, fcol:fcol + 1], ALU.add, ALU.mult), [psrc, v1], [a])
        P.op("dve", lambda e: e.tensor_scalar(q_[:, :n], a[:, :n], 1.0 / TWO_PI, MAGIC, ALU.mult, ALU.add), [a], [q_])
        P.op("dve", lambda e: e.tensor_scalar(q_[:, :n], q_[:, :n], MAGIC, -TWO_PI, ALU.subtract, ALU.mult), [q_], [q_])
        P.op("dve", lambda e: e.tensor_tensor(a[:, :n], a[:, :n], q_[:, :n], ALU.add), [a, q_], [a])
        P.op("dve", lambda e: e.tensor_scalar(a[:, :n], a[:, :n], 3.1415925, -3.1415925, ALU.min, ALU.max), [a], [a])
        P.op("act", lambda e: e.activation(h[:, :n], a[:, :n], AF.Sin), [a], [h])
        return h

    for ci, i0 in enumerate(range(0, NI, 512)):
        n = min(512, NI - i0)
        z = zin[ci % 2]
        P.dma(z[:, :n], zT.t[:, i0:i0 + n], [zT], [z])
        w_ = wn[ci % 2]
        P.dma(w_[:, :n], win_d.t[:, i0:i0 + n], [win_d], [w_])
        p1 = nps()
        P.op("pe", lambda e, p1=p1, z=z, n=n: e.matmul(p1[:64, :n], w1[:, :], z[:, :n], start=True, stop=True), [w1, z], [p1])
        h1 = sin_layer(p1, n, 0, 1)
        p2 = nps()
        P.op("pe", lambda e, p2=p2, h1=h1, n=n: e.matmul(p2[:64, :n], w2[:, :], h1[:, :n], start=True, stop=True), [w2, h1], [p2])
        h2 = sin_layer(p2, n, 2, 3)
        nbw = max(0, min(n, (L - 1) - i0))
        for o in range(2):
            p3 = nps()
            if nbw > 0:
                P.op("pe", lambda e, p3=p3, h2=h2, o=o, nbw=nbw: e.matmul(p3[:C, :nbw], w3[:, o, 1, :], h2[:, :nbw], start=True, stop=True), [w3, h2], [p3])
            if nbw < n:
                P.op("pe", lambda e, p3=p3, h2=h2, o=o, nbw=nbw, n=n: e.matmul(p3[:C, nbw:n], w3[:, o, 0, :], h2[:, nbw:n], start=True, stop=True), [w3, h2], [p3])
            hh = hb[fc["b"] % 4]
            fc["b"] += 1
            P.op("dve", lambda e, p3=p3, hh=hh, w_=w_, n=n: e.tensor_tensor(hh[:, :n], p3[:C, :n], w_[:, :n], ALU.mult), [p3, w_], [hh])
            P.dma(Hd[o].t[:, i0:i0 + n], hh[:, :n], [hh], [Hd[o]], key=f"st{hh.name}")

    FW = NB * CHP
    sh = [P.sb([128, NB, CHP], F32, name=f"sh{i}") for i in range(3)]
    vs = P.sb([128, NB, CHP], BF16, name="vs")
    vr = P.sb([128, NB, CHP], BF16, name="vr")
    zz = P.sb([128, NB, CHP], BF16, name="zz")
    xs = P.sb([128, NB, CHP], F32, name="xs")
    tt = [P.sb([128, TW], BF16, name=f"tt{i}") for i in range(3)]
    ep = [P.sb([128, 8, NB], F32, name=f"ep{i}") for i in range(3)]
    tc_ = {"i": 0, "e": 0}

    def short_conv(xi, c0, dst, dst_buf):
        for s in range(3):
            src = hp.t[s:s + L, xi * C + c0: xi * C + c0 + CHP].rearrange("(b j) c -> j b c", j=128)
            P.dma(sh[s][:, :, :], src, [hp], [sh[s]])
            wb = swb[:, xi, s, c0:c0 + CHP].unsqueeze(1).broadcast_to([128, NB, CHP])
            P.op("dve", lambda e, s=s, wb=wb: e.tensor_tensor(sh[s][:, :, :], sh[s][:, :, :], wb, ALU.mult), [sh[s], swb], [sh[s]])
        P.op("dve", lambda e: e.tensor_tensor(sh[0][:, :, :], sh[0][:, :, :], sh[1][:, :, :], ALU.add), [sh[0], sh[1]], [sh[0]])
        P.op("dve", lambda e: e.tensor_tensor(dst, sh[0][:, :, :], sh[2][:, :, :], ALU.add), [sh[0], sh[2]], [dst_buf])

    def reverse_j(src, dst):
        sv = src.t.rearrange("p b c -> p (b c)")
        dv = dst.t.rearrange("p b c -> p (b c)")
        for k, f0 in enumerate(range(0, FW, 512)):
            n = min(512, FW - f0)
            pr = nps()
            P.op("pe", lambda e, pr=pr, f0=f0, n=n: e.matmul(pr[:, :n], jb[:, :], sv[:, f0:f0 + n], start=True, stop=True), [jb, src], [pr])
            if k % 2 == 0:
                P.op("act", lambda e, pr=pr, f0=f0, n=n: e.copy(dv[:, f0:f0 + n], pr[:, :n]), [pr], [dst])
            else:
                P.op("dve", lambda e, pr=pr, f0=f0, n=n: e.tensor_copy(dv[:, f0:f0 + n], pr[:, :n]), [pr], [dst])

    def long_conv(o, c0, urev, epilogue):
        for g in range(CHP // 8):
            pg = nps()
            for c8 in range(8):
                c = g * 8 + c8
                T = tt[tc_["i"] % 3]
                tc_["i"] += 1
                src = bass.AP(Hd_t[o], (c0 + c) * 2 * L, [[1, 128], [1, TW]])
                P.dma(T[:, :], src, [Hd[o]], [T])
                order = [NB - 1] + [e_ for e_ in range(NE) if e_ != NB - 1]
                for k, e_ in enumerate(order):
                    d = e_ - (NB - 1)
                    bt0 = max(0, d)
                    n = NB - abs(d)
                    bs0 = bt0 - d
                    P.op("pe", lambda e, T=T, e_=e_, pg=pg, c8=c8, bt0=bt0, n=n, bs0=bs0, c=c, k=k: e.matmul(
                        pg[:, c8 * NB + bt0: c8 * NB + bt0 + n], T[:, e_ * 128:(e_ + 1) * 128], urev[:, bs0:bs0 + n, c],
                        start=(k == 0), stop=(k == NE - 1), skip_group_check=True), [T, urev], [pg])
            epilogue(g, pg)

    for c0 in range(0, C, CHP):
        short_conv(0, c0, vs[:, :, :], vs)
        short_conv(1, c0, xs[:, :, :], xs)
        reverse_j(vs, vr)

        def epi1(g, pg, c0=c0):
            t = ep[tc_["e"] % 3]
            tc_["e"] += 1
            vv = vs[:, :, g * 8:(g + 1) * 8].rearrange("p b c -> p c b")
            xv = xs[:, :, g * 8:(g + 1) * 8].rearrange("p b c -> p c b")
            zv = zz[:, :, g * 8:(g + 1) * 8].rearrange("p b c -> p c b")
            sk = skipb[:, 0, c0 + g * 8:c0 + (g + 1) * 8].unsqueeze(2).broadcast_to([128, 8, NB])
            pv = pg[:, :8 * NB].rearrange("p (c b) -> p c b", b=NB)
            P.op("dve", lambda e: e.tensor_tensor(t[:, :, :], vv, sk, ALU.mult), [vs, skipb], [t])
            P.op("dve", lambda e: e.tensor_tensor(t[:, :, :], t[:, :, :], pv, ALU.add), [t, pg], [t])
            P.op("dve", lambda e: e.tensor_tensor(zv, t[:, :, :], xv, ALU.mult), [t, xs], [zz])
        long_conv(0, c0, vr, epi1)
        short_conv(2, c0, xs[:, :, :], xs)
        reverse_j(zz, vr)

        def epi2(g, pg, c0=c0):
            t = ep[tc_["e"] % 3]
            tc_["e"] += 1
            zv = zz[:, :, g * 8:(g + 1) * 8].rearrange("p b c -> p c b")
            xv = xs[:, :, g * 8:(g + 1) * 8].rearrange("p b c -> p c b")
            sk = skipb[:, 1, c0 + g * 8:c0 + (g + 1) * 8].unsqueeze(2).broadcast_to([128, 8, NB])
            pv = pg[:, :8 * NB].rearrange("p (c b) -> p c b", b=NB)
            P.op("dve", lambda e: e.tensor_tensor(t[:, :, :], zv, sk, ALU.mult), [zz, skipb], [t])
            P.op("dve", lambda e: e.tensor_tensor(t[:, :, :], t[:, :, :], pv, ALU.add), [t, pg], [t])
            P.op("dve", lambda e: e.tensor_tensor(t[:, :, :], t[:, :, :], xv, ALU.mult), [t, xs], [t])
            ob = Buf("yhout", None)
            P.dma(yh[:, c0 + g * 8:c0 + (g + 1) * 8, :], t[:, :, :], [t], [ob], key=f"st{t.name}")
        long_conv(1, c0, vr, epi2)
    return P.build()


def hyena_consts(L, C, c_lo, hy_width=1024, bands_n=16):
    idx = np.arange(2 * L - 1)
    t = np.abs(idx - (L - 1)).astype(np.float32)
    t_norm = t / np.float32(L)
    bands = np.linspace(1e-4, bands_n - 1, bands_n, dtype=np.float32)
    ang = (np.float32(2.0 * math.pi / L) * t[:, None]) * bands[None, :]
    z = np.concatenate([t_norm[:, None], np.cos(ang), -np.sin(ang)], axis=-1).astype(np.float32)
    deltas = np.abs(np.linspace(math.log(1e-2) / 1.5, math.log(1e-2) / 0.3, hy_width, dtype=np.float32))
    win = np.exp(-t_norm[None, :] * deltas[c_lo:c_lo + C, None]).astype(np.float32)
    jm = np.eye(128, dtype=np.float32)[::-1].copy()
    return np.ascontiguousarray(z.T), win, jm

import numpy as np
import math

GN_EPS = 64e-5
NG = 7


def build_rw(nc, LQ, LC, want_ctx_out, CH=16, pfx=""):
    P = Prog(nc, pfx)
    LT = LC + LQ
    segs = [(0, LC), (LC, LQ)]
    rinA = P.dram("rinA", [2, 3, 128, LT + 4], F32, kind="ExternalInput")
    rinB = P.dram("rinB", [2, 4, 128, LT + 4], F32, kind="ExternalInput")
    GSRC = {0: (rinA, 0), 1: (rinA, 1), 4: (rinA, 2), 2: (rinB, 0), 3: (rinB, 1), 5: (rinB, 2), 6: (rinB, 3)}
    swt_d = P.dram("swt", [128, NG, 3], F32, kind="ExternalInput")
    pv_d = P.dram("pv", [128, 12], F32, kind="ExternalInput")
    w2_d = P.dram("w2t", [128, 128], F32, kind="ExternalInput")
    a2_d = P.dram("a2t", [128, 128], F32, kind="ExternalInput")
    g2a_d = P.dram("g2a", [128, 128], F32, kind="ExternalInput")
    g2b_d = P.dram("g2b", [32, 128], F32, kind="ExternalInput")
    cm_d = P.dram("cm", [3, 128, 128], F32, kind="ExternalInput")
    yr = P.dt("yr", [128, LQ], F32, kind="ExternalOutput").ap()
    if want_ctx_out:
        yrc = P.dt("yrc", [128, LC], F32, kind="ExternalOutput").ap()
    cols = [Buf(f"cols{h}", P.dt(f"cols{h}", [4, 128, LT], F32, kind="Internal").ap()) for h in range(2)]
    kv_t = [P.dt(f"kv{h}", [2, 2, LT, 64], F32, kind="Internal") for h in range(2)]
    kv = [Buf(f"kv{h}", kv_t[h].ap()) for h in range(2)]
    yT = [Buf(f"yT{h}", P.dt(f"yT{h}", [128, LT], F32, kind="Internal").ap()) for h in range(2)]

    ps = [P.ps([128, 512], F32, name=f"psb{i}") for i in range(8)]
    pc = {"i": 0}

    def nps():
        b = ps[pc["i"] % 8]
        pc["i"] += 1
        return b

    swt = P.sb([128, NG, 3], F32, name="swt_s")
    pv = P.sb([128, 16], F32, name="pv_s")
    w2 = P.sb([128, 128], F32, name="w2_s")
    a2 = P.sb([128, 128], F32, name="a2_s")
    g2a = P.sb([128, 128], F32, name="g2a_s")
    g2b = P.sb([32, 128], F32, name="g2b_s")
    cm = P.sb([128, 3, 128], F32, name="cm_s")
    P.dma(swt[:], swt_d.t, [swt_d], [swt])
    P.dma(pv[:, 0:12], pv_d.t, [pv_d], [pv])
    P.dma(w2[:], w2_d.t, [w2_d], [w2])
    P.dma(a2[:], a2_d.t, [a2_d], [a2])
    P.dma(g2a[:], g2a_d.t, [g2a_d], [g2a])
    P.dma(g2b[:], g2b_d.t, [g2b_d], [g2b])
    P.dma(cm[:], cm_d.t.rearrange("a p n -> p a n"), [cm_d], [cm])
    ident, jm, bones = cm[:, 0, :], cm[:, 1, :], cm[:, 2, :]
    bonus = P.sb([128, LT], F32, name="bonus")
    gout = P.sb([128, LT], F32, name="gout")

    NW = 26
    xin = [P.sb([128, 516], F32, name=f"xin{i}") for i in range(NG + 2)]
    wkb = [P.sb([128, 512], F32, name=f"wk{i}") for i in range(NW)]
    stg = [P.sb([128, 2, 128], F32, name=f"stg{i}") for i in range(2)]
    cnt = {"x": 0, "w": 0, "s": 0}

    def nw():
        b = wkb[cnt["w"] % NW]
        cnt["w"] += 1
        return b

    def conv(d, g, col0, n):
        x = xin[cnt["x"] % len(xin)]
        cnt["x"] += 1
        rsrc, gi = GSRC[g]
        P.dma(x[:, :n + 2], rsrc.t[d, gi, :, col0:col0 + n + 2], [rsrc], [x])
        u = nw()
        taps = (0, 1, 2) if d == 0 else (2, 1, 0)
        P.op("act", lambda e: e.activation(u[:, :n], x[:, 0:n], AF.Identity, scale=swt[:, g, taps[0]:taps[0] + 1]), [x, swt], [u])
        P.op("dve", lambda e: e.scalar_tensor_tensor(u[:, :n], x[:, 1:n + 1], swt[:, g, taps[1]:taps[1] + 1], u[:, :n], ALU.mult, ALU.add), [x, swt, u], [u])
        P.op("dve", lambda e: e.scalar_tensor_tensor(u[:, :n], x[:, 2:n + 2], swt[:, g, taps[2]:taps[2] + 1], u[:, :n], ALU.mult, ALU.add), [x, swt, u], [u])
        return u

    def lowrank(u, wt, d, n, bias_col):
        p_ = nps()
        lo = d * 64
        P.op("pe", lambda e: e.matmul(p_[:, :n], wt[lo:lo + 64, :], u[lo:lo + 64, :n], start=True, stop=True), [wt, u], [p_])
        o = nw()
        P.op("act", lambda e: e.activation(o[:, :n], p_[:, :n], AF.Sigmoid, bias=pv[:, bias_col:bias_col + 1], scale=1.0), [p_, pv], [o])
        return o

    for d in range(2):
        for (s0, sl) in segs:
            for c0 in range(0, sl, 512):
                n = min(512, sl - c0)
                n0 = s0 + c0
                seg_i = 0 if s0 == 0 else 1
                pad0 = s0 + 2 * seg_i + c0
                uk = conv(d, 0, pad0, n)
                uv = conv(d, 1, pad0, n)
                uwd = conv(d, 2, pad0, n)
                uad = conv(d, 3, pad0, n)
                ur = conv(d, 4, pad0, n)
                P.op("act", lambda e, uwd=uwd, n=n: e.activation(uwd[:, :n], uwd[:, :n], AF.Tanh), [uwd], [uwd])
                dirs = (0, 1) if d == 0 else (1,)
                a_ = {}
                dec = None
                for dd in dirs:
                    a_[dd] = lowrank(uad, a2, dd, n, 2 + dd)
                sg = lowrank(uwd, w2, d, n, d)
                dec = nw()
                P.op("act", lambda e, dec=dec, sg=sg, n=n: e.activation(dec[:, :n], sg[:, :n], AF.Exp, scale=-math.exp(-0.5)), [sg], [dec])
                kkr = nw()
                P.op("dve", lambda e, kkr=kkr, uk=uk, n=n: e.tensor_scalar(kkr[:, :n], uk[:, :n], pv[:, 4:5], None, ALU.mult), [uk, pv], [kkr])
                sq = nw()
                P.op("act", lambda e, sq=sq, kkr=kkr, n=n: e.activation(sq[:, :n], kkr[:, :n], AF.Square), [kkr], [sq])
                pn = nps()
                P.op("pe", lambda e, pn=pn, sq=sq, n=n: e.matmul(pn[:, :n], bones, sq[:, :n], start=True, stop=True), [cm, sq], [pn])
                P.op("act", lambda e, pn=pn, sq=sq, n=n: e.activation(sq[:, :n], pn[:, :n], AF.Sqrt), [pn], [sq])
                P.op("dve", lambda e, sq=sq, n=n: e.tensor_scalar(sq[:, :n], sq[:, :n], 1e-12, None, ALU.max), [sq], [sq])
                P.op("dve", lambda e, sq=sq, n=n: e.reciprocal(sq[:, :n], sq[:, :n]), [sq], [sq])
                nkk = nw()
                P.op("dve", lambda e, nkk=nkk, kkr=kkr, sq=sq, n=n: e.scalar_tensor_tensor(nkk[:, :n], kkr[:, :n], -1.0, sq[:, :n], ALU.mult, ALU.mult), [kkr, sq], [nkk])
                bb = nw()
                P.op("dve", lambda e, bb=bb, nkk=nkk, ad=a_[d], n=n: e.scalar_tensor_tensor(bb[:, :n], nkk[:, :n], -1.0, ad[:, :n], ALU.mult, ALU.mult), [nkk, a_[d]], [bb])
                kd = {}
                for dd in dirs:
                    t_ = nw()
                    P.op("dve", lambda e, t_=t_, ad=a_[dd], n=n: e.tensor_scalar(t_[:, :n], ad[:, :n], -1.0, pv[:, 5:6], ALU.add, ALU.mult), [a_[dd], pv], [t_])
                    P.op("dve", lambda e, t_=t_, uk=uk, n=n: e.scalar_tensor_tensor(t_[:, :n], t_[:, :n], 1.0, uk[:, :n], ALU.add, ALU.mult), [t_, uk], [t_])
                    kd[dd] = t_
                if d == 0:
                    ks = nw()
                    P.op("dve", lambda e, ks=ks, kd=kd, n=n: e.tensor_tensor(ks[:, :n], kd[0][:, :n], kd[1][:, :n], ALU.add), [kd[0], kd[1]], [ks])
                    P.op("dve", lambda e, ks=ks, ur=ur, n=n: e.scalar_tensor_tensor(ks[:, :n], ks[:, :n], pv[:, 6:7], ur[:, :n], ALU.mult, ALU.mult), [ks, pv, ur], [ks])
                    pb = nps()
                    P.op("pe", lambda e, pb=pb, ks=ks, n=n: e.matmul(pb[:, :n], bones, ks[:, :n], start=True, stop=True), [cm, ks], [pb])
                    P.op("dve", lambda e, pb=pb, uv=uv, n=n, n0=n0: e.tensor_tensor(bonus[:, n0:n0 + n], pb[:, :n], uv[:, :n], ALU.mult), [pb, uv], [bonus])
                    ug = conv(d, 5, pad0, n)
                    ug2 = conv(d, 6, pad0, n)
                    P.op("act", lambda e, ug=ug, n=n: e.activation(ug[:, :n], ug[:, :n], AF.Sigmoid), [ug], [ug])
                    P.op("act", lambda e, ug2=ug2, n=n: e.activation(ug2[:32, :n], ug2[:32, :n], AF.Sigmoid), [ug2], [ug2])
                    pg_ = nps()
                    P.op("pe", lambda e, pg_=pg_, ug=ug, n=n: e.matmul(pg_[:, :n], g2a[:, :], ug[:, :n], start=True, stop=False), [g2a, ug], [pg_])
                    P.op("pe", lambda e, pg_=pg_, ug2=ug2, n=n: e.matmul(pg_[:, :n], g2b[:, :], ug2[:32, :n], start=False, stop=True), [g2b, ug2], [pg_])
                    P.op("act", lambda e, pg_=pg_, n=n, n0=n0: e.copy(gout[:, n0:n0 + n], pg_[:, :n]), [pg_], [gout])
                for h in range(2):
                    for qi, sb_ in enumerate([dec, nkk, bb, ur]):
                        P.dma(cols[h].t[qi, d * 64:(d + 1) * 64, n0:n0 + n], sb_[h * 64:(h + 1) * 64, :n], [sb_], [cols[h]], key=f"cst{h}")
                for b0 in range(0, n, 128):
                    st = stg[cnt["s"] % 2]
                    cnt["s"] += 1
                    pt_ = nps()
                    for qi, sb_ in enumerate([kd[d], uv]):
                        P.op("pe", lambda e, pt_=pt_, sb_=sb_, b0=b0, qi=qi: e.matmul(pt_[:, qi * 128:(qi + 1) * 128], sb_[:, b0:b0 + 128], ident, start=True, stop=True),
                             [sb_, cm], [pt_])
                    P.op("act", lambda e, pt_=pt_, st=st: e.copy(st[:, 0:2, :], pt_[:, 0:256].rearrange("p (q c) -> p q c", c=128)), [pt_], [st])
                    for h in range(2):
                        dst = kv_t[h].ap()[:, d, n0 + b0:n0 + b0 + 128, :].rearrange("q t k -> t q k")
                        P.dma(dst, st[:, 0:2, h * 64:(h + 1) * 64], [st], [kv[h]], key=f"kst{h}")

    ST = [[P.sb([128, 64], F32, name=f"ST{h}{i}") for i in range(2)] for h in range(2)]
    Tm = [P.sb([128, 64], F32, name=f"Tm{h}") for h in range(2)]
    CB = [[P.sb([128, 4, CH], F32, name=f"CB{h}{i}") for i in range(2)] for h in range(2)]
    AR = [[P.sb([128, CH, 64], F32, name=f"AR{h}{i}") for i in range(2)] for h in range(2)]
    KV = [[P.sb([128, 2, CH, 64], F32, name=f"KV{h}{i}") for i in range(2)] for h in range(2)]
    YC = [[P.sb([64, 2, CH], F32, name=f"YC{h}{i}") for i in range(2)] for h in range(2)]
    psab = [[Buf(f"psab{h}{i}", ps[h].t[:, i * 64:(i + 1) * 64]) for i in range(2)] for h in range(2)]
    pvk = [[Buf(f"pvk{h}{i}", ps[2 + h].t[:, i * 64:(i + 1) * 64]) for i in range(2)] for h in range(2)]
    py = [[Buf(f"py{h}{i}", ps[4 + h].t[0:64, i * 2 * CH:(i + 1) * 2 * CH].rearrange("p (d c) -> p d c", d=2)) for i in range(2)] for h in range(2)]
    for h in range(2):
        P.op("dve", lambda e, h=h: e.memset(ST[h][0][:, :], 0.0), [ps[h], ps[2 + h], ps[4 + h]], [ST[h][0]])

    def flush(h, pci):
        yc = YC[h][pci % 2]
        P.op("act", lambda e, yc=yc, yp=py[h][pci % 2]: e.copy(yc[:, :, :], yp[:, :, :]), [py[h][pci % 2]], [yc])
        for d in range(2):
            P.dma(yT[h].t[d * 64:(d + 1) * 64, pci * CH:(pci + 1) * CH], yc[:, d, :], [yc], [yT[h]], key=f"yst{h}{pci % 2}")

    nchunk = LT // CH
    step = 0
    for ci in range(nchunk):
        n0 = ci * CH
        for h in range(2):
            cb = CB[h][ci % 2]; ar = AR[h][ci % 2]; kvb = KV[h][ci % 2]
            P.dma(cb[:, :, :], cols[h].t[:, :, n0:n0 + CH].rearrange("q p t -> p q t"), [cols[h]], [cb], key=f"cb{h}{ci % 2}")
            for d in range(2):
                P.dma(kvb[d * 64:d * 64 + 1, :, :, :].rearrange("p q t k -> p q (t k)"),
                      kv_t[h].ap()[:, d:d + 1, n0:n0 + CH, :].rearrange("q o t k -> o q (t k)"), [kv[h]], [kvb], key=f"kv{h}{ci % 2}")
            P.op("act", lambda e, ar=ar, cb=cb: e.activation(ar[:, :, :], cb[:, 1, :].unsqueeze(2).broadcast_to([128, CH, 64]), AF.Identity), [cb], [ar])
        for i in range(CH):
            for h in range(2):
                cb = CB[h][ci % 2]; ar = AR[h][ci % 2]; kvb = KV[h][ci % 2]
                so = ST[h][step % 2]; sn = ST[h][(step + 1) % 2]; tm = Tm[h]
                sab = psab[h][step % 2]; vk = pvk[h][step % 2]
                for d in range(2):
                    lo = d * 64
                    P.op("pe", lambda e, ar=ar, so=so, sab=sab, lo=lo, i=i: e.matmul(sab[lo:lo + 64, :], ar[lo:lo + 64, i, :], so[lo:lo + 64, :], start=True, stop=True), [ar, so], [sab])
                for d in range(2):
                    lo = d * 64
                    P.op("pe", lambda e, kvb=kvb, vk=vk, lo=lo, i=i: e.matmul(vk[lo:lo + 64, :], kvb[lo:lo + 1, 0, i, :], kvb[lo:lo + 1, 1, i, :], start=True, stop=True), [kvb], [vk])
                if step > 0:
                    pi = (i - 1) % CH
                    pci = ci if i > 0 else ci - 1
                    ypp = py[h][pci % 2]; cbp = CB[h][pci % 2]
                    for d in range(2):
                        lo = d * 64
                        P.op("pe", lambda e, so=so, cbp=cbp, ypp=ypp, lo=lo, d=d, pi=pi: e.matmul(ypp[:, d, pi:pi + 1], so[lo:lo + 64, :], cbp[lo:lo + 64, 3, pi:pi + 1], start=True, stop=True), [so, cbp], [ypp])
                    if i == 0:
                        flush(h, pci)
                P.op("dve", lambda e, tm=tm, so=so, cb=cb, vk=vk, i=i: e.scalar_tensor_tensor(tm[:, :], so[:, :], cb[:, 0, i:i + 1], vk[:, :], ALU.mult, ALU.add), [so, cb, vk], [tm])
                P.op("dve", lambda e, tm=tm, sn=sn, cb=cb, sab=sab, i=i: e.scalar_tensor_tensor(sn[:, :], sab[:, :], cb[:, 2, i:i + 1], tm[:, :], ALU.mult, ALU.add), [sab, cb, tm], [sn])
            step += 1
    lastc = nchunk - 1
    for h in range(2):
        so = ST[h][step % 2]; cbp = CB[h][lastc % 2]; ypp = py[h][lastc % 2]
        for d in range(2):
            lo = d * 64
            P.op("pe", lambda e, so=so, cbp=cbp, ypp=ypp, lo=lo, d=d: e.matmul(ypp[:, d, CH - 1:CH], so[lo:lo + 64, :], cbp[lo:lo + 64, 3, CH - 1:CH], start=True, stop=True), [so, cbp], [ypp])
        flush(h, lastc)

    yf = [P.sb([128, 128], F32, name=f"yf{i}") for i in range(2)]
    yb = [P.sb([128, 128], F32, name=f"yb{i}") for i in range(2)]
    yt = [P.sb([128, 128], F32, name=f"yt{i}") for i in range(2)]
    w3 = [P.sb([128, 128], F32, name=f"w3_{i}") for i in range(6)]
    k3 = {"i": 0, "w": 0}

    def n3():
        b = w3[k3["w"] % 6]
        k3["w"] += 1
        return b

    for (s0, sl) in segs:
        if s0 == 0 and not want_ctx_out:
            continue
        nb = sl // 128
        for b in range(nb):
            t0 = s0 + b * 128
            tb = s0 + (nb - 1 - b) * 128
            f_, b_, t_ = yf[k3["i"] % 2], yb[k3["i"] % 2], yt[k3["i"] % 2]
            k3["i"] += 1
            for h in range(2):
                P.dma(f_[h * 64:(h + 1) * 64, :], yT[h].t[0:64, t0:t0 + 128], [yT[h]], [f_], key=f"yf{k3['i'] % 2}")
                P.dma(b_[h * 64:(h + 1) * 64, :], yT[h].t[64:128, tb:tb + 128], [yT[h]], [b_], key=f"yb{k3['i'] % 2}")
            p1 = nps()
            P.op("pe", lambda e, p1=p1, b_=b_: e.matmul(p1[:, :128], b_[:, :], ident, start=True, stop=True), [b_, cm], [p1])
            P.op("act", lambda e, p1=p1, t_=t_: e.copy(t_[:, :], p1[:, :128]), [p1], [t_])
            p2 = nps()
            P.op("pe", lambda e, p2=p2, t_=t_: e.matmul(p2[:, :128], t_[:, :], jm, start=True, stop=True), [t_, cm], [p2])
            y = n3()
            P.op("dve", lambda e, y=y, p2=p2, f_=f_: e.tensor_tensor(y[:, :], p2[:, :128], f_[:, :], ALU.add), [p2, f_], [y])
            pm = nps()
            P.op("pe", lambda e, pm=pm, y=y: e.matmul(pm[:, :128], bones, y[:, :], start=True, stop=True), [cm, y], [pm])
            yc_ = n3()
            P.op("dve", lambda e, yc_=yc_, pm=pm, y=y: e.scalar_tensor_tensor(yc_[:, :], pm[:, :128], -1.0 / 64, y[:, :], ALU.mult, ALU.add), [pm, y], [yc_])
            sq = n3()
            P.op("act", lambda e, sq=sq, yc_=yc_: e.activation(sq[:, :], yc_[:, :], AF.Square), [yc_], [sq])
            pv_ = nps()
            P.op("pe", lambda e, pv_=pv_, sq=sq: e.matmul(pv_[:, :128], bones, sq[:, :], start=True, stop=True), [cm, sq], [pv_])
            P.op("dve", lambda e, pv_=pv_, sq=sq: e.tensor_scalar(sq[:, :], pv_[:, :128], 1.0 / 64, GN_EPS, ALU.mult, ALU.add), [pv_], [sq])
            P.op("act", lambda e, sq=sq: e.activation(sq[:, :], sq[:, :], AF.Sqrt), [sq], [sq])
            P.op("dve", lambda e, sq=sq: e.reciprocal(sq[:, :], sq[:, :]), [sq], [sq])
            P.op("dve", lambda e, yc_=yc_, sq=sq: e.scalar_tensor_tensor(yc_[:, :], yc_[:, :], pv[:, 7:8], sq[:, :], ALU.mult, ALU.mult), [yc_, pv, sq], [yc_])
            P.op("dve", lambda e, yc_=yc_, t0=t0: e.scalar_tensor_tensor(yc_[:, :], yc_[:, :], pv[:, 8:9], bonus[:, t0:t0 + 128], ALU.add, ALU.add), [yc_, pv, bonus], [yc_])
            o_ = n3()
            P.op("dve", lambda e, o_=o_, yc_=yc_, t0=t0: e.tensor_tensor(o_[:, :], yc_[:, :], gout[:, t0:t0 + 128], ALU.mult), [yc_, gout], [o_])
            dst = yr[:, t0 - LC:t0 - LC + 128] if s0 > 0 else yrc[:, t0:t0 + 128]
            P.dma(dst, o_[:, :], [o_], [Buf("yrout", None)], key=f"st{o_.name}")
    return P.build()


def build_M(nc, D, NCOL, NLAY, pfx=""):
    P = Prog(nc, pfx)
    DC = D // 128
    cc = P.dram("cc", [128, DC, 2], F32, kind="ExternalInput")
    wm = P.dram("wm", [NLAY, D, NCOL], F32, kind="ExternalInput")
    bm = P.dram("bm", [NLAY, 2, NCOL], F32, kind="ExternalInput")
    mo = P.dt("mo", [NLAY, 2, NCOL], F32, kind="ExternalOutput").ap()
    cs = P.sb([128, DC, 2], F32, name="cs")
    P.dma(cs[:], cc.t, [cc], [cs])
    P.op("act", lambda e: e.activation(cs[:], cs[:], AF.Silu), [cs], [cs])
    wt = [P.sb([128, DC, 256], F32, name=f"wt{i}") for i in range(4)]
    ps = [P.ps([128, 512], F32, name=f"ps{i}") for i in range(4)]
    ob = [P.sb([2, 256], F32, name=f"ob{i}") for i in range(3)]
    bb = [P.sb([2, 256], F32, name=f"bb{i}") for i in range(3)]
    i = 0
    for l in range(NLAY):
        for c0 in range(0, NCOL, 256):
            n = min(256, NCOL - c0)
            w = wt[i % 4]; p_ = ps[i % 4]; o = ob[i % 3]; b = bb[i % 3]
            i += 1
            P.dma(w[:, :, :n], wm.t[l, :, c0:c0 + n].rearrange("(c p) n -> p c n", p=128), [wm], [w])
            P.dma(b[:, :n], bm.t[l, :, c0:c0 + n], [bm], [b])
            for k in range(DC):
                P.op("pe", lambda e, w=w, p_=p_, k=k, n=n: e.matmul(p_[:2, :n], cs[:, k, :], w[:, k, :n], start=(k == 0), stop=(k == DC - 1)), [cs, w], [p_])
            P.op("dve", lambda e, o=o, p_=p_, b=b, n=n: e.tensor_tensor(o[:, :n], p_[:2, :n], b[:, :n], ALU.add), [p_, b], [o])
            P.dma(mo[l, :, c0:c0 + n], o[:, :n], [o], [Buf("moout", None)], key=f"st{o.name}")
    return P.build()

from concourse.bass_utils import run_bass_kernel_spmd

NCORES = 8
D_MODEL = 4096
SEQ = 8192
CTX_LEN = 256
D_FF = 5632
DEPTH = 2
HY_W = 1024
DA_W = 2048
RW_W = 1024
O_HY, O_Q, O_K, O_V, O_RW = 0, 3072, 5120, 7168, 9216
RW_STATE = 2304
O_RW_RG = O_RW + RW_STATE
O_GATE = 12704
TLAT = SEQ // NCORES
TCTX = CTX_LEN // NCORES
TT = TLAT + TCTX
DC = D_MODEL // 128


def _launch(build, in_maps):
    nc = bass.Bass("TRN2", target_bir_lowering=False)
    build(nc)
    res = run_bass_kernel_spmd(nc, in_maps, core_ids=list(range(NCORES)))
    return res.results


def _pc(vec):
    return np.asarray(vec, np.float32).reshape(DC, 128).T


def _vecs(rows):
    return np.ascontiguousarray(np.stack([_pc(r) for r in rows], axis=1))


def kernel(x, c, ctx, c_ctx, w_mod, b_mod, norm_gain, w_ff_in, w_ff_out, w_in,
           hy_short, hy_w1, hy_b1, hy_f1, hy_w2, hy_b2, hy_f2, hy_w3, hy_skip,
           da_lambda, da_subln,
           rw_shift, rw_w0, rw_w2, rw_a0, rw_a2, rw_g2, rw_k_k, rw_k_a, rw_r_k, rw_gn_w, rw_gn_b,
           w_branch, w_out, final_gain):
    f32 = np.float32
    A = lambda a: np.asarray(a, f32)
    x, c, ctx, c_ctx = A(x), A(c), A(ctx), A(c_ctx)
    w_in = A(w_in)
    NCOL = 9 * D_MODEL // NCORES
    cc = np.ascontiguousarray(np.stack([c[0], c_ctx], 0).reshape(2, DC, 128).transpose(2, 1, 0))
    w_mod = A(w_mod); b_mod = A(b_mod)
    ims = []
    for i in range(NCORES):
        sl = slice(i * NCOL, (i + 1) * NCOL)
        ims.append({"cc": cc, "wm": np.ascontiguousarray(w_mod[:, :, sl]),
                    "bm": np.ascontiguousarray(np.broadcast_to(b_mod[:, None, sl], (DEPTH, 2, NCOL)))})
    res = _launch(lambda nc: build_M(nc, D_MODEL, NCOL, DEPTH), ims)
    mo = np.concatenate([r["mo"] for r in res], axis=2)
    del ims
    mod = mo[:, 0].reshape(DEPTH, 9, D_MODEL)
    modc = mo[:, 1].reshape(DEPTH, 9, D_MODEL)

    xT = [np.ascontiguousarray(np.concatenate([x[0, i * TLAT:(i + 1) * TLAT], ctx[0, i * TCTX:(i + 1) * TCTX]], 0).T) for i in range(NCORES)]
    cs_tab, rt = rope_tables(SEQ)
    bones = np.zeros((128, 128), f32); bones[:64, :64] = 1; bones[64:, 64:] = 1
    cm = np.stack([np.eye(128, dtype=f32), np.eye(128, dtype=f32)[::-1].copy(), bones])
    LT = CTX_LEN + SEQ

    for l in range(DEPTH):
        last = l == DEPTH - 1
        lam_init = 0.8 - 0.6 * math.exp(-0.3 * l)
        ng = A(norm_gain[l])
        vA = _vecs([ng[0], ng[1], mod[l, 0], mod[l, 1], mod[l, 2], mod[l, 3], mod[l, 4],
                    modc[l, 0], modc[l, 1], modc[l, 2], modc[l, 3], modc[l, 4]])
        wfi, wfo = A(w_ff_in[l, 0]), A(w_ff_out[l, 0])
        win = np.ascontiguousarray(w_in[l][:, :O_GATE])
        ims = [{"xT": xT[i], "vecs": vA, "wfi": wfi, "wfo": wfo, "win": win} for i in range(NCORES)]
        res = _launch(lambda nc: build_A(nc, D_MODEL, D_FF, O_GATE, TT, TLAT), ims)
        del ims, win
        x1T = [r["x1T"] for r in res]
        p_lat = np.concatenate([r["pT"][:, :TLAT].T for r in res], 0)
        p_ctx = np.concatenate([r["pT"][:, TLAT:].T for r in res], 0)
        del res

        hs = A(hy_short[l]); w3 = A(hy_w3[l]).reshape(64, 2, 2, HY_W); sk = A(hy_skip[l])
        v1 = np.ascontiguousarray(np.stack([A(hy_b1[l]), A(hy_f1[l]), A(hy_b2[l]), A(hy_f2[l])], 1))

        def hy_ims(pp, L):
            ims = []
            for i in range(NCORES):
                c_lo = i * 128
                cols = np.concatenate([np.arange(c_lo, c_lo + 128) + k * HY_W for k in range(3)])
                hp = np.zeros((L + 2, 384), f32); hp[1:L + 1] = pp[:, O_HY + cols]
                zT, win_c, jm = hyena_consts(L, 128, c_lo)
                swb = np.ascontiguousarray(np.broadcast_to(hs[:, cols].reshape(3, 3, 128).transpose(1, 0, 2)[None], (128, 3, 3, 128)))
                skipb = np.ascontiguousarray(np.broadcast_to(sk[:, c_lo:c_lo + 128][None], (128, 2, 128)))
                ims.append({"hp": hp, "swb": swb, "skipb": skipb, "zT": zT, "w1": A(hy_w1[l]), "v1": v1, "w2": A(hy_w2[l]),
                            "w3c": np.ascontiguousarray(w3[:, :, :, c_lo:c_lo + 128]), "win": win_c, "jm": jm})
            return ims
        ims_h = hy_ims(p_lat, SEQ)
        ims_g = hy_ims(p_ctx, CTX_LEN) if not last else None

        ims_a = []
        for i in range(NCORES):
            hc = slice(i * 256, (i + 1) * 256)

            def fm(pp, off):
                return np.ascontiguousarray(pp[:, off:off + DA_W][:, hc].reshape(-1, 2, 2, 64).transpose(1, 2, 3, 0))
            vv = np.concatenate([p_lat[:, O_V:O_V + DA_W][:, hc], p_ctx[:, O_V:O_V + DA_W][:, hc]], 0)
            im = {"q": fm(p_lat, O_Q), "k": fm(p_lat, O_K), "kc": fm(p_ctx, O_K),
                  "v": np.ascontiguousarray(vv.reshape(LT, 2, 128).transpose(1, 0, 2)), "cs": cs_tab, "rt": rt,
                  "lamv": np.ascontiguousarray(A(da_lambda[l]).reshape(1, 256)), "subln": np.ascontiguousarray(A(da_subln[l]).reshape(128, 1))}
            if not last:
                im["qc"] = fm(p_ctx, O_Q)
            ims_a.append(im)

        rs = A(rw_shift[l]); w0 = A(rw_w0[l]); w2 = A(rw_w2[l]); a0 = A(rw_a0[l]); a2 = A(rw_a2[l]); g2 = A(rw_g2[l])

        def padded(rows_c, rows_l):
            n = rows_c.shape[0]
            o = np.zeros((2, n, 128, LT + 4), f32)
            o[0, :, :, 1:1 + CTX_LEN] = rows_c; o[0, :, :, CTX_LEN + 3:CTX_LEN + 3 + SEQ] = rows_l
            o[1, :, :, 1:1 + CTX_LEN] = rows_c[:, :, ::-1]; o[1, :, :, CTX_LEN + 3:CTX_LEN + 3 + SEQ] = rows_l[:, :, ::-1]
            return o

        def grpB(pp):
            r = pp[:, O_RW:O_GATE]
            gd2 = np.zeros((pp.shape[0], 128), f32); gd2[:, :32] = r[:, RW_STATE + RW_W + 128:]
            return np.stack([r[:, 2 * RW_W:2 * RW_W + 128].T, r[:, 2 * RW_W + 128:2 * RW_W + 256].T, r[:, RW_STATE + RW_W:RW_STATE + RW_W + 128].T, gd2.T])
        rinB = padded(grpB(p_ctx), grpB(p_lat))
        ims_r = []
        for i in range(NCORES):
            cs_ = slice(i * 128, (i + 1) * 128)

            def grpA(pp):
                r = pp[:, O_RW:O_GATE]
                return np.stack([r[:, 0:RW_W][:, cs_].T, r[:, RW_W:2 * RW_W][:, cs_].T, r[:, RW_STATE:RW_STATE + RW_W][:, cs_].T])
            sw = lambda cols: rs[:, cols].T
            swt = np.zeros((128, 7, 3), f32)
            swt[:, 0] = sw(np.arange(0, RW_W)[cs_]); swt[:, 1] = sw(np.arange(RW_W, 2 * RW_W)[cs_])
            swt[:, 2] = sw(np.arange(2 * RW_W, 2 * RW_W + 128)); swt[:, 3] = sw(np.arange(2 * RW_W + 128, 2 * RW_W + 256))
            swt[:, 4] = sw(np.arange(RW_STATE, RW_STATE + RW_W)[cs_]); swt[:, 5] = sw(np.arange(RW_STATE + RW_W, RW_STATE + RW_W + 128))
            swt[:32, 6] = sw(np.arange(RW_STATE + RW_W + 128, RW_STATE + RW_W + 160))
            pvv = np.zeros((128, 12), f32)
            pvv[:, 0] = w0[0, cs_]; pvv[:, 1] = w0[1, cs_]; pvv[:, 2] = a0[0, cs_]; pvv[:, 3] = a0[1, cs_]
            pvv[:, 4] = A(rw_k_k[l])[cs_]; pvv[:, 5] = A(rw_k_a[l])[cs_]; pvv[:, 6] = A(rw_r_k[l]).reshape(-1)[cs_]
            pvv[:, 7] = A(rw_gn_w[l])[cs_]; pvv[:, 8] = A(rw_gn_b[l])[cs_]
            ims_r.append({"rinA": padded(grpA(p_ctx), grpA(p_lat)), "rinB": rinB, "swt": swt, "pv": pvv,
                          "w2t": np.ascontiguousarray(np.concatenate([w2[0][:, cs_], w2[1][:, cs_]], 0)),
                          "a2t": np.ascontiguousarray(np.concatenate([a2[0][:, cs_], a2[1][:, cs_]], 0)),
                          "g2a": np.ascontiguousarray(g2[:128, cs_]), "g2b": np.ascontiguousarray(g2[128:, cs_]), "cm": cm})

        def build_B(nc):
            nc.all_engine_barrier()
            build_hy(nc, SEQ, 128, 32, pfx="h_")
            nc.all_engine_barrier()
            if not last:
                build_hy(nc, CTX_LEN, 128, 64, pfx="g_")
                nc.all_engine_barrier()
            build_at(nc, 2, SEQ, CTX_LEN, lam_init, not last, pfx="a_")
        ims = []
        for i in range(NCORES):
            m = {}
            for pf, src in (("h_", ims_h), ("g_", ims_g), ("a_", ims_a)):
                if src is not None:
                    m.update({pf + k_: v_ for k_, v_ in src[i].items()})
            ims.append(m)
        del ims_h, ims_g, ims_a
        res = _launch(build_B, ims)
        del ims
        res_r = _launch(lambda nc: build_rw(nc, SEQ, CTX_LEN, not last), ims_r)
        del ims_r, rinB
        y_h = np.concatenate([r["h_yh"].transpose(2, 0, 1).reshape(SEQ, 128) for r in res], 1)
        yc_h = np.concatenate([r["g_yh"].transpose(2, 0, 1).reshape(CTX_LEN, 128) for r in res], 1) if not last else None
        y_a = np.concatenate([r["a_ya"].transpose(2, 0, 1).reshape(SEQ, 256) for r in res], 1)
        yc_a = np.concatenate([r["a_yac"].transpose(2, 0, 1).reshape(CTX_LEN, 256) for r in res], 1) if not last else None
        y_r = np.concatenate([r["yr"].T for r in res_r], 1)
        yc_r = np.concatenate([r["yrc"].T for r in res_r], 1) if not last else None
        del res, res_r, p_lat, p_ctx

        y_lat = np.concatenate([y_h, y_a, y_r], 1)
        y_ctx = np.concatenate([yc_h, yc_a, yc_r], 1) if not last else np.zeros((CTX_LEN, 4096), f32)
        vC = _vecs([ng[1], ng[2], mod[l, 3], mod[l, 4], mod[l, 5], mod[l, 6], mod[l, 7], mod[l, 8],
                    modc[l, 3], modc[l, 4], modc[l, 5], modc[l, 6], modc[l, 7], modc[l, 8], A(final_gain)])
        wg = np.ascontiguousarray(w_in[l][:, O_GATE:])
        ims = []
        for i in range(NCORES):
            yT = np.ascontiguousarray(np.concatenate([y_lat[i * TLAT:(i + 1) * TLAT], y_ctx[i * TCTX:(i + 1) * TCTX]], 0).T)
            ims.append({"x1T": x1T[i], "yT": yT, "vecs": vC, "wg": wg, "wbr": A(w_branch[l]), "wo": A(w_out[l]),
                        "wfi": A(w_ff_in[l, 1]), "wfo": A(w_ff_out[l, 1])})
        res = _launch(lambda nc: build_C(nc, D_MODEL, D_FF, TT, TLAT, HY_W, DA_W, RW_W, last), ims)
        del ims, wg
        xT = [r["x3T"] for r in res]
        del res
    out = np.concatenate([t[:, :TLAT].T for t in xT], 0)[None]
    return np.ascontiguousarray(out.astype(np.float32))
```
